# Optimizing a Trainium2 kernel written in Bass

```python
import functools
import jax, jax.numpy as jnp
from jax import lax
import numpy as np

D_MODEL = 1024
BATCH = 2
SEQ = 16384
DEPTH = 1
DEC_BATCH = 128
DEC_SEQ = 1
PAST_LEN = 8192
PAGE_SIZE = 128

D_MIX = D_MODEL
GLA_WIDTH = D_MIX // 2
GLA_HEADS = 4
GLA_QK = GLA_WIDTH // 2
GLA_DK = GLA_QK // GLA_HEADS
GLA_DV = GLA_WIDTH // GLA_HEADS
GLA_GATE_RANK = 16
GLA_TAU = 16.0
GLA_CHUNK = 64
NSA_WIDTH = D_MIX - GLA_WIDTH
NSA_HEADS = 8
NSA_HEAD_DIM = NSA_WIDTH // NSA_HEADS
NSA_KV_HEADS = 2
NSA_GROUP = NSA_HEADS // NSA_KV_HEADS
KV_W = NSA_KV_HEADS * NSA_HEAD_DIM
CMP_BLOCK = 64
SEL_BLOCK = CMP_BLOCK
TOP_N = 16
WINDOW = 512
Q_BLOCK = 128
ROT_DIM = NSA_HEAD_DIM // 4
ROPE_THETA = 500000.0
D_FF = 2816
CONV_W = 3
PLE_DIM = 256
EPS = 1e-6
SPLITS = (GLA_QK, GLA_QK, GLA_WIDTH, GLA_WIDTH, GLA_GATE_RANK,
          NSA_WIDTH, KV_W, KV_W, KV_W, KV_W, KV_W, KV_W, 3 * NSA_HEADS)
D_IN = sum(SPLITS)

kernel_name = 'hymba_gla_nsa_convffn_step'


def rmsnorm(x, g):
    xf = x.astype(jnp.float32)
    y = xf * lax.rsqrt(jnp.mean(xf * xf, axis=-1, keepdims=True) + EPS) * g.astype(jnp.float32)
    return y.astype(x.dtype)


def split_cols(z):
    out, start = [], 0
    for n in SPLITS:
        out.append(z[..., start:start + n])
        start += n
    return out


def rope(x, pos):
    half = ROT_DIM // 2
    inv = ROPE_THETA ** (-jnp.arange(half, dtype=jnp.float32) / half)
    ang = pos.astype(jnp.float32)[:, None] * inv[None, :]
    cos = jnp.cos(ang)[None, :, None, :]
    sin = jnp.sin(ang)[None, :, None, :]
    xf = x.astype(jnp.float32)
    x1 = xf[..., :half]
    x2 = xf[..., half:ROT_DIM]
    out = jnp.concatenate([x1 * cos - x2 * sin, x2 * cos + x1 * sin, xf[..., ROT_DIM:]], axis=-1)
    return out.astype(x.dtype)


def masked_softmax(s, mask):
    s = jnp.where(mask, s.astype(jnp.float32), -jnp.inf)
    m = jnp.max(s, axis=-1, keepdims=True)
    m = jnp.where(jnp.isfinite(m), m, 0.0)
    e = jnp.where(mask, jnp.exp(s - m), 0.0)
    return e / jnp.maximum(jnp.sum(e, axis=-1, keepdims=True), 1e-30)


def gla_chunked(q, k, v, log_a, h0):
    B, T, H, dk = q.shape
    dv = v.shape[-1]
    C = min(GLA_CHUNK, T)
    pad = (-T) % C
    n = (T + pad) // C

    def blocks(a):
        a = jnp.pad(a.astype(jnp.float32), ((0, 0), (0, pad), (0, 0), (0, 0)))
        return a.reshape(B, n, C, H, a.shape[-1]).transpose(1, 0, 3, 2, 4)

    qb, kb, vb, lb = blocks(q), blocks(k), blocks(v), blocks(log_a)
    bcum = jnp.cumsum(lb, axis=3)
    causal = jnp.tril(jnp.ones((C, C), dtype=bool))

    def step(h, inp):
        qc, kc, vc, bc = inp
        qe = qc * jnp.exp(bc)
        ke = kc * jnp.exp(-bc)
        att = jnp.where(causal, jnp.einsum('bhid,bhjd->bhij', qe, ke), 0.0)
        o = jnp.einsum('bhij,bhjv->bhiv', att, vc) + jnp.einsum('bhid,bhdv->bhiv', qe, h)
        blast = bc[:, :, -1, :]
        h = jnp.exp(blast)[..., None] * h + jnp.einsum(
            'bhjd,bhjv->bhdv', kc * jnp.exp(blast[:, :, None, :] - bc), vc)
        return h, o

    h, o = lax.scan(step, h0.astype(jnp.float32), (qb, kb, vb, bcum))
    o = o.transpose(1, 0, 3, 2, 4).reshape(B, n * C, H, dv)[:, :T]
    return o, h


def compress(k, w):
    B, T = k.shape[:2]
    kb = k.reshape(B, T // CMP_BLOCK, CMP_BLOCK, NSA_KV_HEADS, NSA_HEAD_DIM)
    return jnp.einsum('bnlgd,lg->bngd', kb, w.astype(k.dtype))


def nsa_core(q, q_pos, kc, vc, get_sel, kw, vw, kw_pos, gates):
    B, Q = q.shape[:2]
    NC = kc.shape[1]
    qg = q.reshape(B, Q, NSA_KV_HEADS, NSA_GROUP, NSA_HEAD_DIM)
    t = q_pos[:, None]
    blk = jnp.arange(NC, dtype=jnp.int32)[None, :]
    cmask = blk * CMP_BLOCK + (CMP_BLOCK - 1) <= t
    p_cmp = masked_softmax(jnp.einsum('bqgrd,bngd->bgrqn', qg, kc), cmask[None, None, None])
    o_cmp = jnp.einsum('bgrqn,bngd->bqgrd', p_cmp.astype(vc.dtype), vc)
    cur = t // SEL_BLOCK
    forced = (blk == 0) | (blk == cur) | (blk == cur - 1)
    score = jnp.where(forced, float(NSA_GROUP + 1), jnp.sum(p_cmp, axis=2))
    score = jnp.where(blk > cur, -1.0, score)
    _, idx = lax.top_k(score, min(TOP_N, NC))
    ks, vs = get_sel(idx)
    n_sel = idx.shape[-1]
    spos = idx[..., None] * SEL_BLOCK + jnp.arange(SEL_BLOCK, dtype=jnp.int32)
    smask = (spos <= q_pos[None, None, :, None, None]).reshape(
        B, NSA_KV_HEADS, 1, Q, n_sel * SEL_BLOCK)
    s = jnp.einsum('bqgrd,bgqnld->bgrqnl', qg, ks).reshape(
        B, NSA_KV_HEADS, NSA_GROUP, Q, n_sel * SEL_BLOCK)
    p = masked_softmax(s, smask).reshape(B, NSA_KV_HEADS, NSA_GROUP, Q, n_sel, SEL_BLOCK)
    o_slc = jnp.einsum('bgrqnl,bgqnld->bqgrd', p.astype(vs.dtype), vs)
    kp = kw_pos[None, :]
    wmask = (kp >= 0) & (kp <= t) & (t - kp < WINDOW)
    p = masked_softmax(jnp.einsum('bqgrd,bwgd->bgrqw', qg, kw), wmask[None, None, None])
    o_win = jnp.einsum('bgrqw,bwgd->bqgrd', p.astype(vw.dtype), vw)
    o = jnp.stack([o_cmp, o_slc, o_win], axis=-1).reshape(B, Q, NSA_HEADS, NSA_HEAD_DIM, 3)
    return jnp.sum(o * gates[:, :, :, None, :], axis=-1)


def nsa_prompt(q, kc_r, vc_r, ks_r, vs_r, kw_r, vw_r, gates, w_ck, w_cv):
    B, T = q.shape[:2]
    kc = compress(kc_r, w_ck)
    vc = compress(vc_r, w_cv)
    nb = T // SEL_BLOCK
    kb = ks_r.reshape(B, nb, SEL_BLOCK, NSA_KV_HEADS, NSA_HEAD_DIM)
    vb = vs_r.reshape(B, nb, SEL_BLOCK, NSA_KV_HEADS, NSA_HEAD_DIM)
    bi = jnp.arange(B)[:, None, None, None]
    gi = jnp.arange(NSA_KV_HEADS)[None, :, None, None]

    def get_sel(idx):
        return kb[bi, idx, :, gi], vb[bi, idx, :, gi]

    wpad = ((0, 0), (WINDOW, 0), (0, 0), (0, 0))
    kw_pad = jnp.pad(kw_r, wpad)
    vw_pad = jnp.pad(vw_r, wpad)
    nq = T // Q_BLOCK

    def to_blocks(a):
        return a.reshape(B, nq, Q_BLOCK, *a.shape[2:]).swapaxes(0, 1)

    def one_block(args):
        qb, gb, c = args
        q0 = c * Q_BLOCK
        kw = lax.dynamic_slice_in_dim(kw_pad, q0, WINDOW + Q_BLOCK, axis=1)
        vw = lax.dynamic_slice_in_dim(vw_pad, q0, WINDOW + Q_BLOCK, axis=1)
        kw_pos = q0 - WINDOW + jnp.arange(WINDOW + Q_BLOCK, dtype=jnp.int32)
        q_pos = q0 + jnp.arange(Q_BLOCK, dtype=jnp.int32)
        return nsa_core(qb, q_pos, kc, vc, get_sel, kw, vw, kw_pos, gb)

    out = lax.map(one_block, (to_blocks(q), to_blocks(gates), jnp.arange(nq, dtype=jnp.int32)))
    out = out.swapaxes(0, 1).reshape(B, T, NSA_HEADS, NSA_HEAD_DIM)
    nw = min(WINDOW, T)
    return out, (kc_r, vc_r, ks_r, vs_r, kw_r[:, T - nw:], vw_r[:, T - nw:])


def nsa_sample(q, kc_r, vc_r, ks_r, vs_r, kw_r, vw_r, gates, w_ck, w_cv,
               pool_kc, pool_vc, pool_ks, pool_vs, buf_kw, buf_vw, page_table):
    B, T = q.shape[:2]
    past_len = page_table.shape[1] * PAGE_SIZE

    def gather_past(pool):
        return pool[page_table].reshape(B, past_len, NSA_KV_HEADS, NSA_HEAD_DIM)

    tpad = (-T) % CMP_BLOCK

    def padn(a):
        return jnp.pad(a, ((0, 0), (0, tpad), (0, 0), (0, 0)))

    kc = jnp.concatenate([compress(gather_past(pool_kc), w_ck), compress(padn(kc_r), w_ck)], axis=1)
    vc = jnp.concatenate([compress(gather_past(pool_vc), w_cv), compress(padn(vc_r), w_cv)], axis=1)
    n_past_blk = past_len // SEL_BLOCK
    n_new_blk = (T + tpad) // SEL_BLOCK
    bpp = PAGE_SIZE // SEL_BLOCK
    pool_kb = pool_ks.reshape(-1, SEL_BLOCK, NSA_KV_HEADS, NSA_HEAD_DIM)
    pool_vb = pool_vs.reshape(-1, SEL_BLOCK, NSA_KV_HEADS, NSA_HEAD_DIM)
    new_kb = padn(ks_r).reshape(B, n_new_blk, SEL_BLOCK, NSA_KV_HEADS, NSA_HEAD_DIM)
    new_vb = padn(vs_r).reshape(B, n_new_blk, SEL_BLOCK, NSA_KV_HEADS, NSA_HEAD_DIM)
    bi = jnp.arange(B)[:, None, None, None]
    gi = jnp.arange(NSA_KV_HEADS)[None, :, None, None]

    def get_sel(idx):
        in_past = (idx < n_past_blk)[..., None, None]
        pi = jnp.clip(idx, 0, n_past_blk - 1)
        phys = page_table[bi, pi // bpp] * bpp + pi % bpp
        ni = jnp.clip(idx - n_past_blk, 0, n_new_blk - 1)
        ks = jnp.where(in_past, pool_kb[phys, :, gi], new_kb[bi, ni, :, gi])
        vs = jnp.where(in_past, pool_vb[phys, :, gi], new_vb[bi, ni, :, gi])
        return ks, vs

    wb = buf_kw.shape[1]
    kw = jnp.concatenate([buf_kw, kw_r], axis=1)
    vw = jnp.concatenate([buf_vw, vw_r], axis=1)
    kw_pos = past_len - wb + jnp.arange(wb + T, dtype=jnp.int32)
    q_pos = past_len + jnp.arange(T, dtype=jnp.int32)
    out = nsa_core(q, q_pos, kc, vc, get_sel, kw, vw, kw_pos, gates)
    nw = min(WINDOW, wb + T)
    return out, (kc_r, vc_r, ks_r, vs_r, kw[:, wb + T - nw:], vw[:, wb + T - nw:])


def conv_ffn(x, prev, w_up, w_conv, b_conv, w_down):
    u = x @ w_up
    T = u.shape[1]
    ext = jnp.concatenate([prev.astype(u.dtype), u], axis=1)
    c = b_conv
    for j in range(CONV_W):
        c = c + ext[:, j:j + T] * w_conv[j]
    a, b = jnp.split(c, 2, axis=-1)
    return (jax.nn.silu(a) * b) @ w_down, ext[:, ext.shape[1] - (CONV_W - 1):]


def trunk_layer(x, pe, pos, gla_h0, conv_prev, nsa_fn, g_attn, w_in, w_gla_gate, b_gla_gate,
                g_gla_out, b_nsa_gate, w_o, g_ffn, w_up, w_conv, b_conv, w_down,
                g_ple, w_ple, w_ple_gate):
    B, T, _ = x.shape
    xn = rmsnorm(x, g_attn)
    (gq, gk, gv, gr, glr, nq, nkc, nvc, nks, nvs, nkw, nvw, ng) = split_cols(xn @ w_in)
    q = gq.reshape(B, T, GLA_HEADS, GLA_DK) * (GLA_DK ** -0.5)
    k = gk.reshape(B, T, GLA_HEADS, GLA_DK)
    v = gv.reshape(B, T, GLA_HEADS, GLA_DV)
    log_a = jax.nn.log_sigmoid((glr @ w_gla_gate + b_gla_gate).astype(jnp.float32)) / GLA_TAU
    o, h_gla = gla_chunked(q, k, v, log_a.reshape(B, T, GLA_HEADS, GLA_DK), gla_h0)
    o_gla = rmsnorm(o.astype(x.dtype), g_gla_out.reshape(GLA_HEADS, GLA_DV)).reshape(
        B, T, GLA_WIDTH) * jax.nn.silu(gr)
    kvs = (B, T, NSA_KV_HEADS, NSA_HEAD_DIM)
    qn = rope(nq.reshape(B, T, NSA_HEADS, NSA_HEAD_DIM), pos) * (NSA_HEAD_DIM ** -0.5)
    gates = jax.nn.sigmoid(ng + b_nsa_gate).reshape(B, T, NSA_HEADS, 3)
    o_nsa, nsa_new = nsa_fn(qn, rope(nkc.reshape(kvs), pos), nvc.reshape(kvs),
                            rope(nks.reshape(kvs), pos), nvs.reshape(kvs),
                            rope(nkw.reshape(kvs), pos), nvw.reshape(kvs), gates)
    h = x + jnp.concatenate([o_gla, o_nsa.reshape(B, T, NSA_WIDTH)], axis=-1) @ w_o
    f, conv_new = conv_ffn(rmsnorm(h, g_ffn), conv_prev, w_up, w_conv, b_conv, w_down)
    h = h + f
    h = h + (pe @ w_ple) * jax.nn.sigmoid(rmsnorm(h, g_ple) @ w_ple_gate)
    return h, (*nsa_new, h_gla.astype(gla_h0.dtype), conv_new)


def setup_inputs(seed: int = 0) -> dict:
    key = jax.random.key(seed)
    keys = jax.random.split(key, 32)

    def nrm(i, shape, scale):
        return jax.random.normal(keys[i], shape, jnp.float32) * scale

    n_pages = PAST_LEN // PAGE_SIZE
    n_pool = (DEC_BATCH * n_pages * 5) // 4
    win_buf = min(WINDOW, PAST_LEN)
    kv_pool = (DEPTH, n_pool, PAGE_SIZE, NSA_KV_HEADS, NSA_HEAD_DIM)
    win = (DEPTH, DEC_BATCH, win_buf, NSA_KV_HEADS, NSA_HEAD_DIM)
    page_table = jax.random.permutation(keys[12], n_pool)[:DEC_BATCH * n_pages].reshape(
        DEC_BATCH, n_pages).astype(jnp.int32)
    return {
        'x_prompt': nrm(0, (BATCH, SEQ, D_MODEL), 1.0),
        'x_sample': nrm(1, (DEC_BATCH, DEC_SEQ, D_MODEL), 1.0),
        'p_prompt': nrm(2, (DEPTH, BATCH, SEQ, PLE_DIM), 1.0),
        'p_sample': nrm(3, (DEPTH, DEC_BATCH, DEC_SEQ, PLE_DIM), 1.0),
        'cache_k_cmp': nrm(4, kv_pool, 1.0),
        'cache_v_cmp': nrm(5, kv_pool, 1.0),
        'cache_k_slc': nrm(6, kv_pool, 1.0),
        'cache_v_slc': nrm(7, kv_pool, 1.0),
        'cache_k_win': nrm(8, win, 1.0),
        'cache_v_win': nrm(9, win, 1.0),
        'state_gla': nrm(10, (DEPTH, DEC_BATCH, GLA_HEADS, GLA_DK, GLA_DV), 0.5),
        'state_conv': nrm(11, (DEPTH, DEC_BATCH, CONV_W - 1, 2 * D_FF), 1.0),
        'page_table': page_table,
        'g_attn': 1.0 + nrm(13, (DEPTH, D_MODEL), 0.02),
        'w_in': nrm(14, (DEPTH, D_MODEL, D_IN), D_MODEL ** -0.5),
        'w_gla_gate': nrm(15, (DEPTH, GLA_GATE_RANK, GLA_QK), GLA_GATE_RANK ** -0.5),
        'b_gla_gate': nrm(16, (DEPTH, GLA_QK), 0.1),
        'g_gla_out': 1.0 + nrm(17, (DEPTH, GLA_WIDTH), 0.02),
        'b_nsa_gate': nrm(18, (DEPTH, 3 * NSA_HEADS), 0.1),
        'w_cmp_k': (1.0 + nrm(19, (DEPTH, CMP_BLOCK, NSA_KV_HEADS), 0.1)) / CMP_BLOCK,
        'w_cmp_v': (1.0 + nrm(20, (DEPTH, CMP_BLOCK, NSA_KV_HEADS), 0.1)) / CMP_BLOCK,
        'w_o': nrm(21, (DEPTH, D_MIX, D_MODEL), D_MIX ** -0.5),
        'g_ffn': 1.0 + nrm(22, (DEPTH, D_MODEL), 0.02),
        'w_up': nrm(23, (DEPTH, D_MODEL, 2 * D_FF), D_MODEL ** -0.5),
        'w_conv': nrm(24, (DEPTH, CONV_W, 2 * D_FF), CONV_W ** -0.5),
        'b_conv': nrm(25, (DEPTH, 2 * D_FF), 0.02),
        'w_down': nrm(26, (DEPTH, D_FF, D_MODEL), D_FF ** -0.5),
        'g_ple': 1.0 + nrm(27, (DEPTH, D_MODEL), 0.02),
        'w_ple': nrm(28, (DEPTH, PLE_DIM, D_MODEL), PLE_DIM ** -0.5),
        'w_ple_gate': nrm(29, (DEPTH, D_MODEL, D_MODEL), D_MODEL ** -0.5),
        'g_final': 1.0 + nrm(30, (D_MODEL,), 0.02),
    }


def reference(x_prompt, x_sample, p_prompt, p_sample, cache_k_cmp, cache_v_cmp, cache_k_slc,
              cache_v_slc, cache_k_win, cache_v_win, state_gla, state_conv, page_table,
              g_attn, w_in, w_gla_gate, b_gla_gate, g_gla_out, b_nsa_gate, w_cmp_k, w_cmp_v,
              w_o, g_ffn, w_up, w_conv, b_conv, w_down, g_ple, w_ple, w_ple_gate, g_final):
    B, S = x_prompt.shape[:2]
    Sd = x_sample.shape[1]
    pos_p = jnp.arange(S, dtype=jnp.int32)
    pos_s = PAST_LEN + jnp.arange(Sd, dtype=jnp.int32)
    hp, hs = x_prompt, x_sample
    st_p = [[] for _ in range(8)]
    st_s = [[] for _ in range(8)]
    for i in range(DEPTH):
        lw = (g_attn[i], w_in[i], w_gla_gate[i], b_gla_gate[i], g_gla_out[i], b_nsa_gate[i],
              w_o[i], g_ffn[i], w_up[i], w_conv[i], b_conv[i], w_down[i],
              g_ple[i], w_ple[i], w_ple_gate[i])
        nsa_p = functools.partial(nsa_prompt, w_ck=w_cmp_k[i], w_cv=w_cmp_v[i])
        hp, new_p = trunk_layer(
            hp, p_prompt[i], pos_p,
            jnp.zeros((B, GLA_HEADS, GLA_DK, GLA_DV), jnp.float32),
            jnp.zeros((B, CONV_W - 1, 2 * D_FF), hp.dtype), nsa_p, *lw)
        nsa_s = functools.partial(
            nsa_sample, w_ck=w_cmp_k[i], w_cv=w_cmp_v[i],
            pool_kc=cache_k_cmp[i], pool_vc=cache_v_cmp[i],
            pool_ks=cache_k_slc[i], pool_vs=cache_v_slc[i],
            buf_kw=cache_k_win[i], buf_vw=cache_v_win[i], page_table=page_table)
        hs, new_s = trunk_layer(hs, p_sample[i], pos_s, state_gla[i], state_conv[i], nsa_s, *lw)
        for j in range(8):
            st_p[j].append(new_p[j])
            st_s[j].append(new_s[j])
    sp = [jnp.stack(a, axis=0) for a in st_p]
    ss = [jnp.stack(a, axis=0) for a in st_s]
    y_prompt = rmsnorm(hp, g_final)
    y_sample = rmsnorm(hs, g_final)
    return (y_prompt, y_sample, sp[0], sp[1], sp[2], sp[3], sp[4], sp[5], sp[6], sp[7],
            ss[0], ss[1], ss[2], ss[3], ss[4], ss[5], ss[6], ss[7])
```

```python
import contextlib
import numpy as np
import ml_dtypes
import concourse.bass as bass
import concourse.mybir as mybir
from concourse.bass_utils import run_bass_kernel_spmd

import os
SKIPKV = bool(os.environ.get('SKIPKV'))
F32 = mybir.dt.float32
BF16 = mybir.dt.bfloat16
I32 = mybir.dt.int32
ALU = mybir.AluOpType
AF = mybir.ActivationFunctionType
AX = mybir.AxisListType

D = 1024
DFF = 2816
DIN = 2856
OFF = dict(gq=0, gk=256, gv=512, gr=1024, glr=1536, nq=1552, kc=2064, vc=2192, ks=2320, vs=2448,
           kw=2576, vw=2704, ng=2832)
NEG = -30000.0
EPS = 1e-6


class _Rec:
    def __getattr__(self, name):
        def mk(*a, **k):
            return lambda eng: getattr(eng, name)(*a, **k)
        return mk


REC = _Rec()


class KB:
    ENG = ['pe', 'act', 'dve', 'pool', 'sp']
    NDS = 40

    def __init__(self, nc):
        self.nc = nc
        self.q = {e: [] for e in self.ENG}
        self.cnt = {e: 0 for e in self.ENG}
        self.seen = {e: {} for e in self.ENG}
        self.lastw = {}
        self.readers = {}
        self.dtot = [0] * self.NDS
        self.dnext = 0
        self.dnext_sw = 0
        self.pend = {e: [] for e in self.ENG}
        self.alias = {}

    def _need(self, E, tok, waits):
        if tok is None:
            return
        key, val = tok
        if key == 'pe' and E == 'pe':
            return
        if self.seen[E].get(key, 0) >= val:
            return
        self.seen[E][key] = val
        waits.append(tok)

    def _deps(self, E, r, w):
        waits = []
        r = [self.alias.get(x, x) for x in r]
        w = [self.alias.get(x, x) for x in w]
        for reg in r:
            self._need(E, self.lastw.get(reg), waits)
        for reg in w:
            self._need(E, self.lastw.get(reg), waits)
            for key, val in self.readers.get(reg, {}).items():
                self._need(E, (key, val), waits)
        return waits

    def _commit(self, tok, r, w):
        r = [self.alias.get(x, x) for x in r]
        w = [self.alias.get(x, x) for x in w]
        for reg in w:
            self.lastw[reg] = tok
            self.readers[reg] = {}
        for reg in r:
            d = self.readers.setdefault(reg, {})
            if d.get(tok[0], 0) < tok[1]:
                d[tok[0]] = tok[1]

    def barrier(self):
        for E in self.ENG:
            waits = self.pend[E]
            for e2 in ['pe', 'act', 'dve', 'pool']:
                if e2 != E and self.cnt[e2]:
                    self._need(E, (e2, self.cnt[e2]), waits)
            if E != 'pe' and self.cnt.get(E) and E != 'sp':
                self._need(E, (E, self.cnt[E]), waits)
            for k in range(self.NDS):
                if self.dtot[k]:
                    self._need(E, (('d', k), self.dtot[k]), waits)

    def op(self, E, fn, r=(), w=()):
        waits = self.pend[E] + self._deps(E, r, w)
        self.pend[E] = []
        self.cnt[E] += 1
        tok = (E, self.cnt[E])
        self.q[E].append((waits, fn, (E, 1)))
        self._commit(tok, r, w)

    def dma(self, E, fn, r=(), w=()):
        if E == 'pool':
            k = 32 + self.dnext_sw
            self.dnext_sw = (self.dnext_sw + 1) % 8
        else:
            k = self.dnext
            self.dnext = (self.dnext + 1) % 32
        waits = self.pend[E] + self._deps(E, r, w)
        self.pend[E] = []
        if self.dtot[k]:
            self._need(E, (('d', k), self.dtot[k]), waits)
        self.dtot[k] += 16
        tok = (('d', k), self.dtot[k])
        self.q[E].append((waits, fn, (('d', k), 16)))
        self._commit(tok, r, w)

    def emit(self):
        nc = self.nc
        waits = []
        for k in range(self.NDS):
            if self.dtot[k]:
                self._need('sp', (('d', k), self.dtot[k]), waits)
        for e in ['pe', 'act', 'dve', 'pool']:
            if self.cnt[e]:
                self._need('sp', (e, self.cnt[e]), waits)
        self.q['sp'].append((waits, None, None))
        with contextlib.ExitStack() as st:
            sems = {}
            for e in ['pe', 'act', 'dve', 'pool']:
                sems[e] = st.enter_context(nc.semaphore('c_' + e))
            for k in range(self.NDS):
                sems[('d', k)] = st.enter_context(nc.semaphore('d%d' % k))
            block = st.enter_context(nc.Block())

            def run(eng, items):
                for waits, fn, inc in items:
                    for key, val in waits:
                        eng.wait_ge(sems[key], val)
                    if fn is None:
                        continue
                    fn(eng).then_inc(sems[inc[0]], inc[1])

            @block.tensor
            def _(eng):
                run(eng, self.q['pe'])

            @block.scalar
            def _(eng):
                run(eng, self.q['act'])

            @block.vector
            def _(eng):
                run(eng, self.q['dve'])

            @block.gpsimd
            def _(eng):
                run(eng, self.q['pool'])

            @block.sync
            def _(eng):
                run(eng, self.q['sp'])


def build(S, NSMP, PAST, do_sample=True, dbg=None, NPOOLP=0):
    NT = S // 128
    NB = S // 64
    OWN = NT // 4
    HALO = NT - OWN - 1
    NQ = OWN + 1
    W0 = HALO - 4
    NW = NT - W0
    BT = min(128, NB)
    NBT = NB // BT
    assert W0 >= 0

    nc = bass.Bass("TRN2", target_bir_lowering=False)

    def din(name, shape, dt=F32):
        return nc.dram_tensor(name, list(shape), dt, kind="ExternalInput").ap()

    def dout(name, shape, dt=F32):
        return nc.dram_tensor(name, list(shape), dt, kind="ExternalOutput").ap()

    xl = din("xl", [S, D])
    pl = din("pl", [OWN * 128, 256])
    cs = din("cs", [S, 16])
    cmask = din("cmask", [128, 2 * NB + NB + 128 + 128 + 5 * 128 + 128 + 128])
    kbias = din("kbias", [1, S])
    ind = din("ind", [64, min(32, NT) * 128])
    wc = din("wc", [128, 8])
    w_in = din("w_in", [D, DIN])
    wgg = din("wgg", [17, 256])
    vecsA = din("vecsA", [1, 1560])
    vecsB = din("vecsB", [1, 3072])
    w_o = din("w_o", [D, D])
    w_up = din("w_up", [D, 2 * DFF])
    convw = din("convw", [128, 44 * 4])
    w_down = din("w_down", [DFF, D])
    w_ple = din("w_ple", [256, D])
    w_pg = din("w_pg", [D, D])

    NS = NSMP
    NPG = PAST // 128
    NBP = PAST // 64
    SPG = 128 // NPG
    NGR = NS // SPG
    NREP = 1 + NGR + 2
    if do_sample:
        xs_i = din("xs", [NS, D])
        ps_i = din("ps", [NS, 256])
        ptab_i = din("ptab", [128, NGR], I32)
        ptabs_i = din("ptabs", [NS, NPG], I32)
        pools = [nc.dram_tensor("pool%d" % i, [NPOOLP, 16384], F32, kind="ExternalInput") for i in range(4)]
        wink_i = din("wink", [NS, 512, 128])
        winv_i = din("winv", [NS, 512, 128])
        sgla_i = din("sgla", [NS, 4, 64, 128])
        sconv_i = din("sconv", [NS, 2 * 2 * DFF])
        scs_i = din("scs", [NS, 16 + 64 * 16 + 128 * NREP + 16 + 128 + NPG + 16])
        rep_i = din("rep", [128, NREP * 16 + 256 + 64 + 4 * 64 + 1])
        ys_o = dout("ys", [NS, D])
        skv_o = dout("skv6", [NS, 768])
        swk_o = dout("swk", [NS, 512, 128])
        swv_o = dout("swv", [NS, 512, 128])
        sgl_o = dout("sglo", [NS, 4, 64, 128])
        scv_o = dout("scvo", [NS, 2, 2 * DFF])
        oscr_s = nc.dram_tensor("oscr_s", [NS, D], BF16, kind="Internal").ap()
        scmp_d = nc.dram_tensor("scmp_d", [NS * NPG, 16], F32, kind="Internal").ap()
        vc_d = nc.dram_tensor("vc_d", [NS * NPG, 256], F32, kind="Internal").ap()
        idx_d = nc.dram_tensor("idx_d", [2, NS * 16], I32, kind="Internal").ap()
        sct_d = nc.dram_tensor("sct_d", [128, 88 * NS], F32, kind="Internal").ap()
    y_o = dout("y", [OWN * 128, D])
    kv_o = dout("kv6", [OWN * 128, 768])
    gl_o = dout("glast", [4, 64, 128])
    cv_o = dout("convrows", [2, 2 * DFF])
    oscr = nc.dram_tensor("oscr", [NQ * 128, D], BF16, kind="Internal").ap()

    kb = KB(nc)
    dbg_o = dout("dbg", [128, 8192]) if dbg is not None else None
    dbg_items = []

    def dump(name, ap, regs, g=None, n=None):
        if dbg is None or (g, n) != tuple(dbg):
            return
        P_, C_ = ap.shape[0], int(np.prod(ap.shape[1:]))
        c0 = sum(it[2] for it in dbg_items)
        dbg_items.append((name, P_, C_, c0, tuple(ap.shape)))
        kb.dma('sp', REC.dma_start(out=dbg_o[0:P_, c0:c0 + C_], in_=ap), r=regs, w=['dbg_o'])
    nc._dbg_items = dbg_items
    with contextlib.ExitStack() as st0:
        def sb(st, name, shape, dt):
            return st.enter_context(nc.sbuf_tensor("s_" + name, list(shape), dt))

        def pst(st, name, shape, dt):
            return st.enter_context(nc.psum_tensor(name, list(shape), dt))

        PB = [pst(st0, "pb%d" % i, [128, 512], F32) for i in range(8)]

        def pbf(i):
            return PB[i][:].bitcast(BF16)

        identb = sb(st0, "identb", [128, 128], BF16)
        identf = sb(st0, "identf", [128, 128], F32)
        CMst = contextlib.ExitStack()
        CM = sb(CMst, "CM", [128, 2 * NB + NB + 128 + 128 + 5 * 128 + 128 + 128], F32)
        vrep = sb(CMst, "vrep", [128, 1560], F32)
        kb.dma('sp', REC.dma_start(out=CM[:], in_=cmask), w=['CM'])
        kb.dma('sp', REC.dma_start(out=vrep[:], in_=vecsA.partition_broadcast(128)), w=['vrep'])
        valid = CM[:, 0:NB]
        first = CM[:, NB:2 * NB]
        D0 = CM[:, 2 * NB:3 * NB]
        o_ = 3 * NB
        identf_src = CM[:, o_:o_ + 128]
        tri_src = CM[:, o_ + 128:o_ + 256]
        wb_src = CM[:, o_ + 256:o_ + 256 + 640]
        tribd = CM[:, o_ + 896:o_ + 1024]
        rmask = CM[0:64, o_ + 1024:o_ + 1152]
        kb.op('dve', REC.tensor_copy(out=identb[:], in_=identf_src), r=['CM'], w=['identb'])
        kb.op('dve', REC.tensor_copy(out=identf[:], in_=identf_src), r=['CM'], w=['identf'])
        g_attn = vrep[:, 0:1024]
        g_gla = vrep[:, 1024:1536]
        b_ng = vrep[:, 1536:1560]

        with contextlib.ExitStack() as stA:
            w_in_sb = sb(stA, "w_in_sb", [128, 8, DIN], BF16)
            for k in range(8):
                for c0 in (0, 1428):
                    kb.dma('pool', REC.dma_start(
                        out=w_in_sb[:, k, c0:c0 + 1428], in_=w_in[k * 128:(k + 1) * 128, c0:c0 + 1428]), w=['w_in_sb'])
            wgg_sb = sb(stA, "wgg_sb", [17, 256], BF16)
            kb.dma('pool', REC.dma_start(out=wgg_sb[:], in_=wgg), w=['wgg_sb'])
            wc_sb = sb(stA, "wc_sb", [128, 8], BF16)
            kb.dma('pool', REC.dma_start(out=wc_sb[:], in_=wc), w=['wc_sb'])
            tri4 = sb(stA, "tri4", [128, 4, 128], BF16)
            wb4 = sb(stA, "wb4", [128, 5, 4, 128], BF16)
            for h in range(4):
                kb.op('dve', REC.tensor_copy(out=tri4[:, h, :], in_=tri_src), r=['CM'], w=['tri4'])
                kb.op('dve', REC.tensor_copy(out=wb4[:, :, h, :], in_=wb_src.rearrange("p (t f) -> p t f", t=5)),
                      r=['CM'], w=['wb4'])
            ksT = sb(stA, "ksT", [128, S], BF16)
            vsx = sb(stA, "vsx", [128, NT, 65], BF16)
            kwT = sb(stA, "kwT", [65, NW * 128], BF16)
            vwx = sb(stA, "vwx", [128, NW, 65], BF16)
            kcT = sb(stA, "kcT", [64, NB], BF16)
            vcT = sb(stA, "vcT", [64, NB], BF16)
            vcs = sb(stA, "vcs", [BT, NBT, 64], BF16)
            kb.op('pool', REC.memset(kcT[:], 0.0), w=['kcT'])
            kb.op('pool', REC.memset(vcT[:], 0.0), w=['vcT'])
            kb.op('pool', REC.memset(ksT[0:64, :], 0.0), w=['ksT'])
            kb.op('pool', REC.memset(vsx[:], 1.0), w=['vsx'])
            kb.op('pool', REC.memset(vwx[:], 1.0), w=['vwx'])
            rep = S // (min(32, NT) * 128)
            for r_ in range(rep):
                w_ = min(32, NT) * 128
                kb.dma('pool', REC.dma_start(out=ksT[64:128, r_ * w_:(r_ + 1) * w_], in_=ind), w=['ksT'])
            kb.dma('pool', REC.dma_start(out=kwT[64:65, :], in_=kbias[:, W0 * 128:S]), w=['kwT'])
            hst = sb(stA, "hst", [64, 4, 128], F32)
            hbf = sb(stA, "hbf", [64, 4, 128], BF16)
            hbf1 = sb(stA, "hbf1", [64, 4, 128], BF16)
            kb.op('pool', REC.memset(hst[:], 0.0), w=['hst'])
            kb.op('pool', REC.memset(hbf[:], 0.0), w=['hbf'])
            glr_x = sb(stA, "glr_x", [32, 128], BF16)
            kb.op('pool', REC.memset(glr_x[:], 1.0), w=['glr_x'])
            qez = sb(stA, "qez", [64, 2, 4, 128], BF16)
            kb.op('pool', REC.memset(qez[:], 0.0), w=['qez'])
            xt = [sb(stA, "xt%d" % i, [128, D], F32) for i in range(2)]
            cst = [sb(stA, "cst%d" % i, [128, 16], F32) for i in range(2)]
            ss = sb(stA, "ss", [128, 8], F32)
            junk = sb(stA, "junk", [128, D], BF16)
            xn2 = [sb(stA, "xn0", [128, D], BF16)] * 2
            xnT2 = [sb(stA, "xnT%d" % i, [128, 8, 128], BF16) for i in range(2)]
            kvf2 = [sb(stA, "kvf0", [128, 6, 64], F32)] * 2
            ktmp = sb(stA, "ktmp", [128, 4, 4, 8], F32)
            kvb2 = [sb(stA, "kvb%d" % i, [128, 6, 64], BF16) for i in range(2)]
            kv6 = sb(stA, "kv6", [128, 6, 2, 64], F32)
            v_bf2 = [sb(stA, "v_bf%d" % i, [128, 512], BF16) for i in range(2)]
            G4 = sb(stA, "G4", [64, 4, 512], F32)
            gb = sb(stA, "gb", [64, 3, 512], BF16)
            kd_tok = sb(stA, "kd_tok", [128, 4, 64], BF16)
            attm = sb(stA, "attm", [128, 4, 128], BF16)
            sg = sb(stA, "sg", [64, 16], F32)
            rmask4 = sb(stA, "rmask4", [64, 512], F32)
            for hh_ in range(4):
                kb.op('dve', REC.tensor_copy(out=rmask4[:, 128 * hh_:128 * hh_ + 128], in_=rmask), r=['CM'], w=['rmask4'])
            sgr = sb(stA, "sgr", [128, 512], BF16)
            og = sb(stA, "og", [128, 512], F32)
            ogb = sb(stA, "ogb", [128, 512], BF16)
            sm8 = sb(stA, "sm8", [128, 16], F32)
            qf = sb(stA, "qf", [128, 4, 64], F32)
            gts = sb(stA, "gts", [128, 12], F32)
            qtmp = sb(stA, "qtmp", [128, 4, 4, 8], F32)
            qpair = sb(stA, "qpair", [128, 4, 128], BF16)
            qT = sb(stA, "qT", [65, 4, 128], BF16)
            kb.op('pool', REC.memset(qpair[:], 0.0), w=['qpair'])
            kb.op('pool', REC.memset(qT[:], 1.0), w=['qT'])
            dd = sb(stA, "dd", [128, NB], F32)
            cb = sb(stA, "cb", [128, NB], F32)
            smx = sb(stA, "smx", [128, 4, NB], F32)
            sc = sb(stA, "sc", [128, NB], F32)
            sc2 = sb(stA, "sc2", [128, NB], F32)
            t1 = sb(stA, "t1", [128, NB], F32)
            t2 = cb
            selb = sb(stA, "selb", [128, NB], F32)
            m8 = sb(stA, "m8", [128, 16], F32)
            pT = sb(stA, "pT", [BT, 4, NBT, 128], BF16)
            NM = max(1, NB // 64)
            rhsm = sb(stA, "rhsm", [128, NM, 512], BF16)
            SBANKS = [0, 1, 7]
            PTbuf = sb(stA, "PTbuf", [128, 3, max(512, 2 * NB)], BF16)
            PT = [PTbuf[:, i, 0:512] for i in range(3)]
            pcb = PTbuf[:].rearrange("p a b -> p (a b)")[:, 0:4 * NB].rearrange("p (h b) -> p h b", b=NB)
            oT = sb(stA, "oT", [65, 2, 512], F32)
            ocm = sb(stA, "ocm", [128, 4, 64], F32)
            onb = sb(stA, "onb", [128, 256], BF16)
            rc = sb(stA, "rc", [128, 16], F32)

            def rmsnorm_tile(src, gvec, dst_bf, srcreg, dstreg):
                kb.op('act', REC.activation(out=junk[:], in_=src, func=AF.Square, accum_out=ss[:, 0:1]),
                      r=[srcreg], w=['junk', 'ss'])
                kb.op('dve', REC.tensor_scalar(out=ss[:, 1:2], in0=ss[:, 0:1], scalar1=1.0 / D, scalar2=EPS,
                                                       op0=ALU.mult, op1=ALU.add), r=['ss'], w=['ss1'])
                kb.op('act', REC.activation(out=ss[:, 3:4], in_=ss[:, 1:2], func=AF.Sqrt), r=['ss1'], w=['ss3'])
                kb.op('dve', REC.reciprocal(out=ss[:, 2:3], in_=ss[:, 3:4]), r=['ss3'], w=['ss2'])
                kb.op('dve', REC.scalar_tensor_tensor(out=dst_bf, in0=src, scalar=ss[:, 2:3], in1=gvec,
                                                              op0=ALU.mult, op1=ALU.mult), r=[srcreg, 'ss2', 'vrep'], w=[dstreg])

            def transpose8(src_bf, dstT, srcreg, dstreg, bank=4):
                for k in range(8):
                    kb.op('pe', REC.transpose(out=pbf(bank)[:, k * 128:(k + 1) * 128],
                                                           in_=src_bf[:, k * 128:(k + 1) * 128], identity=identb[:]),
                          r=[srcreg, 'identb'], w=['pb%d' % bank])
                kb.op('act', REC.copy(out=dstT[:].rearrange("p a b -> p (a b)"), in_=pbf(bank)[:, 0:1024]),
                      r=['pb%d' % bank], w=[dstreg])

            def proj(bank, ncols, rhs_fn, M=128, lhs_fn=None, r=()):
                for k in range(8):
                    kb.op('pe', REC.matmul(PB[bank][0:M, 0:ncols], lhsT=xnT[:, k, 0:M], rhs=rhs_fn(k),
                                                        start=(k == 0), stop=(k == 7)),
                          r=['xnT', 'w_in_sb'] + list(r), w=['pb%d' % bank])

            def rope(dst, src, nh, cs_t, tmp, scale, sreg, dreg, tmpreg):
                c = cs_t[:, 0:8].unsqueeze(1).to_broadcast([128, nh, 8])
                s = cs_t[:, 8:16].unsqueeze(1).to_broadcast([128, nh, 8])
                x1 = src[:, :, 0:8]
                x2 = src[:, :, 8:16]
                kb.op('dve', REC.scalar_tensor_tensor(out=tmp[:, 0:nh, 0, :], in0=x1, scalar=scale, in1=c, op0=ALU.mult, op1=ALU.mult), r=[sreg, 'cst'], w=[tmpreg])
                kb.op('dve', REC.scalar_tensor_tensor(out=tmp[:, 0:nh, 1, :], in0=x2, scalar=scale, in1=s, op0=ALU.mult, op1=ALU.mult), r=[sreg, 'cst'], w=[tmpreg])
                kb.op('dve', REC.scalar_tensor_tensor(out=tmp[:, 0:nh, 2, :], in0=x2, scalar=scale, in1=c, op0=ALU.mult, op1=ALU.mult), r=[sreg, 'cst'], w=[tmpreg])
                kb.op('dve', REC.scalar_tensor_tensor(out=tmp[:, 0:nh, 3, :], in0=x1, scalar=scale, in1=s, op0=ALU.mult, op1=ALU.mult), r=[sreg, 'cst'], w=[tmpreg])
                kb.op('act', REC.mul(out=dst[:, :, 16:64], in_=src[:, :, 16:64], mul=scale), r=[sreg], w=[dreg])
                kb.op('dve', REC.tensor_tensor(out=dst[:, :, 0:8], in0=tmp[:, 0:nh, 0, :], in1=tmp[:, 0:nh, 1, :], op=ALU.subtract), r=[tmpreg], w=[dreg])
                kb.op('dve', REC.tensor_tensor(out=dst[:, :, 8:16], in0=tmp[:, 0:nh, 2, :], in1=tmp[:, 0:nh, 3, :], op=ALU.add), r=[tmpreg], w=[dreg])

            def stageA(n):
                nonlocal xn, xnT
                xn, xnT = xn2[n % 2], xnT2[n % 2]
                kb.alias = {r_: r_ + '_%d' % (n % 2) for r_ in ('xnT', 'kvb', 'v_bf', 'cst')}
                xb_ = xt[n % 2]
                xreg = 'xt%d' % (n % 2)
                kb.dma('sp', REC.dma_start(out=xb_[:], in_=xl[n * 128:(n + 1) * 128, :]), w=[xreg])
                kb.dma('sp', REC.dma_start(out=cst[n % 2][:], in_=cs[n * 128:(n + 1) * 128, :]), w=['cst'])
                rmsnorm_tile(xb_[:], g_attn, xn[:], xreg, 'xn')
                transpose8(xn, xnT, 'xn', 'xnT')

            xn = xnT = None
            stageA(0)
            for g in range(2):
                for n in range(NT):
                    if not (g == 1 and n == NT - 1):
                        stageA((n + 1) % NT)
                    xn, xnT, kvf, kvb, v_bf = xn2[n % 2], xnT2[n % 2], kvf2[n % 2], kvb2[n % 2], v_bf2[n % 2]
                    kb.alias = {r_: r_ + '_%d' % (n % 2) for r_ in ('xnT', 'kvb', 'v_bf', 'cst')}
                    cst_ = cst[n % 2]
                    proj(0, 384, lambda k: w_in_sb[:, k, OFF['kc']:OFF['kc'] + 768].rearrange("p (i c) -> p i c", c=128)[:, :, 64 * g:64 * g + 64])
                    pkv = PB[0][:, 0:384].rearrange("p (i c) -> p i c", c=64)
                    kb.op('act', REC.copy(out=kvf[:], in_=pkv), r=['pb0'], w=['kvf'])
                    kview = kvf[:].rearrange("p (i two) c -> p i two c", two=2)
                    rope(kview[:, :, 0, :], kview[:, :, 0, :], 3, cst_, ktmp, 1.0, 'kvf', 'kvf', 'ktmp')
                    kb.op('act', REC.copy(out=kvb[:], in_=kvf[:]), r=['kvf'], w=['kvb'])
                    if n >= NT - OWN and SKIPKV:
                        pass
                    elif n >= NT - OWN:
                        for i6 in range(6):
                            kb.dma('sp', REC.dma_start(
                                out=kv_o[(n - (NT - OWN)) * 128:(n - (NT - OWN) + 1) * 128, i6 * 128 + g * 64:i6 * 128 + g * 64 + 64],
                                in_=kvf[:, i6, :]), r=['kvf'], w=['kv_o'])
                    kb.op('pe', REC.transpose(out=pbf(4)[0:64, 0:128], in_=kvb[:, 2, :], identity=identb[:]),
                          r=['kvb', 'identb'], w=['pb4'])
                    kb.op('act', REC.copy(out=ksT[0:64, n * 128:(n + 1) * 128], in_=pbf(4)[0:64, 0:128]), r=['pb4'], w=['ksT'])
                    kb.op('pool', REC.tensor_copy(out=vsx[:, n, 0:64], in_=kvb[:, 3, :]), r=['kvb'], w=['vsx'])
                    if n >= W0:
                        kb.op('pe', REC.transpose(out=pbf(4)[0:64, 128:256], in_=kvb[:, 4, :], identity=identb[:]),
                              r=['kvb', 'identb'], w=['pb4'])
                        kb.op('act', REC.copy(out=kwT[0:64, (n - W0) * 128:(n - W0 + 1) * 128], in_=pbf(4)[0:64, 128:256]),
                              r=['pb4'], w=['kwT'])
                        kb.op('pool', REC.tensor_copy(out=vwx[:, n - W0, 0:64], in_=kvb[:, 5, :]), r=['kvb'], w=['vwx'])
                    kb.op('pe', REC.matmul(PB[5][0:64, 0:2], lhsT=kvb[:, 0, :], rhs=wc_sb[:, g * 2:g * 2 + 2], start=True, stop=True),
                          r=['kvb', 'wc_sb'], w=['pb5'])
                    kb.op('pe', REC.matmul(PB[5][0:64, 2:4], lhsT=kvb[:, 1, :], rhs=wc_sb[:, 4 + g * 2:4 + g * 2 + 2], start=True, stop=True),
                          r=['kvb', 'wc_sb'], w=['pb5'])
                    kb.op('dve', REC.tensor_copy(out=kcT[:, 2 * n:2 * n + 2], in_=PB[5][0:64, 0:2]), r=['pb5'], w=['kcT'])
                    kb.op('dve', REC.tensor_copy(out=vcT[:, 2 * n:2 * n + 2], in_=PB[5][0:64, 2:4]), r=['pb5'], w=['vcT'])

                    if g == 0:
                        need_o = n >= HALO
                        proj(1, 512, lambda k: w_in_sb[:, k, OFF['gv']:OFF['gv'] + 512])
                        kb.op('act', REC.copy(out=v_bf[:], in_=PB[1][:, :]), r=['pb1'], w=['v_bf'])
                        if need_o:
                            proj(1, 512, lambda k: w_in_sb[:, k, OFF['gr']:OFF['gr'] + 512])
                            kb.op('act', REC.activation(out=sgr[:], in_=PB[1][:, :], func=AF.Silu), r=['pb1'], w=['sgr'])
                        for k in range(8):
                            kb.op('pe', REC.matmul(PB[5][0:16, 128:256], lhsT=w_in_sb[:, k, OFF['glr']:OFF['glr'] + 16], rhs=xnT[:, k, :],
                                                                start=(k == 0), stop=(k == 7)), r=['xnT', 'w_in_sb'], w=['pb5'])
                        kb.op('dve', REC.tensor_copy(out=glr_x[0:16, :], in_=PB[5][0:16, 128:256]), r=['pb5'], w=['glr_x'])
                        for which, bank in ((('gq', 6), ('gk', 7)) if need_o else (('gk', 7),)):
                            for hh in range(4):
                                for k in range(8):
                                    kb.op('pe', REC.matmul(PB[bank][0:64, 128 * hh:128 * hh + 128], lhsT=w_in_sb[:, k, OFF[which] + 64 * hh:OFF[which] + 64 * hh + 64],
                                                           rhs=xnT[:, k, :], start=(k == 0), stop=(k == 7)), r=['xnT', 'w_in_sb'], w=['pb%d' % bank])
                        for hh in range(4):
                            kb.op('pe', REC.matmul(PB[2][0:64, 128 * hh:128 * hh + 128], lhsT=wgg_sb[0:17, 64 * hh:64 * hh + 64], rhs=glr_x[0:17, :],
                                                   start=True, stop=True), r=['glr_x', 'wgg_sb'], w=['pb2'])
                        G = G4
                        kb.op('act', REC.activation(out=G[:, 0, :], in_=PB[2][0:64, :], func=AF.Exp, scale=-1.0), r=['pb2'], w=['G0'])
                        kb.op('act', REC.activation(out=G[:, 0, :], in_=G[:, 0, :], func=AF.Ln, bias=1.0), r=['G0'], w=['G0'])
                        kb.op('dve', REC.tensor_tensor_scan(out=G[:, 1, :], data0=rmask4[:], data1=G[:, 0, :], initial=0.0, op0=ALU.mult, op1=ALU.add),
                              r=['G0', 'rmask4'], w=['G1'])
                        kb.op('act', REC.activation(out=G[:, 2, :], in_=G[:, 1, :], func=AF.Exp, scale=-1.0 / 16), r=['G1'], w=['G2'])
                        kb.op('act', REC.activation(out=G[:, 3, :], in_=G[:, 1, :], func=AF.Exp, scale=1.0 / 16), r=['G1'], w=['G3'])
                        kb.op('dve', REC.tensor_scalar(out=sg[:, 0:8].rearrange("p (h c) -> p h c", c=2), in0=G[:, 1, :].rearrange("p (h t) -> p h t", t=128)[:, :, 63:128:64],
                                                       scalar1=-1.0 / 16, scalar2=None, op0=ALU.mult), r=['G1'], w=['sg0'])
                        kb.op('act', REC.activation(out=sg[:, 8:16], in_=sg[:, 0:8], func=AF.Exp), r=['sg0'], w=['sg8'])
                        kb.op('dve', REC.tensor_tensor(out=G[:, 0, :].rearrange("p (a t) -> p a t", t=64), in0=G[:, 3, :].rearrange("p (a t) -> p a t", t=64),
                                                       in1=sg[:, 8:16].unsqueeze(2).to_broadcast([64, 8, 64]), op=ALU.mult), r=['G3', 'sg8', 'G0'], w=['G0'])
                        kb.op('dve', REC.tensor_tensor(out=gb[:, 2, :], in0=PB[7][0:64, :], in1=G[:, 0, :], op=ALU.mult), r=['pb7', 'G0'], w=['gb2'])
                        for hh in range(4):
                            kb.op('pe', REC.transpose(out=pbf(4)[:, hh * 64:(hh + 1) * 64], in_=gb[:, 2, 128 * hh:128 * hh + 128], identity=identb[0:64, 0:64]),
                                  r=['gb2', 'identb'], w=['pb4'])
                        kb.op('act', REC.copy(out=kd_tok[:].rearrange("p a b -> p (a b)"), in_=pbf(4)[:, 0:256]), r=['pb4'], w=['kd_tok'])
                        if need_o:
                            kb.op('dve', REC.scalar_tensor_tensor(out=gb[:, 0, :], in0=PB[6][0:64, :], scalar=0.125, in1=G[:, 2, :], op0=ALU.mult, op1=ALU.mult),
                                  r=['pb6', 'G2'], w=['gb0'])
                            kb.op('dve', REC.tensor_tensor(out=gb[:, 1, :], in0=PB[7][0:64, :], in1=G[:, 3, :], op=ALU.mult), r=['pb7', 'G3'], w=['gb1'])
                            gb0v = gb[:, 0, :].rearrange("p (h t) -> p h t", t=128)
                            kb.op('pool', REC.tensor_copy(out=qez[:, 0, :, 0:64], in_=gb0v[:, :, 0:64]), r=['gb0'], w=['qez'])
                            kb.op('pool', REC.tensor_copy(out=qez[:, 1, :, 64:128], in_=gb0v[:, :, 64:128]), r=['gb0'], w=['qez'])
                            for hh in range(4):
                                kb.op('pe', REC.matmul(PB[3][:, 128 * hh:128 * hh + 128], lhsT=gb[:, 1, 128 * hh:128 * hh + 128], rhs=gb[:, 0, 128 * hh:128 * hh + 128],
                                                       start=True, stop=True), r=['gb0', 'gb1'], w=['pb3'])
                            kb.op('dve', REC.tensor_tensor(out=attm[:], in0=PB[3][:, :].rearrange("p (h t) -> p h t", t=128),
                                                           in1=tribd.unsqueeze(1).to_broadcast([128, 4, 128]), op=ALU.mult), r=['pb3', 'CM'], w=['attm'])

                        def upd(c):
                            for hh in range(4):
                                kb.op('pe', REC.matmul(PB[0][0:64, 128 * hh:128 * hh + 128], lhsT=kd_tok[64 * c:64 * c + 64, hh, :],
                                                       rhs=v_bf[64 * c:64 * c + 64, 128 * hh:128 * hh + 128], start=True, stop=True), r=['kd_tok', 'v_bf'], w=['pb0'])
                            ebc = sg[:, 8:16].rearrange("p (h c) -> p h c", c=2)[:, :, c:c + 1].to_broadcast([64, 4, 128])
                            kb.op('dve', REC.tensor_tensor(out=hst[:], in0=hst[:], in1=ebc, op=ALU.mult), r=['hst', 'sg8'], w=['hst'])
                            kb.op('dve', REC.tensor_tensor(out=hst[:].rearrange("p a b -> p (a b)"), in0=hst[:].rearrange("p a b -> p (a b)"), in1=PB[0][0:64, :], op=ALU.add),
                                  r=['hst', 'pb0'], w=['hst'])
                        if need_o:
                            upd(0)
                            kb.op('act', REC.copy(out=hbf1[:], in_=hst[:]), r=['hst'], w=['hbf1'])
                            for hh in range(4):
                                kb.op('pe', REC.matmul(PB[2][:, 128 * hh:128 * hh + 128], lhsT=attm[:, hh, :], rhs=v_bf[:, 128 * hh:128 * hh + 128], start=True, stop=False),
                                      r=['attm', 'v_bf'], w=['pb2'])
                                kb.op('pe', REC.matmul(PB[2][:, 128 * hh:128 * hh + 128], lhsT=qez[:, 0, hh, :], rhs=hbf[:, hh, :], start=False, stop=False),
                                      r=['qez', 'hbf'], w=['pb2'])
                                kb.op('pe', REC.matmul(PB[2][:, 128 * hh:128 * hh + 128], lhsT=qez[:, 1, hh, :], rhs=hbf1[:, hh, :], start=False, stop=True),
                                      r=['qez', 'hbf1'], w=['pb2'])
                            upd(1)
                            kb.op('act', REC.copy(out=hbf[:], in_=hst[:]), r=['hst'], w=['hbf'])
                            og4 = og[:].rearrange("p (h v) -> p h v", v=128)
                            kb.op('act', REC.activation(out=og[:], in_=PB[2][:, :], func=AF.Square), r=['pb2'], w=['og'])
                            kb.op('dve', REC.tensor_reduce(out=sm8[:, 0:4], in_=og4, axis=AX.X, op=ALU.add), r=['og'], w=['sm8a'])
                            kb.op('dve', REC.tensor_scalar(out=sm8[:, 0:4], in0=sm8[:, 0:4], scalar1=1.0 / 128, scalar2=EPS, op0=ALU.mult, op1=ALU.add), r=['sm8a'], w=['sm8a'])
                            kb.op('act', REC.activation(out=sm8[:, 4:8], in_=sm8[:, 0:4], func=AF.Sqrt), r=['sm8a'], w=['sm8b'])
                            kb.op('dve', REC.reciprocal(out=sm8[:, 8:12], in_=sm8[:, 4:8]), r=['sm8b'], w=['sm8c'])
                            kb.op('dve', REC.tensor_tensor(out=og4, in0=PB[2][:, :].rearrange("p (h v) -> p h v", v=128),
                                                           in1=sm8[:, 8:12].unsqueeze(2).to_broadcast([128, 4, 128]), op=ALU.mult), r=['pb2', 'sm8c', 'og'], w=['og'])
                            kb.op('dve', REC.tensor_tensor(out=og[:], in0=og[:], in1=g_gla, op=ALU.mult), r=['og', 'vrep'], w=['og'])
                            kb.op('dve', REC.tensor_tensor(out=ogb[:], in0=og[:], in1=sgr[:], op=ALU.mult), r=['og', 'sgr'], w=['ogb'])
                        else:
                            upd(0)
                            upd(1)
                        if need_o:
                            kb.dma('sp', REC.dma_start(out=oscr[(n - HALO) * 128:(n - HALO + 1) * 128, 0:512], in_=ogb[:]),
                                   r=['ogb'], w=['oscr'])
                    if n < HALO:
                        continue
                    proj(1, 256, lambda k: w_in_sb[:, k, OFF['nq'] + 256 * g:OFF['nq'] + 256 * g + 256], r=())
                    for k in range(8):
                        kb.op('pe', REC.matmul(PB[1][:, 256:268], lhsT=xnT[:, k, :], rhs=w_in_sb[:, k, OFF['ng'] + 12 * g:OFF['ng'] + 12 * g + 12],
                                                            start=(k == 0), stop=(k == 7)), r=['xnT', 'w_in_sb'], w=['pb1'])
                    kb.op('dve', REC.tensor_tensor(out=gts[:], in0=PB[1][:, 256:268], in1=b_ng[:, 12 * g:12 * g + 12], op=ALU.add),
                          r=['pb1', 'vrep'], w=['gts'])
                    kb.op('act', REC.activation(out=gts[:], in_=gts[:], func=AF.Sigmoid), r=['gts'], w=['gts'])
                    pq = PB[1][:, 0:256].rearrange("p (h c) -> p h c", c=64)
                    rope(qf[:], pq, 4, cst_, qtmp, 0.125, 'pb1', 'qf', 'qtmp')
                    dump('qf', qf[:].rearrange('p a b -> p (a b)'), ['qf'], g, n)
                    dump('gts', gts[:], ['gts'], g, n)
                    kb.op('act', REC.copy(out=qpair[:, :, 0:64], in_=qf[:]), r=['qf'], w=['qpair'])
                    for h in range(4):
                        kb.op('pe', REC.transpose(out=pbf(4)[0:64, 512 + h * 128:512 + (h + 1) * 128], in_=qpair[:, h, 0:64], identity=identb[:]),
                              r=['qpair', 'identb'], w=['pb4'])
                    kb.op('act', REC.copy(out=qT[0:64, :, :].rearrange("p a b -> p (a b)"), in_=pbf(4)[0:64, 512:1024]), r=['pb4'], w=['qT'])
                    for t in range(NBT):
                        kb.op('pe', REC.transpose(out=pbf(4)[0:BT, t * 64:(t + 1) * 64], in_=vcT[:, t * BT:(t + 1) * BT], identity=identb[0:64, 0:64]),
                              r=['vcT', 'identb'], w=['pb4'])
                    kb.op('act', REC.copy(out=vcs[:].rearrange("p a b -> p (a b)"), in_=pbf(4)[0:BT, 0:NBT * 64]), r=['pb4'], w=['vcs'])
                    hb_ = 512 // NB if NB <= 512 else 1
                    for h in range(4):
                        bank = 5 + (h // hb_) if hb_ < 4 else 5
                        col = (h % hb_) * NB
                        kb.op('pe', REC.matmul(PB[bank][:, col:col + NB], lhsT=qT[0:64, h, :], rhs=kcT[:, :], start=True, stop=True),
                              r=['qT', 'kcT'], w=['pb%d' % bank])
                    kb.op('dve', REC.tensor_scalar(out=dd[:], in0=D0, scalar1=float(128 * n), scalar2=None, op0=ALU.add), r=['CM'], w=['dd'])
                    kb.op('dve', REC.scalar_tensor_tensor(out=cb[:], in0=dd[:], scalar=63.0, in1=valid, op0=ALU.is_ge, op1=ALU.mult), r=['dd', 'CM'], w=['cb'])
                    kb.op('dve', REC.tensor_scalar(out=cb[:], in0=cb[:], scalar1=-1.0, scalar2=-NEG, op0=ALU.add, op1=ALU.mult), r=['cb'], w=['cb'])
                    for h in range(4):
                        bank = 5 + (h // hb_) if hb_ < 4 else 5
                        col = (h % hb_) * NB
                        kb.op('dve', REC.tensor_tensor(out=smx[:, h, :], in0=PB[bank][:, col:col + NB], in1=cb[:], op=ALU.add),
                              r=['pb%d' % bank, 'cb'], w=['smx'])
                    kb.op('dve', REC.tensor_reduce(out=rc[:, 0:4], in_=smx[:], axis=AX.X, op=ALU.max), r=['smx'], w=['rc0'])
                    kb.op('dve', REC.tensor_scalar(out=rc[:, 4:8], in0=rc[:, 0:4], scalar1=-1000.0, scalar2=-1.0, op0=ALU.max, op1=ALU.mult), r=['rc0'], w=['rc1'])
                    for h in range(4):
                        kb.op('act', REC.activation(out=smx[:, h, :], in_=smx[:, h, :], func=AF.Exp, bias=rc[:, 4 + h:5 + h], accum_out=rc[:, 8 + h:9 + h]),
                              r=['smx', 'rc1'], w=['smx', 'rc2'])
                    kb.op('dve', REC.tensor_scalar(out=rc[:, 8:12], in0=rc[:, 8:12], scalar1=1e-30, scalar2=None, op0=ALU.max), r=['rc2'], w=['rc2'])
                    kb.op('dve', REC.reciprocal(out=rc[:, 12:16], in_=rc[:, 8:12]), r=['rc2'], w=['rc3'])
                    kb.op('dve', REC.tensor_tensor(out=smx[:], in0=smx[:], in1=rc[:, 12:16].unsqueeze(2).to_broadcast([128, 4, NB]), op=ALU.mult),
                          r=['smx', 'rc3'], w=['smx'])
                    dump('cb', cb[:], ['cb'], g, n)
                    dump('p', smx[:].rearrange('p a b -> p (a b)'), ['smx'], g, n)
                    kb.op('act', REC.copy(out=pcb, in_=smx[:]), r=['smx'], w=['PT0', 'PT1'])
                    kb.op('dve', REC.tensor_reduce(out=sc[:], in_=smx[:].rearrange("p h b -> p b h"), axis=AX.X, op=ALU.add), r=['smx'], w=['sc'])
                    for h in range(4):
                        for t in range(NBT):
                            kb.op('pe', REC.transpose(out=pbf(4)[0:BT, (h * NBT + t) * 128:(h * NBT + t + 1) * 128],
                                                                        in_=pcb[:, h, t * BT:(t + 1) * BT], identity=identb[:]),
                                  r=['PT0', 'PT1', 'identb'], w=['pb4'])
                    kb.op('act', REC.copy(out=pT[:].rearrange("p a b c -> p (a b c)"), in_=pbf(4)[0:BT, 0:4 * NBT * 128]), r=['pb4'], w=['pT'])
                    for h in range(4):
                        for t in range(NBT):
                            kb.op('pe', REC.matmul(PB[0][:, h * 64:(h + 1) * 64], lhsT=pT[:, h, t, :], rhs=vcs[:, t, :],
                                                                     start=(t == 0), stop=(t == NBT - 1)), r=['pT', 'vcs'], w=['pb0'])
                    kb.op('act', REC.copy(out=ocm[:].rearrange("p a b -> p (a b)"), in_=PB[0][:, 0:256]), r=['pb0'], w=['ocm'])
                    dump('ocmp', ocm[:].rearrange('p a b -> p (a b)'), ['ocm'], g, n)
                    dump('score0', sc[:], ['sc'], g, n)
                    kb.op('dve', REC.tensor_scalar(out=t1[:], in0=dd[:], scalar1=0.0, scalar2=None, op0=ALU.is_ge), r=['dd'], w=['t1'])
                    kb.op('dve', REC.scalar_tensor_tensor(out=t2[:], in0=dd[:], scalar=128.0, in1=t1[:], op0=ALU.is_lt, op1=ALU.mult), r=['dd', 't1'], w=['cb'])
                    kb.op('dve', REC.tensor_tensor(out=t2[:], in0=t2[:], in1=first, op=ALU.max), r=['cb', 'CM'], w=['cb'])
                    kb.op('dve', REC.tensor_tensor(out=t1[:], in0=t1[:], in1=valid, op=ALU.mult), r=['t1', 'CM'], w=['t1'])
                    kb.op('dve', REC.scalar_tensor_tensor(out=sc[:], in0=t2[:], scalar=5.0, in1=sc[:], op0=ALU.mult, op1=ALU.max), r=['cb', 'sc'], w=['sc'])
                    kb.op('dve', REC.scalar_tensor_tensor(out=sc[:], in0=sc[:], scalar=1.0, in1=t1[:], op0=ALU.add, op1=ALU.mult), r=['sc', 't1'], w=['sc'])
                    kb.op('dve', REC.tensor_scalar(out=sc[:], in0=sc[:], scalar1=-1.0, scalar2=None, op0=ALU.add), r=['sc'], w=['sc'])
                    kb.op('dve', REC.max(out=m8[:, 0:8], in_=sc[:]), r=['sc'], w=['m8a'])
                    kb.op('dve', REC.match_replace(out=sc2[:], in_to_replace=m8[:, 0:8], in_values=sc[:], imm_value=-2.0), r=['sc', 'm8a'], w=['sc2'])
                    kb.op('dve', REC.max(out=m8[:, 8:16], in_=sc2[:]), r=['sc2'], w=['m8b'])
                    kb.op('dve', REC.scalar_tensor_tensor(out=selb[:], in0=sc[:], scalar=m8[:, 15:16], in1=t1[:], op0=ALU.is_ge, op1=ALU.mult),
                          r=['sc', 'm8b', 't1'], w=['selb'])
                    kb.op('dve', REC.tensor_scalar(out=selb[:], in0=selb[:], scalar1=-1.0, scalar2=-NEG, op0=ALU.add, op1=ALU.mult), r=['selb'], w=['selb'])
                    dump('score', sc[:], ['sc'], g, n)
                    dump('m8', m8[:], ['m8a', 'm8b'], g, n)
                    dump('selb', selb[:], ['selb'], g, n)
                    nm_need = (2 * n + 1) // 64 + 1
                    for m in range(nm_need):
                        wdt = min(64, NB)
                        kb.op('dve', REC.tensor_copy(out=qpair[:, :, 64:64 + wdt],
                                                                           in_=selb[:, m * 64:m * 64 + wdt].unsqueeze(1).to_broadcast([128, 4, wdt])),
                              r=['selb', 'qpair'], w=['qpair'])
                        for h in range(4):
                            kb.op('pe', REC.transpose(out=pbf(4)[:, h * 128:(h + 1) * 128], in_=qpair[:, h, :], identity=identb[:]),
                                  r=['qpair', 'identb'], w=['pb4'])
                        kb.op('act', REC.copy(out=rhsm[:, m, :], in_=pbf(4)[:, 0:512]), r=['pb4'], w=['rhsm'])
                    tri_flat = tri4[:].rearrange("p a b -> p (a b)")
                    qT_flat = qT[:].rearrange("p a b -> p (a b)")
                    jobs = []
                    for j in range(n + 1):
                        jobs.append(dict(l=ksT[:, j * 128:(j + 1) * 128], r=rhsm[:, (2 * j) // 64, :], rr=['ksT', 'rhsm'],
                                         bias=(tri_flat, 'tri4') if j == n else None, v=vsx[:, j, :], vr='vsx', acc=2, first=(j == 0), last=(j == n)))
                    for t in range(5):
                        j = n - 4 + t
                        jobs.append(dict(l=kwT[:, (j - W0) * 128:(j - W0 + 1) * 128], r=qT_flat, rr=['kwT', 'qT'],
                                         bias=(wb4[:, t, :, :].rearrange("p a b -> p (a b)"), 'wb4'), v=vwx[:, j - W0, :], vr='vwx', acc=3, first=(t == 0), last=(t == 4)))
                    NBUF = len(SBANKS)
                    LOOK = NBUF - 1
                    for idx in range(len(jobs) + LOOK):
                        if idx < len(jobs):
                            jb = jobs[idx]
                            bank = SBANKS[idx % NBUF]
                            breg = 'pb%d' % bank
                            kb.op('pe', REC.matmul(PB[bank][:, :], lhsT=jb['l'], rhs=jb['r'], start=True, stop=(jb['bias'] is None)), r=jb['rr'], w=[breg])
                            if jb['bias'] is not None:
                                kb.op('pe', REC.matmul(PB[bank][:, :], lhsT=identb[:], rhs=jb['bias'][0], start=False, stop=True), r=['identb', jb['bias'][1]], w=[breg])
                        k_ = idx - LOOK
                        if k_ >= 0:
                            jb = jobs[k_]
                            bank = SBANKS[k_ % NBUF]
                            breg = 'pb%d' % bank
                            preg = 'PT%d' % (k_ % NBUF)
                            kb.op('act', REC.activation(out=PT[k_ % NBUF], in_=PB[bank][:, :], func=AF.Exp), r=[breg], w=[preg])
                            kb.op('pe', REC.matmul(PB[jb['acc']][0:65, :], lhsT=jb['v'], rhs=PT[k_ % NBUF], start=jb['first'], stop=jb['last']),
                                  r=[jb['vr'], preg], w=['pb%d' % jb['acc']])
                    kb.op('act', REC.copy(out=oT[:, 0, :], in_=PB[2][0:65, :]), r=['pb2'], w=['oT0'])
                    kb.op('act', REC.copy(out=oT[:, 1, :], in_=PB[3][0:65, :]), r=['pb3'], w=['oT1'])
                    dump('oT0', oT[:, 0, :], ['oT0'], g, n)
                    dump('oT1', oT[:, 1, :], ['oT1'], g, n)
                    for br in range(2):
                        for h in range(4):
                            kb.op('pe', REC.transpose(out=PB[5 + br][:, h * 65:(h + 1) * 65], in_=oT[:, br, h * 128:(h + 1) * 128],
                                                                          identity=identf[0:65, 0:65]), r=['oT%d' % br, 'identf'], w=['pb%d' % (5 + br)])
                    gv = gts[:].rearrange("p (h c) -> p h c", c=3)
                    for br in range(2):
                        pv = PB[5 + br][:, 0:260].rearrange("p (h c) -> p h c", c=65)
                        kb.op('dve', REC.tensor_scalar(out=rc[:, 4 * br:4 * br + 4].unsqueeze(2), in0=pv[:, :, 64:65], scalar1=1e-30, scalar2=None,
                                                                             op0=ALU.max), r=['pb%d' % (5 + br)], w=['rcf%d' % br])
                        kb.op('dve', REC.reciprocal(out=rc[:, 4 * br:4 * br + 4], in_=rc[:, 4 * br:4 * br + 4]), r=['rcf%d' % br], w=['rcf%d' % br])
                        kb.op('dve', REC.tensor_tensor(out=rc[:, 4 * br:4 * br + 4].unsqueeze(2), in0=rc[:, 4 * br:4 * br + 4].unsqueeze(2),
                                                                      in1=gv[:, :, 1 + br:2 + br], op=ALU.mult), r=['rcf%d' % br, 'gts'], w=['rcf%d' % br])
                    kb.op('dve', REC.tensor_tensor(out=ocm[:], in0=ocm[:], in1=gv[:, :, 0:1].to_broadcast([128, 4, 64]), op=ALU.mult), r=['ocm', 'gts'], w=['ocm'])
                    for br in range(2):
                        pv = PB[5 + br][:, 0:260].rearrange("p (h c) -> p h c", c=65)
                        kb.op('dve', REC.tensor_tensor(out=qf[:], in0=pv[:, :, 0:64],
                                                                             in1=rc[:, 4 * br:4 * br + 4].unsqueeze(2).to_broadcast([128, 4, 64]), op=ALU.mult),
                              r=['pb%d' % (5 + br), 'rcf%d' % br, 'qf'], w=['qf'])
                        kb.op('dve', REC.tensor_tensor(out=ocm[:], in0=ocm[:], in1=qf[:], op=ALU.add), r=['ocm', 'qf'], w=['ocm'])
                    dump('ofin', ocm[:].rearrange('p a b -> p (a b)'), ['ocm'], g, n)
                    kb.op('act', REC.copy(out=onb[:, 0:256], in_=ocm[:].rearrange("p a b -> p (a b)")), r=['ocm'], w=['onb'])
                    kb.dma('sp', REC.dma_start(out=oscr[(n - HALO) * 128:(n - HALO + 1) * 128, 512 + 256 * g:768 + 256 * g], in_=onb[:, 0:256]),
                           r=['onb'], w=['oscr'])
            kb.dma('sp', REC.dma_start(out=gl_o.rearrange("h k v -> k h v"), in_=hst[:]), r=['hst'], w=['gl_o'])

        kb.alias = {}
        CMst.close()
        kb.barrier()

        if do_sample:
          kb.barrier()
          with contextlib.ExitStack() as stS:
            cS = sb(stS, "cS", [NS, 16 + 64 * 16 + 128 * NREP + 16 + 128 + NPG + 16], F32)
            kb.dma('sp', REC.dma_start(out=cS[:], in_=scs_i), w=['cS'])
            csS = cS[:, 0:16]
            o_ = 16
            selS = cS[:, o_:o_ + 1024].rearrange("p (s m) -> p s m", m=64)
            o_ += 1024
            RepAll = cS[:, o_:o_ + 128 * NREP].rearrange("p (r m) -> p r m", m=128)
            o_ += 128 * NREP
            id16 = cS[:, o_:o_ + 16]
            o_ += 16
            o_ += 128
            ptabs = cS[:, o_:o_ + NPG]
            o_ += NPG
            slotw = cS[:, o_:o_ + 16]
            cR = sb(stS, "cR", [128, NREP * 16 + 256 + 64 + 4 * 64 + 1], F32)
            kb.dma('sp', REC.dma_start(out=cR[:], in_=rep_i), w=['cR'])
            BAll = cR[:, 0:NREP * 16].rearrange("p (r m) -> p r m", m=16)
            eye16 = cR[0:64, NREP * 16:NREP * 16 + 256].rearrange("p (a b) -> p a b", b=16)
            maskW = cR[:, NREP * 16 + 256:NREP * 16 + 320]
            wrow = cR[:, NREP * 16 + 320:NREP * 16 + 576].rearrange("p (t c g) -> p t c g", t=2, g=2)
            slotm = cR[:, NREP * 16 + 576:NREP * 16 + 577]
            vrS = sb(stS, "vrS", [128, 1560], F32)
            kb.dma('sp', REC.dma_start(out=vrS[:], in_=vecsA.partition_broadcast(128)), w=['vrS'])
            zs = sb(stS, "zs", [NS, DIN], F32)
            qs = sb(stS, "qs", [NS, 8, 64], F32)
            gS = sb(stS, "gS", [NS, 24], F32)
            oall = sb(stS, "oall", [NS, 4, 8, 64], F32)
            ogs = sb(stS, "ogs", [NS, 512], F32)
            smS = sb(stS, "smS", [128, 64], F32)
            tmpS = sb(stS, "tmpS", [128, 8192], F32)
            scS = sb(stS, "scS", [NS, 2, NBP], F32)
            with contextlib.ExitStack() as stSa:
                w_in_sb = sb(stSa, "w_in_sbS", [128, 8, DIN], BF16)
                for k in range(8):
                    for c0 in (0, 1428):
                        kb.dma('pool', REC.dma_start(out=w_in_sb[:, k, c0:c0 + 1428], in_=w_in[k * 128:(k + 1) * 128, c0:c0 + 1428]), w=['w_in_sbS'])
                wgg_sb = sb(stSa, "wgg_sbS", [17, 256], BF16)
                kb.dma('pool', REC.dma_start(out=wgg_sb[:], in_=wgg), w=['wgg_sbS'])
                xsb = sb(stSa, "xsb", [NS, D], F32)
                xnS = sb(stSa, "xnS", [NS, D], BF16)
                xnTS = sb(stSa, "xnTS", [128, 8, NS], BF16)
                junkS = tmpS
                kb.dma('sp', REC.dma_start(out=xsb[:], in_=xs_i), w=['xsb'])
                kb.op('act', REC.activation(out=junkS[0:NS, 0:D], in_=xsb[:], func=AF.Square, accum_out=smS[0:NS, 0:1]), r=['xsb'], w=['tmpS', 'smS0'])
                kb.op('dve', REC.tensor_scalar(out=smS[0:NS, 1:2], in0=smS[0:NS, 0:1], scalar1=1.0 / D, scalar2=EPS, op0=ALU.mult, op1=ALU.add), r=['smS0'], w=['smS1'])
                kb.op('act', REC.activation(out=smS[0:NS, 2:3], in_=smS[0:NS, 1:2], func=AF.Sqrt), r=['smS1'], w=['smS2'])
                kb.op('dve', REC.reciprocal(out=smS[0:NS, 3:4], in_=smS[0:NS, 2:3]), r=['smS2'], w=['smS3'])
                kb.op('dve', REC.scalar_tensor_tensor(out=xnS[:], in0=xsb[:], scalar=smS[0:NS, 3:4], in1=vrS[0:NS, 0:1024], op0=ALU.mult, op1=ALU.mult),
                      r=['xsb', 'smS3', 'vrS'], w=['xnS'])
                for k in range(8):
                    kb.op('pe', REC.transpose(out=pbf(4)[:, k * NS:(k + 1) * NS], in_=xnS[:, k * 128:(k + 1) * 128], identity=identb[0:NS, 0:NS]),
                          r=['xnS', 'identb'], w=['pb4'])
                kb.op('act', REC.copy(out=xnTS[:].rearrange("p a b -> p (a b)"), in_=pbf(4)[:, 0:8 * NS]), r=['pb4'], w=['xnTS'])
                for ci, c0 in enumerate(range(0, DIN, 512)):
                    c1 = min(DIN, c0 + 512)
                    bank = ci % 2
                    for k in range(8):
                        kb.op('pe', REC.matmul(PB[bank][0:NS, 0:c1 - c0], lhsT=xnTS[:, k, :], rhs=w_in_sb[:, k, c0:c1], start=(k == 0), stop=(k == 7)),
                              r=['xnTS', 'w_in_sbS'], w=['pb%d' % bank])
                    kb.op('act', REC.copy(out=zs[:, c0:c1], in_=PB[bank][0:NS, 0:c1 - c0]), r=['pb%d' % bank], w=['zs'])

                def ropeS(dst, src, nh, scale):
                    c = csS[:, 0:8].unsqueeze(1).to_broadcast([NS, nh, 8])
                    sn = csS[:, 8:16].unsqueeze(1).to_broadcast([NS, nh, 8])
                    tmp = tmpS[0:NS, 0:nh * 32].rearrange("p (h a b) -> p h a b", a=4, b=8)
                    x1 = src[:, :, 0:8]
                    x2 = src[:, :, 8:16]
                    kb.op('dve', REC.scalar_tensor_tensor(out=tmp[:, :, 0, :], in0=x1, scalar=scale, in1=c, op0=ALU.mult, op1=ALU.mult), r=['zs', 'cS'], w=['tmpS'])
                    kb.op('dve', REC.scalar_tensor_tensor(out=tmp[:, :, 1, :], in0=x2, scalar=scale, in1=sn, op0=ALU.mult, op1=ALU.mult), r=['zs', 'cS'], w=['tmpS'])
                    kb.op('dve', REC.scalar_tensor_tensor(out=tmp[:, :, 2, :], in0=x2, scalar=scale, in1=c, op0=ALU.mult, op1=ALU.mult), r=['zs', 'cS'], w=['tmpS'])
                    kb.op('dve', REC.scalar_tensor_tensor(out=tmp[:, :, 3, :], in0=x1, scalar=scale, in1=sn, op0=ALU.mult, op1=ALU.mult), r=['zs', 'cS'], w=['tmpS'])
                    kb.op('dve', REC.tensor_scalar(out=dst[:, :, 16:64], in0=src[:, :, 16:64], scalar1=scale, scalar2=None, op0=ALU.mult), r=['zs'], w=['zs', 'qs'])
                    kb.op('dve', REC.tensor_tensor(out=dst[:, :, 0:8], in0=tmp[:, :, 0, :], in1=tmp[:, :, 1, :], op=ALU.subtract), r=['tmpS'], w=['zs', 'qs'])
                    kb.op('dve', REC.tensor_tensor(out=dst[:, :, 8:16], in0=tmp[:, :, 2, :], in1=tmp[:, :, 3, :], op=ALU.add), r=['tmpS'], w=['zs', 'qs'])
                ropeS(qs[:], zs[:, OFF['nq']:OFF['nq'] + 512].rearrange("p (h c) -> p h c", c=64), 8, 0.125)
                for nm in ('kc', 'ks', 'kw'):
                    v_ = zs[:, OFF[nm]:OFF[nm] + 128].rearrange("p (h c) -> p h c", c=64)
                    ropeS(v_, v_, 2, 1.0)
                kb.dma('sp', REC.dma_start(out=skv_o, in_=zs[:, OFF['kc']:OFF['kc'] + 768]), r=['zs'], w=['skv_o'])
                kb.op('dve', REC.tensor_tensor(out=gS[:], in0=zs[:, OFF['ng']:OFF['ng'] + 24], in1=vrS[0:NS, 1536:1560], op=ALU.add), r=['zs', 'vrS'], w=['gS'])
                kb.op('act', REC.activation(out=gS[:], in_=gS[:], func=AF.Sigmoid), r=['gS'], w=['gS'])
                kb.dma('sp', REC.dma_start(out=swk_o[:, 0:511, :], in_=wink_i[:, 1:512, :]), w=['swk_o'])
                kb.dma('sp', REC.dma_start(out=swv_o[:, 0:511, :], in_=winv_i[:, 1:512, :]), w=['swv_o'])
                kb.dma('sp', REC.dma_start(out=swk_o[:, 511, :], in_=zs[:, OFF['kw']:OFF['kw'] + 128]), r=['zs'], w=['swk_o'])
                kb.dma('sp', REC.dma_start(out=swv_o[:, 511, :], in_=zs[:, OFF['vw']:OFF['vw'] + 128]), r=['zs'], w=['swv_o'])
                glrS = sb(stSa, "glrS", [32, NS], BF16)
                kb.op('pool', REC.memset(glrS[:], 1.0), w=['glrS'])
                kb.op('pe', REC.transpose(out=PB[6][0:16, 0:NS], in_=zs[:, OFF['glr']:OFF['glr'] + 16], identity=identf[0:NS, 0:NS]), r=['zs', 'identf'], w=['pb6'])
                kb.op('dve', REC.tensor_copy(out=glrS[0:16, :], in_=PB[6][0:16, 0:NS]), r=['pb6'], w=['glrS'])
                kb.op('pe', REC.matmul(PB[6][0:NS, 256:512], lhsT=glrS[0:17, :], rhs=wgg_sb[0:17, :], start=True, stop=True), r=['glrS', 'wgg_sbS'], w=['pb6'])
                aS = sb(stSa, "aS", [NS, 256], F32)
                kb.op('act', REC.activation(out=aS[:], in_=PB[6][0:NS, 256:512], func=AF.Exp, scale=-1.0), r=['pb6'], w=['aS'])
                kb.op('act', REC.activation(out=aS[:], in_=aS[:], func=AF.Ln, bias=1.0), r=['aS'], w=['aS'])
                kb.op('act', REC.activation(out=aS[:], in_=aS[:], func=AF.Exp, scale=-1.0 / 16), r=['aS'], w=['aS'])
                gT = sb(stSa, "gT", [64, 3, 4, NS], F32)
                for wi, src in enumerate((aS[:], zs[:, OFF['gk']:OFF['gk'] + 256], zs[:, OFF['gq']:OFF['gq'] + 256])):
                    for h in range(4):
                        kb.op('pe', REC.transpose(out=PB[7][0:64, (wi * 4 + h) * NS:(wi * 4 + h + 1) * NS], in_=src[:, 64 * h:64 * h + 64], identity=identf[0:NS, 0:NS]),
                              r=['aS', 'zs', 'identf'], w=['pb7'])
                kb.op('act', REC.copy(out=gT[:].rearrange("p a b c -> p (a b c)"), in_=PB[7][0:64, 0:12 * NS]), r=['pb7'], w=['gT'])
                kb.op('dve', REC.tensor_scalar(out=gT[:, 2, :, :], in0=gT[:, 2, :, :], scalar1=0.125, scalar2=None, op0=ALU.mult), r=['gT'], w=['gT'])
                hS = sb(stSa, "hS", [64, NS * 4, 128], F32)
                kb.dma('sp', REC.dma_start(out=hS[:], in_=sgla_i.rearrange("s h k v -> k (s h) v")), w=['hS'])
                kvt = sb(stSa, "kvt", [64, 128], F32)
                for s_ in range(NS):
                    kb.op('pe', REC.matmul(PB[s_ % 2][0:64, :], lhsT=selS[:, s_, :], rhs=zs[:, OFF['gv']:OFF['gv'] + 512], start=True, stop=True),
                          r=['cS', 'zs'], w=['pb%d' % (s_ % 2)])
                    for h in range(4):
                        kb.op('dve', REC.tensor_scalar(out=kvt[:], in0=PB[s_ % 2][0:64, 128 * h:128 * h + 128], scalar1=gT[:, 1, h, s_:s_ + 1], scalar2=None, op0=ALU.mult),
                              r=['pb%d' % (s_ % 2), 'gT'], w=['kvt'])
                        kb.op('dve', REC.scalar_tensor_tensor(out=hS[:, s_ * 4 + h, :], in0=hS[:, s_ * 4 + h, :], scalar=gT[:, 0, h, s_:s_ + 1], in1=kvt[:],
                                                              op0=ALU.mult, op1=ALU.add), r=['hS', 'gT', 'kvt'], w=['hS'])
                kb.dma('sp', REC.dma_start(out=sgl_o.rearrange("s h k v -> k (s h) v"), in_=hS[:]), r=['hS'], w=['sgl_o'])
                QM = sb(stSa, "QM", [64, 4, NS, 16], F32)
                kb.op('dve', REC.tensor_tensor(out=QM[:], in0=gT[:, 2, :, :].unsqueeze(3).to_broadcast([64, 4, NS, 16]),
                                               in1=eye16.unsqueeze(1).to_broadcast([64, 4, NS, 16]), op=ALU.mult), r=['gT', 'cR'], w=['QM'])
                for h in range(4):
                    for s_ in range(NS):
                        kb.op('pe', REC.matmul(PB[3][0:NS, 128 * h:128 * h + 128], lhsT=QM[:, h, s_, :], rhs=hS[:, s_ * 4 + h, :], start=(s_ == 0), stop=(s_ == NS - 1)),
                              r=['QM', 'hS'], w=['pb3'])
                og4 = ogs[:].rearrange("p (h v) -> p h v", v=128)
                kb.op('act', REC.copy(out=ogs[:], in_=PB[3][0:NS, :]), r=['pb3'], w=['ogs'])
                sq = tmpS[0:NS, 0:512].rearrange("p (h v) -> p h v", v=128)
                kb.op('dve', REC.tensor_tensor(out=sq, in0=og4, in1=og4, op=ALU.mult), r=['ogs'], w=['tmpS'])
                kb.op('dve', REC.tensor_reduce(out=smS[0:NS, 8:12], in_=sq, axis=AX.X, op=ALU.add), r=['tmpS'], w=['smS8'])
                kb.op('dve', REC.tensor_scalar(out=smS[0:NS, 8:12], in0=smS[0:NS, 8:12], scalar1=1.0 / 128, scalar2=EPS, op0=ALU.mult, op1=ALU.add), r=['smS8'], w=['smS8'])
                kb.op('act', REC.activation(out=smS[0:NS, 8:12], in_=smS[0:NS, 8:12], func=AF.Sqrt), r=['smS8'], w=['smS8'])
                kb.op('dve', REC.reciprocal(out=smS[0:NS, 12:16], in_=smS[0:NS, 8:12]), r=['smS8'], w=['smS12'])
                kb.op('dve', REC.tensor_tensor(out=og4, in0=og4, in1=smS[0:NS, 12:16].unsqueeze(2).to_broadcast([NS, 4, 128]), op=ALU.mult), r=['ogs', 'smS12'], w=['ogs'])
                kb.op('dve', REC.tensor_tensor(out=ogs[:], in0=ogs[:], in1=vrS[0:NS, 1024:1536], op=ALU.mult), r=['ogs', 'vrS'], w=['ogs'])
                kb.op('act', REC.activation(out=tmpS[0:NS, 512:1024], in_=zs[:, OFF['gr']:OFF['gr'] + 512], func=AF.Silu), r=['zs'], w=['tmpS'])
                kb.op('dve', REC.tensor_tensor(out=ogs[:], in0=ogs[:], in1=tmpS[0:NS, 512:1024], op=ALU.mult), r=['ogs', 'tmpS'], w=['ogs'])
            kb.barrier()
            with contextlib.ExitStack() as stSa:
                sct = sb(stSa, "sct", [NS, 2 * 2 * DFF], F32)
                scT = sb(stSa, "scT", [128, 88, NS], F32)
                kb.dma('sp', REC.dma_start(out=sct[:], in_=sconv_i), w=['sct'])
                for c8 in range(0, 88, 32):
                    nn_ = min(32, 88 - c8)
                    for u_ in range(nn_):
                        kb.op('pe', REC.transpose(out=PB[5][:, u_ * NS:(u_ + 1) * NS], in_=sct[:, (c8 + u_) * 128:(c8 + u_ + 1) * 128], identity=identf[0:NS, 0:NS]),
                              r=['sct', 'identf'], w=['pb5'])
                    kb.op('act', REC.copy(out=scT[:, c8:c8 + nn_, :].rearrange("p a b -> p (a b)"), in_=PB[5][:, 0:nn_ * NS]), r=['pb5'], w=['scT'])
                kb.dma('sp', REC.dma_start(out=scv_o[:, 0, :], in_=sct[:, 2 * DFF:4 * DFF]), r=['sct'], w=['scv_o'])
                kb.dma('sp', REC.dma_start(out=sct_d, in_=scT[:].rearrange("p a b -> p (a b)")), r=['scT'], w=['sct_d'])
            kb.barrier()

            qsf = qs[:].rearrange("p h c -> p (h c)")

            def qrep(dst, ri):
                kb.op('pe', REC.matmul(PB[6][:, :], lhsT=RepAll[:, ri, :], rhs=qsf, start=True, stop=True), r=['cS', 'qs'], w=['pb6'])
                kb.op('act', REC.copy(out=dst[:], in_=PB[6][:, :]), r=['pb6'], w=['qr'])

            def newkey(br, kname, vname, acc_ps, accreg):
                kk = zs[:, OFF[kname]:OFF[kname] + 128].rearrange("p (g c) -> p g c", c=64).unsqueeze(2).to_broadcast([NS, 2, 4, 64])
                vv = zs[:, OFF[vname]:OFF[vname] + 128].rearrange("p (g c) -> p g c", c=64).unsqueeze(2).to_broadcast([NS, 2, 4, 64])
                t4 = tmpS[0:NS, 0:512].rearrange("p (g h c) -> p g h c", g=2, h=4)
                kb.op('dve', REC.tensor_tensor(out=t4, in0=qs[:].rearrange("p (g h) c -> p g h c", g=2), in1=kk, op=ALU.mult), r=['qs', 'zs'], w=['tmpS'])
                kb.op('dve', REC.tensor_reduce(out=smS[0:NS, 16:24], in_=tmpS[0:NS, 0:512].rearrange("p (a c) -> p a c", c=64), axis=AX.X, op=ALU.add), r=['tmpS'], w=['smS16'])
                kb.op('act', REC.activation(out=smS[0:NS, 16:24], in_=smS[0:NS, 16:24], func=AF.Exp), r=['smS16'], w=['smS16'])
                kb.op('dve', REC.tensor_tensor(out=smS[0:NS, 24:32], in0=smS[0:NS, 16:24], in1=acc_ps[:, 512:520], op=ALU.add), r=['smS16', accreg], w=['smS24'])
                kb.op('dve', REC.reciprocal(out=smS[0:NS, 24:32], in_=smS[0:NS, 24:32]), r=['smS24'], w=['smS24'])
                kb.op('dve', REC.tensor_tensor(out=t4, in0=vv, in1=smS[0:NS, 16:24].rearrange("p (g h) -> p g h", g=2).unsqueeze(3).to_broadcast([NS, 2, 4, 64]), op=ALU.mult),
                      r=['zs', 'smS16'], w=['tmpS'])
                kb.op('dve', REC.tensor_tensor(out=tmpS[0:NS, 0:512], in0=tmpS[0:NS, 0:512], in1=acc_ps[:, 0:512], op=ALU.add), r=['tmpS', accreg], w=['tmpS'])
                kb.op('dve', REC.tensor_tensor(out=oall[:, br, :, :], in0=tmpS[0:NS, 0:512].rearrange("p (a c) -> p a c", c=64),
                                               in1=smS[0:NS, 24:32].unsqueeze(2).to_broadcast([NS, 8, 64]), op=ALU.mult), r=['tmpS', 'smS24'], w=['oall%d' % br])

            def attend(kbuf, vbuf, qr, nrow, rowstride_g, mask, bsel_list, acc_bank, first, last, regs, ghs=range(8)):
                e = tmpS[:, 0:8 * nrow].rearrange("p (a r) -> p a r", r=nrow)
                pr = tmpS[:, 4096:4096 + nrow * 64]
                for gh in ghs:
                    g_ = gh // 4
                    kb.op('dve', REC.tensor_tensor(out=pr.rearrange("p (r c) -> p r c", c=64), in0=kbuf[:, :, 64 * g_:64 * g_ + 64],
                                                   in1=qr[:, 64 * gh:64 * gh + 64].unsqueeze(1).to_broadcast([128, nrow, 64]), op=ALU.mult), r=regs + ['qr'], w=['tmpS'])
                    kb.op('dve', REC.tensor_reduce(out=e[:, gh, :], in_=pr.rearrange("p (r c) -> p r c", c=64), axis=AX.X, op=ALU.add), r=['tmpS'], w=['tmpS'])
                g0_, g1_ = ghs[0], ghs[-1] + 1
                esl = e[:, g0_:g1_, :]
                if mask is not None:
                    kb.op('dve', REC.tensor_tensor(out=esl, in0=esl, in1=mask.unsqueeze(1).to_broadcast([128, g1_ - g0_, nrow]), op=ALU.add), r=['tmpS', 'cR'], w=['tmpS'])
                kb.op('act', REC.activation(out=esl, in_=esl, func=AF.Exp), r=['tmpS'], w=['tmpS'])
                ov = sb_ov
                for gh in ghs:
                    g_ = gh // 4
                    kb.op('dve', REC.tensor_tensor(out=pr.rearrange("p (r c) -> p r c", c=64), in0=vbuf[:, :, 64 * g_:64 * g_ + 64],
                                                   in1=e[:, gh, :].unsqueeze(2).to_broadcast([128, nrow, 64]), op=ALU.mult), r=regs + ['tmpS'], w=['tmpS'])
                    kb.op('dve', REC.tensor_reduce(out=ov[:, 64 * gh:64 * gh + 64], in_=pr.rearrange("p (r c) -> p c r", c=64), axis=AX.X, op=ALU.add), r=['tmpS'], w=['ovS'])
                kb.op('dve', REC.tensor_reduce(out=ov[:, 512 + g0_:512 + g1_], in_=esl, axis=AX.X, op=ALU.add), r=['tmpS'], w=['ovS'])
                return ov

            sb_ov = sb(stS, "sb_ov", [128, 520], F32)
            kb.op('pool', REC.memset(sb_ov[:], 0.0), w=['ovS'])
            qr = sb(stS, "qr", [128, 512], F32)
            with contextlib.ExitStack() as stSb:
                wk = sb(stSb, "wk", [128, 64, 128], F32)
                wv = sb(stSb, "wv", [128, 64, 128], F32)
                kb.dma('sp', REC.dma_start(out=wk[:], in_=wink_i.rearrange("s (c r) f -> (s c) r f", c=8)), w=['wk'])
                kb.dma('sp', REC.dma_start(out=wv[:], in_=winv_i.rearrange("s (c r) f -> (s c) r f", c=8)), w=['wv'])
                qrep(qr, 0)
                ov = attend(wk, wv, qr, 64, None, maskW, None, 3, True, True, ['wk', 'wv'])
                kb.op('pe', REC.matmul(PB[3][0:NS, 0:512], lhsT=BAll[:, 0, :], rhs=ov[:, 0:512], start=True, stop=True), r=['ovS', 'cR'], w=['pb3'])
                kb.op('pe', REC.matmul(PB[2][0:NS, 0:8], lhsT=BAll[:, 0, :], rhs=ov[:, 512:520], start=True, stop=True), r=['ovS', 'cR'], w=['pb2'])
                accw = sb(stSb, "accw", [NS, 520], F32)
                kb.op('act', REC.copy(out=accw[:, 0:512], in_=PB[3][0:NS, 0:512]), r=['pb3'], w=['accw'])
                kb.op('act', REC.copy(out=accw[:, 512:520], in_=PB[2][0:NS, 0:8]), r=['pb2'], w=['accw'])
                newkey(2, 'kw', 'vw', accw, 'accw')
            kb.barrier()
            with contextlib.ExitStack() as stSc:
                ptab_sb = sb(stSc, "ptab_sb", [128, NGR], I32)
                idx4 = sb(stSc, "idx4", [128, NGR], I32)
                kb.dma('sp', REC.dma_start(out=ptab_sb[:], in_=ptab_i), w=['ptab_sb'])
                kb.op('dve', REC.tensor_scalar(out=idx4[:], in0=ptab_sb[:], scalar1=4.0, scalar2=None, op0=ALU.mult), r=['ptab_sb'], w=['idx4'])
                pg = [sb(stSc, "pg%d" % i, [128, 32, 2, 64], F32) for i in range(2)]
                kcP = sb(stSc, "kcP", [128, 2, 2, 128], F32)
                scP = sb(stSc, "scP", [128, 2, 8], F32)
                part = sb(stSc, "part", [128, 128], F32)
                it = 0
                for r_ in range(NGR):
                    qrep(qr, 1 + r_)
                    for ti in range(2):
                        pool_ap = pools[ti].ap().rearrange("n (c f) -> (n c) f", c=4)
                        for c in range(4):
                            b_ = pg[it % 2]
                            breg = 'pg%d' % (it % 2)
                            it += 1
                            kb.dma('pool', REC.indirect_dma_start(out=b_[:].rearrange("p a b c -> p (a b c)"), out_offset=None, in_=pool_ap, element_offset=c * 4096,
                                                                  in_offset=bass.IndirectOffsetOnAxis(ap=idx4[:, r_:r_ + 1], axis=0)), r=['idx4'], w=[breg])
                            m_ = c // 2
                            wsl = wrow[:, ti, (c % 2) * 32:(c % 2) * 32 + 32, :].unsqueeze(3).to_broadcast([128, 32, 2, 64])
                            pr4 = tmpS[:, 0:4096].rearrange("p (a b c) -> p a b c", b=2, c=64)
                            kb.op('dve', REC.tensor_tensor(out=pr4, in0=b_[:], in1=wsl, op=ALU.mult), r=[breg, 'cR'], w=['tmpS'])
                            dst = kcP[:, ti, m_, :] if c % 2 == 0 else part[:]
                            kb.op('dve', REC.tensor_reduce(out=dst, in_=tmpS[:, 0:4096].rearrange("p (a f) -> p f a", f=128), axis=AX.X, op=ALU.add), r=['tmpS'], w=['kcP' if c % 2 == 0 else 'part'])
                            if c % 2 == 1:
                                kb.op('dve', REC.tensor_tensor(out=kcP[:, ti, m_, :], in0=kcP[:, ti, m_, :], in1=part[:], op=ALU.add), r=['kcP', 'part'], w=['kcP'])
                    for gh in range(8):
                        g_ = gh // 4
                        pr3 = tmpS[:, 0:128].rearrange("p (m c) -> p m c", c=64)
                        kb.op('dve', REC.tensor_tensor(out=pr3, in0=kcP[:, 0, :, 64 * g_:64 * g_ + 64], in1=qr[:, 64 * gh:64 * gh + 64].unsqueeze(1).to_broadcast([128, 2, 64]), op=ALU.mult),
                              r=['kcP', 'qr'], w=['tmpS'])
                        kb.op('dve', REC.tensor_reduce(out=scP[:, :, gh], in_=pr3, axis=AX.X, op=ALU.add), r=['tmpS'], w=['scP'])
                    kb.dma('sp', REC.dma_start(out=scmp_d[r_ * 128:(r_ + 1) * 128, :], in_=scP[:].rearrange("p a b -> p (a b)")), r=['scP'], w=['scmp_d'])
                    kb.dma('sp', REC.dma_start(out=vc_d[r_ * 128:(r_ + 1) * 128, :], in_=kcP[:, 1, :, :].rearrange("p a b -> p (a b)")), r=['kcP'], w=['vc_d'])
                scm = sb(stSc, "scm", [NS, NPG * 16], F32)
                vcs_ = sb(stSc, "vcs_", [NS, NPG * 256], F32)
                kb.dma('sp', REC.dma_start(out=scm[:], in_=scmp_d.rearrange("(s p) f -> s (p f)", p=NPG)), r=['scmp_d'], w=['scm'])
                kb.dma('sp', REC.dma_start(out=vcs_[:], in_=vc_d.rearrange("(s p) f -> s (p f)", p=NPG)), r=['vc_d'], w=['vcs_'])
                sview = scm[:].rearrange("p (b a) -> p a b", a=8)
                pS = sb(stSc, "pS", [NS, 8, NBP], F32)
                kb.op('dve', REC.tensor_reduce(out=smS[0:NS, 32:40], in_=sview, axis=AX.X, op=ALU.max), r=['scm'], w=['smS32'])
                kb.op('dve', REC.tensor_scalar(out=smS[0:NS, 32:40], in0=smS[0:NS, 32:40], scalar1=-1.0, scalar2=None, op0=ALU.mult), r=['smS32'], w=['smS32'])
                for gh in range(8):
                    kb.op('act', REC.activation(out=pS[:, gh, :], in_=sview[:, gh, :], func=AF.Exp, bias=smS[0:NS, 32 + gh:33 + gh], accum_out=smS[0:NS, 40 + gh:41 + gh]),
                          r=['scm', 'smS32'], w=['pS', 'smS40'])
                kb.op('dve', REC.reciprocal(out=smS[0:NS, 48:56], in_=smS[0:NS, 40:48]), r=['smS40'], w=['smS48'])
                kb.op('dve', REC.tensor_tensor(out=pS[:], in0=pS[:], in1=smS[0:NS, 48:56].unsqueeze(2).to_broadcast([NS, 8, NBP]), op=ALU.mult), r=['pS', 'smS48'], w=['pS'])
                kb.op('dve', REC.tensor_reduce(out=scS[:], in_=pS[:].rearrange("p (g h) b -> p g b h", g=2), axis=AX.X, op=ALU.add), r=['pS'], w=['scS'])
                vview = vcs_[:].rearrange("p (b g c) -> p b g c", g=2, c=64)
                for gh in range(8):
                    g_ = gh // 4
                    pr3 = tmpS[0:NS, 0:NBP * 64].rearrange("p (b c) -> p b c", c=64)
                    kb.op('dve', REC.tensor_tensor(out=pr3, in0=vview[:, :, g_, :], in1=pS[:, gh, :].unsqueeze(2).to_broadcast([NS, NBP, 64]), op=ALU.mult), r=['vcs_', 'pS'], w=['tmpS'])
                    kb.op('dve', REC.tensor_reduce(out=oall[:, 0, gh, :], in_=tmpS[0:NS, 0:NBP * 64].rearrange("p (b c) -> p c b", c=64), axis=AX.X, op=ALU.add), r=['tmpS'], w=['oall0'])
            kb.barrier()
            with contextlib.ExitStack() as stSd:
                ksg = sb(stSd, "ksg", [128, 64, 128], F32)
                vsg = sb(stSd, "vsg", [128, 64, 128], F32)
                accs = sb(stSd, "accs", [NS, 520], F32)
                m8s = sb(stSd, "m8s", [NS, 16], F32)
                i8s = sb(stSd, "i8s", [NS, 16], mybir.dt.uint32)
                idf = sb(stSd, "idf", [NS, 16], F32)
                mm_ = sb(stSd, "mm_", [NS, 16], F32)
                ohs = sb(stSd, "ohs", [NS, 16, NBP], F32)
                ptb = sb(stSd, "ptb", [NS, NPG, 2], F32)
                idi = sb(stSd, "idi", [NS, 16], I32)
                idp = sb(stSd, "idp", [128, 2], I32)
                sc2s = sb(stSd, "sc2s", [NS, NBP], F32)
                pti = sb(stSd, "pti", [NS, NPG], I32)
                ptf = sb(stSd, "ptf", [NS, NPG], F32)
                kb.dma('sp', REC.dma_start(out=pti[:], in_=ptabs_i), w=['pti'])
                kb.op('dve', REC.tensor_copy(out=ptf[:], in_=pti[:]), r=['pti'], w=['ptf'])
                kb.op('dve', REC.tensor_scalar(out=ptb[:, :, 0], in0=ptf[:], scalar1=2.0, scalar2=None, op0=ALU.mult), r=['ptf'], w=['ptb'])
                kb.op('dve', REC.tensor_scalar(out=ptb[:, :, 1], in0=ptf[:], scalar1=2.0, scalar2=1.0, op0=ALU.mult, op1=ALU.add), r=['ptf'], w=['ptb'])
                iotp = cS[:, 16 + 1024 + 128 * NREP + 16:16 + 1024 + 128 * NREP + 16 + 128]
                for g_ in range(2):
                    kb.op('pool', REC.memset(scS[:, g_, 0:1], 5.0), r=['scS'], w=['scS'])
                    kb.op('pool', REC.memset(scS[:, g_, NBP - 1:NBP], 5.0), r=['scS'], w=['scS'])
                    kb.op('dve', REC.max(out=m8s[:, 0:8], in_=scS[:, g_, :]), r=['scS'], w=['m8s'])
                    kb.op('dve', REC.max_index(out=i8s[:, 0:8], in_max=m8s[:, 0:8], in_values=scS[:, g_, :]), r=['scS', 'm8s'], w=['i8s'])
                    kb.op('dve', REC.match_replace(out=sc2s[:], in_to_replace=m8s[:, 0:8], in_values=scS[:, g_, :], imm_value=-2.0), r=['scS', 'm8s'], w=['sc2s'])
                    kb.op('dve', REC.max(out=m8s[:, 8:16], in_=sc2s[:]), r=['sc2s'], w=['m8s'])
                    kb.op('dve', REC.max_index(out=i8s[:, 8:16], in_max=m8s[:, 8:16], in_values=sc2s[:]), r=['sc2s', 'm8s'], w=['i8s'])
                    kb.op('dve', REC.tensor_copy(out=idf[:], in_=i8s[:]), r=['i8s'], w=['idf'])
                    kb.op('dve', REC.tensor_tensor(out=ohs[:], in0=idf[:].unsqueeze(2).to_broadcast([NS, 16, NBP]), in1=iotp[:, 0:NBP].unsqueeze(1).to_broadcast([NS, 16, NBP]), op=ALU.is_equal),
                          r=['idf', 'cS'], w=['ohs'])
                    kb.op('dve', REC.tensor_tensor(out=ohs[:], in0=ohs[:], in1=ptb[:].rearrange("p a b -> p (a b)").unsqueeze(1).to_broadcast([NS, 16, NBP]), op=ALU.mult), r=['ohs', 'ptb'], w=['ohs'])
                    kb.op('dve', REC.tensor_reduce(out=idf[:], in_=ohs[:], axis=AX.X, op=ALU.add), r=['ohs'], w=['idf'])
                    kb.op('dve', REC.tensor_copy(out=idi[:], in_=idf[:]), r=['idf'], w=['idi'])
                    kb.dma('sp', REC.dma_start(out=idx_d[g_:g_ + 1, :].rearrange("o (s k) -> (o s) k", k=16), in_=idi[:]), r=['idi'], w=['idx_d'])
                    kb.dma('sp', REC.dma_start(out=idp[:], in_=idx_d[g_:g_ + 1, :].rearrange("o (h s k) -> (o s k) h", h=2, k=16), allow_slow_non_contiguous=True), r=['idx_d'], w=['idp'])
                    for half in range(2):
                        for buf, pl_, breg in ((ksg, pools[2], 'ksg'), (vsg, pools[3], 'vsg')):
                            kb.dma('pool', REC.indirect_dma_start(out=buf[:].rearrange("p a b -> p (a b)"), out_offset=None, in_=pl_.ap().rearrange("n (c f) -> (n c) f", c=2),
                                                                  in_offset=bass.IndirectOffsetOnAxis(ap=idp[:, half:half + 1], axis=0)), r=['idp'], w=[breg])
                        qrep(qr, 1 + NGR + half)
                        ov = attend(ksg, vsg, qr, 64, None, None, None, 3, True, True, ['ksg', 'vsg'], ghs=range(4 * g_, 4 * g_ + 4))
                        kb.op('dve', REC.tensor_scalar(out=ov[:], in0=ov[:], scalar1=slotm, scalar2=None, op0=ALU.mult), r=['ovS', 'cR'], w=['ovS'])
                        kb.op('pe', REC.matmul(PB[3][0:NS, 0:512], lhsT=BAll[:, 1 + NGR + half, :], rhs=ov[:, 0:512], start=(half == 0), stop=(half == 1)), r=['ovS', 'cR'], w=['pb3'])
                        kb.op('pe', REC.matmul(PB[2][0:NS, 0:8], lhsT=BAll[:, 1 + NGR + half, :], rhs=ov[:, 512:520], start=(half == 0), stop=(half == 1)), r=['ovS', 'cR'], w=['pb2'])
                    kb.op('act', REC.copy(out=accs[:, 256 * g_:256 * g_ + 256], in_=PB[3][0:NS, 256 * g_:256 * g_ + 256]), r=['pb3'], w=['accs'])
                    kb.op('act', REC.copy(out=accs[:, 512 + 4 * g_:516 + 4 * g_], in_=PB[2][0:NS, 4 * g_:4 * g_ + 4]), r=['pb2'], w=['accs'])
                newkey(1, 'ks', 'vs', accs, 'accs')
            kb.barrier()
            gv3 = gS[:].rearrange("p (h c) -> p h c", c=3)
            for br in range(3):
                kb.op('dve', REC.tensor_tensor(out=oall[:, br, :, :], in0=oall[:, br, :, :], in1=gv3[:, :, br:br + 1].to_broadcast([NS, 8, 64]), op=ALU.mult),
                      r=['oall%d' % br, 'gS'], w=['oall%d' % br])
            kb.op('dve', REC.tensor_tensor(out=oall[:, 0, :, :], in0=oall[:, 0, :, :], in1=oall[:, 1, :, :], op=ALU.add), r=['oall0', 'oall1'], w=['oall0'])
            kb.op('dve', REC.tensor_tensor(out=oall[:, 0, :, :], in0=oall[:, 0, :, :], in1=oall[:, 2, :, :], op=ALU.add), r=['oall0', 'oall2'], w=['oall0'])
            ocs = sb(stS, "ocs", [NS, D], BF16)
            kb.op('act', REC.copy(out=ocs[:, 0:512], in_=ogs[:]), r=['ogs'], w=['ocs'])
            kb.op('act', REC.copy(out=ocs[:, 512:1024], in_=oall[:, 0, :, :].rearrange("p a b -> p (a b)")), r=['oall0'], w=['ocs'])
            kb.dma('sp', REC.dma_start(out=oscr_s, in_=ocs[:]), r=['ocs'], w=['oscr_s'])
          kb.barrier()
        NTOK = NQ * 128 + (NS if do_sample else 0)
        h1_d = nc.dram_tensor("h1_d", [NTOK, D], F32, kind="Internal").ap()
        hnT_d = nc.dram_tensor("hnT_d", [128, 8, NTOK], BF16, kind="Internal").ap()
        tiles = [(tq * 128, 128, 'halo' if tq == 0 else 'own', tq) for tq in range(NQ)]
        if do_sample:
            tiles.append((NQ * 128, NS, 'sample', None))

        def castload(st_, name, src_ap, rows_k, ncols):
            dst = sb(st_, name, [128, rows_k, ncols], BF16)
            for k in range(rows_k):
                for c0 in range(0, ncols, 2048):
                    c1 = min(ncols, c0 + 2048)
                    kb.dma('pool', REC.dma_start(out=dst[:, k, c0:c1], in_=src_ap[k * 128:(k + 1) * 128, c0:c1]), w=[name])
            return dst

        def rmsn(ss, junk, src, gvec, dst, srcreg, dstreg, P):
            kb.op('act', REC.activation(out=junk[0:P, :], in_=src, func=AF.Square, accum_out=ss[0:P, 0:1]), r=[srcreg], w=['junkB', 'ssB'])
            kb.op('dve', REC.tensor_scalar(out=ss[0:P, 1:2], in0=ss[0:P, 0:1], scalar1=1.0 / D, scalar2=EPS, op0=ALU.mult, op1=ALU.add), r=['ssB'], w=['ssB1'])
            kb.op('act', REC.activation(out=ss[0:P, 3:4], in_=ss[0:P, 1:2], func=AF.Sqrt), r=['ssB1'], w=['ssB3'])
            kb.op('dve', REC.reciprocal(out=ss[0:P, 2:3], in_=ss[0:P, 3:4]), r=['ssB3'], w=['ssB2'])
            kb.op('dve', REC.scalar_tensor_tensor(out=dst, in0=src, scalar=ss[0:P, 2:3], in1=gvec[0:P, :], op0=ALU.mult, op1=ALU.mult),
                  r=[srcreg, 'ssB2', 'gv'], w=[dstreg])

        def tr8q(src_bf, dstT, srcreg, dstreg, P, nk=8, par=0):
            for k in range(nk):
                kb.op('pe', REC.transpose(out=pbf(4)[:, k * P:(k + 1) * P], in_=src_bf[0:P, k * 128:(k + 1) * 128], identity=identb[0:P, 0:P]),
                      r=[srcreg, 'identb'], w=['pb4'])
            kb.op('act', REC.copy(out=dstT[:, 0:nk, 0:P], in_=pbf(4)[:, 0:nk * P].rearrange("p (a b) -> p a b", b=P)), r=['pb4'], w=[dstreg])

        with contextlib.ExitStack() as stB:
            w_o_sb = castload(stB, 'w_o_sb', w_o, 8, D)
            gv1 = sb(stB, "gv1", [128, D], F32)
            kb.dma('sp', REC.dma_start(out=gv1[:], in_=vecsB[:, 0:1024].partition_broadcast(128)), w=['gv'])
            ss = sb(stB, "ssB", [128, 8], F32)
            junk = sb(stB, "junkB", [128, D], BF16)
            xb2 = [sb(stB, "xb2_%d" % i, [128, D], F32) for i in range(2)]
            ocat = [sb(stB, "ocat%d" % i, [128, D], BF16) for i in range(2)]
            ocT = [sb(stB, "ocT%d" % i, [128, 8, 128], BF16) for i in range(2)]
            h1 = [sb(stB, "h1_%d" % i, [128, D], F32) for i in range(2)]
            hnb = [sb(stB, "hnb%d" % i, [128, D], BF16) for i in range(2)]
            hnT = [sb(stB, "hnT%d" % i, [128, 8, 128], BF16) for i in range(2)]
            for ti, (t0, P, mode, tq) in enumerate(tiles):
                pr_ = ti % 2
                kb.alias = {r_: r_ + '_%d' % pr_ for r_ in ('xb2', 'ocat', 'ocT', 'h1', 'hnb', 'hnT')}
                x_ap = xs_i if mode == 'sample' else xl[(HALO + tq) * 128:(HALO + tq + 1) * 128, :]
                oc_ap = oscr_s if mode == 'sample' else oscr[tq * 128:(tq + 1) * 128, :]
                kb.dma('sp', REC.dma_start(out=xb2[pr_][0:P, :], in_=x_ap), w=['xb2'])
                kb.dma('sp', REC.dma_start(out=ocat[pr_][0:P, :], in_=oc_ap), r=['oscr', 'oscr_s'], w=['ocat'])
                tr8q(ocat[pr_], ocT[pr_], 'ocat', 'ocT', P)
                for half in range(2):
                    for k in range(8):
                        kb.op('pe', REC.matmul(PB[half][0:P, :], lhsT=ocT[pr_][:, k, 0:P], rhs=w_o_sb[:, k, half * 512:(half + 1) * 512],
                                               start=(k == 0), stop=(k == 7)), r=['ocT', 'w_o_sb'], w=['pb%d' % half])
                    kb.op('dve', REC.tensor_tensor(out=h1[pr_][0:P, half * 512:(half + 1) * 512], in0=PB[half][0:P, :], in1=xb2[pr_][0:P, half * 512:(half + 1) * 512], op=ALU.add),
                          r=['pb%d' % half, 'xb2'], w=['h1'])
                kb.dma('sp', REC.dma_start(out=h1_d[t0:t0 + P, :], in_=h1[pr_][0:P, :]), r=['h1'], w=['h1_d'])
                rmsn(ss, junk, h1[pr_][0:P, :], gv1, hnb[pr_][0:P, :], 'h1', 'hnb', P)
                tr8q(hnb[pr_], hnT[pr_], 'hnb', 'hnT', P)
                kb.dma('sp', REC.dma_start(out=hnT_d[:, :, t0:t0 + P], in_=hnT[pr_][:, :, 0:P]), r=['hnT'], w=['hnT_d'])
            kb.alias = {}
        kb.barrier()
        with contextlib.ExitStack() as stB:
            w_up_sb = castload(stB, 'w_up_sb', w_up, 8, 2 * DFF)
            w_dn_sb = castload(stB, 'w_dn_sb', w_down, 22, D)
            cw = sb(stB, "cw", [128, 44, 4], F32)
            kb.dma('sp', REC.dma_start(out=cw[:].rearrange("p a b -> p (a b)"), in_=convw), w=['cw'])
            carry = sb(stB, "carry", [128, 44, 2], F32)
            kb.op('pool', REC.memset(carry[:], 0.0), w=['carry'])
            GT = 512
            hg = sb(stB, "hg", [128, 8, GT], BF16)
            uext2 = [sb(stB, "uext%d" % i, [128, 2, GT + 2], F32) for i in range(2)]
            ca2 = [sb(stB, "ca%d" % i, [128, 2, GT], F32) for i in range(2)]
            actT = sb(stB, "actT", [128, 22, GT], BF16)
            hio = [sb(stB, "hio%d" % i, [128, D], F32) for i in range(2)]
            sc0T = sb(stB, "sc0T", [128, 44, NS], F32) if do_sample else None
            sc1T = sb(stB, "sc1T", [128, 44, NS], F32) if do_sample else None
            if do_sample:
                kb.dma('sp', REC.dma_start(out=sc0T[:].rearrange("p a b -> p (a b)"), in_=sct_d[:, 0:44 * NS]), r=['sct_d'], w=['sc0T'])
                kb.dma('sp', REC.dma_start(out=sc1T[:].rearrange("p a b -> p (a b)"), in_=sct_d[:, 44 * NS:88 * NS]), r=['sct_d'], w=['sc1T'])
            groups = [(0, 128, 'halo')] + [(128 + gi * GT, min(GT, NQ * 128 - 128 - gi * GT), 'own') for gi in range((NQ * 128 - 128 + GT - 1) // GT)]
            if do_sample:
                groups.append((NQ * 128, NS, 'sample'))
            for (t0, W, mode) in groups:
                is_halo = (mode == 'halo')
                c0 = W - 2 if is_halo else 0
                ncol = W - c0
                kb.dma('sp', REC.dma_start(out=hg[:, :, 0:W], in_=hnT_d[:, :, t0:t0 + W]), r=['hnT_d'], w=['hg'])
                for i in range(22):
                    bk = (2, 3) if i % 2 == 0 else (5, 6)
                    uext = uext2[i % 2]
                    ca = ca2[i % 2]
                    UR = 'uext%d' % (i % 2)
                    CR = 'ca%d' % (i % 2)
                    for ab in range(2):
                        col = ab * DFF + i * 128
                        for k in range(8):
                            kb.op('pe', REC.matmul(PB[bk[ab]][:, 0:ncol], lhsT=w_up_sb[:, k, col:col + 128], rhs=hg[:, k, c0:W],
                                                   start=(k == 0), stop=(k == 7)), r=['hg', 'w_up_sb'], w=['pb%d' % bk[ab]])
                    if is_halo:
                        for ab in range(2):
                            kb.op('act', REC.copy(out=carry[:, ab * 22 + i, :], in_=PB[bk[ab]][:, 0:2]), r=['pb%d' % bk[ab]], w=['carry'])
                        continue
                    for ab in range(2):
                        ci = ab * 22 + i
                        if mode == 'sample':
                            kb.op('act', REC.activation(out=ca[:, ab, 0:W], in_=sc0T[:, ci, :], func=AF.Identity, scale=cw[:, ci, 0:1], bias=cw[:, ci, 3:4]),
                                  r=['sc0T', 'cw', 'actT'], w=[CR])
                            kb.op('dve', REC.scalar_tensor_tensor(out=ca[:, ab, 0:W], in0=sc1T[:, ci, :], scalar=cw[:, ci, 1:2], in1=ca[:, ab, 0:W],
                                                                  op0=ALU.mult, op1=ALU.add), r=['sc1T', 'cw', CR], w=[CR])
                            kb.op('dve', REC.scalar_tensor_tensor(out=ca[:, ab, 0:W], in0=PB[bk[ab]][:, 0:W], scalar=cw[:, ci, 2:3], in1=ca[:, ab, 0:W],
                                                                  op0=ALU.mult, op1=ALU.add), r=['pb%d' % bk[ab], 'cw', CR], w=[CR])
                            continue
                        kb.op('act', REC.copy(out=uext[:, ab, 0:2], in_=carry[:, ci, :]), r=['carry', CR], w=[UR])
                        kb.op('act', REC.copy(out=uext[:, ab, 2:W + 2], in_=PB[bk[ab]][:, 0:W]), r=['pb%d' % bk[ab], CR], w=[UR])
                        kb.op('pool', REC.tensor_copy(out=carry[:, ci, :], in_=uext[:, ab, W:W + 2]), r=[UR], w=['carry'])
                        kb.op('act', REC.activation(out=ca[:, ab, 0:W], in_=uext[:, ab, 0:W], func=AF.Identity, scale=cw[:, ci, 0:1], bias=cw[:, ci, 3:4]),
                              r=[UR, 'cw', 'actT'], w=[CR])
                        kb.op('dve', REC.scalar_tensor_tensor(out=ca[:, ab, 0:W], in0=uext[:, ab, 1:W + 1], scalar=cw[:, ci, 1:2], in1=ca[:, ab, 0:W],
                                                              op0=ALU.mult, op1=ALU.add), r=[UR, 'cw', CR], w=[CR])
                        kb.op('dve', REC.scalar_tensor_tensor(out=ca[:, ab, 0:W], in0=uext[:, ab, 2:W + 2], scalar=cw[:, ci, 2:3], in1=ca[:, ab, 0:W],
                                                              op0=ALU.mult, op1=ALU.add), r=[UR, 'cw', CR], w=[CR])
                    kb.op('act', REC.activation(out=ca[:, 0, 0:W], in_=ca[:, 0, 0:W], func=AF.Silu), r=[CR], w=[CR])
                    kb.op('dve', REC.tensor_tensor(out=actT[:, i, 0:W], in0=ca[:, 0, 0:W], in1=ca[:, 1, 0:W], op=ALU.mult), r=[CR], w=['actT'])
                if is_halo:
                    continue
                if mode == 'sample':
                    for ci_, c0_ in enumerate(range(0, 2 * DFF, 512)):
                        for k in range(8):
                            kb.op('pe', REC.matmul(PB[2][0:W, :], lhsT=hg[:, k, 0:W], rhs=w_up_sb[:, k, c0_:c0_ + 512], start=(k == 0), stop=(k == 7)),
                                  r=['hg', 'w_up_sb'], w=['pb2'])
                        kb.op('act', REC.copy(out=hio[ci_ % 2][0:W, 0:512], in_=PB[2][0:W, :]), r=['pb2'], w=['hio%d' % (ci_ % 2)])
                        kb.dma('sp', REC.dma_start(out=scv_o[:, 1, c0_:c0_ + 512], in_=hio[ci_ % 2][0:W, 0:512]), r=['hio%d' % (ci_ % 2)], w=['scv_o'])
                for st_ in range(0, W, 128):
                    P = min(128, W - st_)
                    hb = hio[(st_ // 128) % 2]
                    hr = 'hio%d' % ((st_ // 128) % 2)
                    kb.dma('sp', REC.dma_start(out=hb[0:P, :], in_=h1_d[t0 + st_:t0 + st_ + P, :]), r=['h1_d'], w=[hr])
                    for half in range(2):
                        for i in range(22):
                            kb.op('pe', REC.matmul(PB[half][0:P, :], lhsT=actT[:, i, st_:st_ + P], rhs=w_dn_sb[:, i, half * 512:(half + 1) * 512],
                                                   start=(i == 0), stop=(i == 21)), r=['actT', 'w_dn_sb'], w=['pb%d' % half])
                        kb.op('dve', REC.tensor_tensor(out=hb[0:P, half * 512:(half + 1) * 512], in0=PB[half][0:P, :], in1=hb[0:P, half * 512:(half + 1) * 512], op=ALU.add),
                              r=['pb%d' % half, hr], w=[hr])
                    kb.dma('sp', REC.dma_start(out=h1_d[t0 + st_:t0 + st_ + P, :], in_=hb[0:P, :]), r=[hr], w=['h1_d'])
            for t_ in range(2):
                kb.dma('sp', REC.dma_start(out=cv_o[t_:t_ + 1, :].rearrange("o (c p) -> p (o c)", p=128), in_=carry[:, :, t_],
                                           allow_slow_non_contiguous=True), r=['carry'], w=['cv_o'])
        kb.barrier()
        with contextlib.ExitStack() as stB:
            w_pg_sb = castload(stB, 'w_pg_sb', w_pg, 8, D)
            w_ple_sb = castload(stB, 'w_ple_sb', w_ple, 2, D)
            gv2 = sb(stB, "gv2", [128, 2048], F32)
            kb.dma('sp', REC.dma_start(out=gv2[:], in_=vecsB[:, 1024:3072].partition_broadcast(128)), w=['gv'])
            ss = sb(stB, "ssB3", [128, 8], F32)
            junk = sb(stB, "junkB3", [128, D], BF16)
            h2 = [sb(stB, "h2_%d" % i, [128, D], F32) for i in range(2)]
            hnb = [sb(stB, "hnc%d" % i, [128, D], BF16) for i in range(2)]
            hnT = [sb(stB, "hnU%d" % i, [128, 8, 128], BF16) for i in range(2)]
            pbb = [sb(stB, "pbb%d" % i, [128, 256], BF16) for i in range(2)]
            peT = [sb(stB, "peT%d" % i, [128, 2, 128], BF16) for i in range(2)]
            gsg = [sb(stB, "gsg%d" % i, [128, D], F32) for i in range(2)]
            yb = [sb(stB, "yb%d" % i, [128, D], F32) for i in range(2)]
            for ti, (t0, P, mode, tq) in enumerate(tiles):
                if mode == 'halo':
                    continue
                pr_ = ti % 2
                kb.alias = {r_: r_ + '_%d' % pr_ for r_ in ('h2', 'hnb', 'hnT', 'pbb', 'peT', 'gsg', 'yb')}
                pe_ap = ps_i if mode == 'sample' else pl[(tq - 1) * 128:tq * 128, :]
                y_ap = ys_o if mode == 'sample' else y_o[(tq - 1) * 128:tq * 128, :]
                kb.dma('sp', REC.dma_start(out=h2[pr_][0:P, :], in_=h1_d[t0:t0 + P, :]), r=['h1_d'], w=['h2'])
                kb.dma('pool', REC.dma_start(out=pbb[pr_][0:P, :], in_=pe_ap), w=['pbb'])
                rmsn(ss, junk, h2[pr_][0:P, :], gv2[:, 0:1024], hnb[pr_][0:P, :], 'h2', 'hnb', P)
                tr8q(hnb[pr_], hnT[pr_], 'hnb', 'hnT', P)
                tr8q(pbb[pr_], peT[pr_], 'pbb', 'peT', P, nk=2)
                for half in range(2):
                    for k in range(8):
                        kb.op('pe', REC.matmul(PB[half][0:P, :], lhsT=hnT[pr_][:, k, 0:P], rhs=w_pg_sb[:, k, half * 512:(half + 1) * 512],
                                               start=(k == 0), stop=(k == 7)), r=['hnT', 'w_pg_sb'], w=['pb%d' % half])
                    kb.op('act', REC.activation(out=gsg[pr_][0:P, half * 512:(half + 1) * 512], in_=PB[half][0:P, :], func=AF.Sigmoid), r=['pb%d' % half], w=['gsg'])
                    for k in range(2):
                        kb.op('pe', REC.matmul(PB[2 + half][0:P, :], lhsT=peT[pr_][:, k, 0:P], rhs=w_ple_sb[:, k, half * 512:(half + 1) * 512],
                                               start=(k == 0), stop=(k == 1)), r=['peT', 'w_ple_sb'], w=['pb%d' % (2 + half)])
                    kb.op('dve', REC.tensor_tensor(out=gsg[pr_][0:P, half * 512:(half + 1) * 512], in0=PB[2 + half][0:P, :], in1=gsg[pr_][0:P, half * 512:(half + 1) * 512], op=ALU.mult),
                          r=['pb%d' % (2 + half), 'gsg'], w=['gsg'])
                kb.op('pool', REC.tensor_tensor(out=h2[pr_][0:P, :], in0=h2[pr_][0:P, :], in1=gsg[pr_][0:P, :], op=ALU.add), r=['h2', 'gsg'], w=['h2'])
                rmsn(ss, junk, h2[pr_][0:P, :], gv2[:, 1024:2048], yb[pr_][0:P, :], 'h2', 'yb', P)
                kb.dma('sp', REC.dma_start(out=y_ap, in_=yb[pr_][0:P, :]), r=['yb'], w=['y_o'])
            kb.alias = {}
        kb.emit()
    return nc


def _consts(S, pad):
    NB = S // 64
    p = np.arange(128, dtype=np.float32)[:, None]
    f = np.arange(128, dtype=np.float32)[None, :]
    blk = np.arange(NB, dtype=np.float32)[None, :]
    padblk = pad // 64
    valid = np.broadcast_to((blk >= padblk).astype(np.float32), (128, NB))
    first = np.broadcast_to((blk == padblk).astype(np.float32), (128, NB))
    D0 = p - 64.0 * blk
    ident = np.eye(128, dtype=np.float32)
    tri = np.where(p <= f, 0.0, NEG).astype(np.float32)
    wbs = []
    for t in range(5):
        dlt = (t - 4) * 128 + p - f
        wbs.append(np.where((dlt <= 0) & (dlt > -512), 0.0, NEG).astype(np.float32))
    tribd = ((p <= f) & ((p // 64) == (f // 64))).astype(np.float32)
    rmask = np.broadcast_to(((np.arange(128) % 64) != 0).astype(np.float32)[None, :], (128, 128))
    return np.ascontiguousarray(np.concatenate([valid, first, D0, ident, tri] + wbs + [tribd, rmask], axis=1).astype(np.float32))


def _rope_table(S, pad):
    pos = (np.arange(S) - pad).astype(np.float32)
    inv = (500000.0 ** (-np.arange(8, dtype=np.float32) / 8)).astype(np.float32)
    ang = (pos[:, None] * inv[None, :]).astype(np.float32)
    return np.concatenate([np.cos(ang), np.sin(ang)], axis=1).astype(np.float32)


_NC_CACHE = {}


def kernel(x_prompt, x_sample, p_prompt, p_sample, cache_k_cmp, cache_v_cmp, cache_k_slc,
           cache_v_slc, cache_k_win, cache_v_win, state_gla, state_conv, page_table,
           g_attn, w_in, w_gla_gate, b_gla_gate, g_gla_out, b_nsa_gate, w_cmp_k, w_cmp_v,
           w_o, g_ffn, w_up, w_conv, b_conv, w_down, g_ple, w_ple, w_ple_gate, g_final, _prompt_only=False, _dbg=None, _do_sample=True):
    x_prompt = np.asarray(x_prompt)
    B, S, _ = x_prompt.shape
    NSEQ = x_sample.shape[0]
    PAST = page_table.shape[1] * 128
    chunk = S // 4
    NT = S // 128
    OWN = NT // 4
    NPOOLP = np.asarray(cache_k_cmp).shape[1]
    key = (S, NSEQ // 8, PAST, _dbg, _do_sample)
    if key not in _NC_CACHE:
        _NC_CACHE[key] = build(S, NSEQ // 8, PAST, dbg=_dbg, do_sample=_do_sample, NPOOLP=NPOOLP)
    nc = _NC_CACHE[key]
    f32 = lambda a: np.ascontiguousarray(np.asarray(a, dtype=np.float32))
    vecsA = np.concatenate([f32(g_attn)[0], f32(g_gla_out)[0], f32(b_nsa_gate)[0]])[None, :]
    vecsB = np.concatenate([f32(g_ffn)[0], f32(g_ple)[0], f32(g_final)])[None, :]
    wgg = np.concatenate([f32(w_gla_gate)[0], f32(b_gla_gate)], axis=0)
    wcm = np.zeros((128, 8), np.float32)
    for ti, wsrc in enumerate((f32(w_cmp_k)[0], f32(w_cmp_v)[0])):
        for g in range(2):
            for m in range(2):
                wcm[64 * m:64 * m + 64, ti * 4 + g * 2 + m] = wsrc[:, g]
    convw = np.zeros((128, 44, 4), np.float32)
    wcv = f32(w_conv)[0]
    convw[:, :, 0:3] = wcv.T.reshape(44, 128, 3).transpose(1, 0, 2)
    convw[:, :, 3] = f32(b_conv)[0].reshape(44, 128).T
    nti = min(32, NT)
    indp = np.zeros((64, nti * 128), np.float32)
    for m in range(nti):
        for k_ in range(128):
            indp[2 * m + k_ // 64, m * 128 + k_] = 1.0
    in_maps = []
    for c in range(8):
        b, j = c // 4, c % 4
        pad = (3 - j) * chunk
        xl = np.zeros((S, D), np.float32)
        xl[pad:] = x_prompt[b, 0:(j + 1) * chunk]
        kbias = np.zeros((1, S), np.float32)
        kbias[0, :pad] = NEG
        in_maps.append(dict(
            xl=xl, pl=f32(p_prompt[0, b, j * chunk:(j + 1) * chunk]), cs=_rope_table(S, pad), cmask=_consts(S, pad), kbias=kbias,
            ind=indp, wc=wcm, w_in=f32(w_in)[0], wgg=wgg, vecsA=vecsA, vecsB=vecsB, w_o=f32(w_o)[0], w_up=f32(w_up)[0],
            convw=np.ascontiguousarray(convw.reshape(128, 176)), w_down=f32(w_down)[0], w_ple=f32(w_ple)[0], w_pg=f32(w_ple_gate)[0]))

    if _do_sample:
        NS = NSEQ // 8
        NPG = PAST // 128
        SPG = 128 // NPG
        NGR = NS // SPG
        NREP = 1 + NGR + 2
        pools_h = [np.ascontiguousarray(np.asarray(a, dtype=np.float32)[0].reshape(NPOOLP, 16384)) for a in (cache_k_cmp, cache_v_cmp, cache_k_slc, cache_v_slc)]
        inv = (500000.0 ** (-np.arange(8, dtype=np.float32) / 8)).astype(np.float32)
        ang = (np.float32(PAST) * inv).astype(np.float32)
        csrow = np.concatenate([np.cos(ang), np.sin(ang)]).astype(np.float32)
        pidx = np.arange(128)
        Rep = np.zeros((NS, NREP, 128), np.float32)
        for s_ in range(NS):
            Rep[s_, 0, :] = (pidx // 8 == s_)
            for r_ in range(NGR):
                Rep[s_, 1 + r_, :] = (r_ * SPG + pidx // NPG == s_)
            for hf in range(2):
                Rep[s_, 1 + NGR + hf, :] = (8 * hf + pidx // 16 == s_)
        selS = np.zeros((NS, NS, 64), np.float32)
        for s_ in range(NS):
            selS[s_, s_, :] = 1.0
        eye = np.eye(16, dtype=np.float32)
        maskW = np.zeros((128, 64), np.float32)
        maskW[pidx % 8 == 0, 0] = NEG
        wrow = np.stack([f32(w_cmp_k)[0], f32(w_cmp_v)[0]], axis=0)
        repc = np.concatenate([Rep.transpose(2, 1, 0).reshape(128, NREP * 16), np.broadcast_to(eye.reshape(1, 256), (128, 256)),
                               maskW, np.broadcast_to(wrow.reshape(1, 256), (128, 256)), (pidx % 16 != 15).astype(np.float32)[:, None]], axis=1)
        ptab = np.asarray(page_table).astype(np.int32)
        for c in range(8):
            sl = slice(c * NS, (c + 1) * NS)
            scs = np.concatenate([np.broadcast_to(csrow[None, :], (NS, 16)), selS.reshape(NS, 1024), Rep.reshape(NS, NREP * 128), eye[:NS],
                                  np.broadcast_to(np.arange(128, dtype=np.float32)[None, :], (NS, 128)), np.zeros((NS, NPG), np.float32),
                                  np.zeros((NS, 16), np.float32)], axis=1)
            pt_c = ptab[sl]
            ptab_pg = np.ascontiguousarray(pt_c.reshape(NGR, SPG * NPG).T)
            in_maps[c].update(dict(
                xs=f32(x_sample[sl, 0]), ps=f32(p_sample[0, sl, 0]), ptab=ptab_pg, ptabs=np.ascontiguousarray(pt_c),
                pool0=pools_h[0], pool1=pools_h[1], pool2=pools_h[2], pool3=pools_h[3],
                wink=np.ascontiguousarray(f32(cache_k_win)[0, sl].reshape(NS, 512, 128)), winv=np.ascontiguousarray(f32(cache_v_win)[0, sl].reshape(NS, 512, 128)),
                sgla=f32(state_gla)[0, sl], sconv=np.ascontiguousarray(f32(state_conv)[0, sl].reshape(NS, -1)),
                scs=np.ascontiguousarray(scs.astype(np.float32)), rep=np.ascontiguousarray(repc.astype(np.float32))))
    res = run_bass_kernel_spmd(nc, in_maps, core_ids=list(range(8)))
    R = res.results
    global _LAST, _LASTNC
    _LAST = R
    _LASTNC = nc
    y_prompt = np.zeros((B, S, D), np.float32)
    kv = np.zeros((B, S, 768), np.float32)
    for c in range(8):
        b, j = c // 4, c % 4
        y_prompt[b, j * chunk:(j + 1) * chunk] = R[c]["y"]
        kv[b, j * chunk:(j + 1) * chunk] = R[c]["kv6"]
    kv = kv.reshape(B, S, 6, 2, 64)
    nw = min(512, S)
    outs_p = [kv[None, :, :, i] for i in range(4)] + [kv[None, :, S - nw:, 4], kv[None, :, S - nw:, 5]]
    gl = np.stack([R[3]["glast"], R[7]["glast"]])[None]
    cv = np.stack([R[3]["convrows"], R[7]["convrows"]])[None]
    outs_p = [np.ascontiguousarray(o) for o in outs_p] + [gl, cv]
    if _prompt_only:
        return (y_prompt, None, *outs_p)
    cat = lambda k: np.concatenate([np.asarray(R[c][k]) for c in range(8)], axis=0)
    y_s = cat("ys")[:, None, :]
    skv = cat("skv6").reshape(NSEQ, 1, 6, 2, 64)
    outs_s = [np.ascontiguousarray(skv[None, :, :, i]) for i in range(4)]
    outs_s += [cat("swk").reshape(1, NSEQ, 512, 2, 64), cat("swv").reshape(1, NSEQ, 512, 2, 64)]
    outs_s += [cat("sglo")[None], cat("scvo")[None]]
    return (y_prompt, y_s, *outs_p, *outs_s)
```

```python
import contextlib
import numpy as np
import ml_dtypes
import concourse.bass as bass
import concourse.mybir as mybir
from concourse.bass_utils import run_bass_kernel_spmd

import os
SKIPKV = bool(os.environ.get('SKIPKV'))
F32 = mybir.dt.float32
BF16 = mybir.dt.bfloat16
I32 = mybir.dt.int32
ALU = mybir.AluOpType
AF = mybir.ActivationFunctionType
AX = mybir.AxisListType

D = 1024
DFF = 2816
DIN = 2856
OFF = dict(gq=0, gk=256, gv=512, gr=1024, glr=1536, nq=1552, kc=2064, vc=2192, ks=2320, vs=2448,
           kw=2576, vw=2704, ng=2832)
NEG = -30000.0
EPS = 1e-6


class _Rec:
    def __getattr__(self, name):
        def mk(*a, **k):
            return lambda eng: getattr(eng, name)(*a, **k)
        return mk


REC = _Rec()


class KB:
    ENG = ['pe', 'act', 'dve', 'pool', 'sp']
    NDS = 40

    def __init__(self, nc):
        self.nc = nc
        self.q = {e: [] for e in self.ENG}
        self.cnt = {e: 0 for e in self.ENG}
        self.seen = {e: {} for e in self.ENG}
        self.lastw = {}
        self.readers = {}
        self.dtot = [0] * self.NDS
        self.dnext = 0
        self.dnext_sw = 0
        self.pend = {e: [] for e in self.ENG}
        self.alias = {}

    def _need(self, E, tok, waits):
        if tok is None:
            return
        key, val = tok
        if key == 'pe' and E == 'pe':
            return
        if self.seen[E].get(key, 0) >= val:
            return
        self.seen[E][key] = val
        waits.append(tok)

    def _deps(self, E, r, w):
        waits = []
        r = [self.alias.get(x, x) for x in r]
        w = [self.alias.get(x, x) for x in w]
        for reg in r:
            self._need(E, self.lastw.get(reg), waits)
        for reg in w:
            self._need(E, self.lastw.get(reg), waits)
            for key, val in self.readers.get(reg, {}).items():
                self._need(E, (key, val), waits)
        return waits

    def _commit(self, tok, r, w):
        r = [self.alias.get(x, x) for x in r]
        w = [self.alias.get(x, x) for x in w]
        for reg in w:
            self.lastw[reg] = tok
            self.readers[reg] = {}
        for reg in r:
            d = self.readers.setdefault(reg, {})
            if d.get(tok[0], 0) < tok[1]:
                d[tok[0]] = tok[1]

    def barrier(self):
        for E in self.ENG:
            waits = self.pend[E]
            for e2 in ['pe', 'act', 'dve', 'pool']:
                if e2 != E and self.cnt[e2]:
                    self._need(E, (e2, self.cnt[e2]), waits)
            if E != 'pe' and self.cnt.get(E) and E != 'sp':
                self._need(E, (E, self.cnt[E]), waits)
            for k in range(self.NDS):
                if self.dtot[k]:
                    self._need(E, (('d', k), self.dtot[k]), waits)

    def op(self, E, fn, r=(), w=()):
        waits = self.pend[E] + self._deps(E, r, w)
        self.pend[E] = []
        self.cnt[E] += 1
        tok = (E, self.cnt[E])
        self.q[E].append((waits, fn, (E, 1)))
        self._commit(tok, r, w)

    def dma(self, E, fn, r=(), w=()):
        if E == 'pool':
            k = 32 + self.dnext_sw
            self.dnext_sw = (self.dnext_sw + 1) % 8
        else:
            k = self.dnext
            self.dnext = (self.dnext + 1) % 32
        waits = self.pend[E] + self._deps(E, r, w)
        self.pend[E] = []
        if self.dtot[k]:
            self._need(E, (('d', k), self.dtot[k]), waits)
        self.dtot[k] += 16
        tok = (('d', k), self.dtot[k])
        self.q[E].append((waits, fn, (('d', k), 16)))
        self._commit(tok, r, w)

    def emit(self):
        nc = self.nc
        waits = []
        for k in range(self.NDS):
            if self.dtot[k]:
                self._need('sp', (('d', k), self.dtot[k]), waits)
        for e in ['pe', 'act', 'dve', 'pool']:
            if self.cnt[e]:
                self._need('sp', (e, self.cnt[e]), waits)
        self.q['sp'].append((waits, None, None))
        with contextlib.ExitStack() as st:
            sems = {}
            for e in ['pe', 'act', 'dve', 'pool']:
                sems[e] = st.enter_context(nc.semaphore('c_' + e))
            for k in range(self.NDS):
                sems[('d', k)] = st.enter_context(nc.semaphore('d%d' % k))
            block = st.enter_context(nc.Block())

            def run(eng, items):
                for waits, fn, inc in items:
                    for key, val in waits:
                        eng.wait_ge(sems[key], val)
                    if fn is None:
                        continue
                    fn(eng).then_inc(sems[inc[0]], inc[1])

            @block.tensor
            def _(eng):
                run(eng, self.q['pe'])

            @block.scalar
            def _(eng):
                run(eng, self.q['act'])

            @block.vector
            def _(eng):
                run(eng, self.q['dve'])

            @block.gpsimd
            def _(eng):
                run(eng, self.q['pool'])

            @block.sync
            def _(eng):
                run(eng, self.q['sp'])


def build(S, NSMP, PAST, do_sample=True, dbg=None, NPOOLP=0):
    NT = S // 128
    NB = S // 64
    OWN = NT // 4
    HALO = NT - OWN - 1
    NQ = OWN + 1
    W0 = HALO - 4
    NW = NT - W0
    BT = min(128, NB)
    NBT = NB // BT
    assert W0 >= 0

    nc = bass.Bass("TRN2", target_bir_lowering=False)

    def din(name, shape, dt=F32):
        return nc.dram_tensor(name, list(shape), dt, kind="ExternalInput").ap()

    def dout(name, shape, dt=F32):
        return nc.dram_tensor(name, list(shape), dt, kind="ExternalOutput").ap()

    xl = din("xl", [S, D])
    pl = din("pl", [OWN * 128, 256])
    cs = din("cs", [S, 16])
    cmask = din("cmask", [128, 2 * NB + NB + 128 + 128 + 5 * 128 + 128 + 128])
    kbias = din("kbias", [1, S])
    ind = din("ind", [64, min(32, NT) * 128])
    wc = din("wc", [128, 8])
    w_in = din("w_in", [D, DIN])
    wgg = din("wgg", [17, 256])
    vecsA = din("vecsA", [1, 1560])
    vecsB = din("vecsB", [1, 3072])
    w_o = din("w_o", [D, D])
    w_up = din("w_up", [D, 2 * DFF])
    convw = din("convw", [128, 44 * 4])
    w_down = din("w_down", [DFF, D])
    w_ple = din("w_ple", [256, D])
    w_pg = din("w_pg", [D, D])

    NS = NSMP
    NPG = PAST // 128
    NBP = PAST // 64
    SPG = 128 // NPG
    NGR = NS // SPG
    NREP = 1 + NGR + 2
    if do_sample:
        xs_i = din("xs", [NS, D])
        ps_i = din("ps", [NS, 256])
        ptab_i = din("ptab", [128, NGR], I32)
        ptabs_i = din("ptabs", [NS, NPG], I32)
        pools = [nc.dram_tensor("pool%d" % i, [NPOOLP, 16384], F32, kind="ExternalInput") for i in range(4)]
        wink_i = din("wink", [NS, 512, 128])
        winv_i = din("winv", [NS, 512, 128])
        sgla_i = din("sgla", [NS, 4, 64, 128])
        sconv_i = din("sconv", [NS, 2 * 2 * DFF])
        scs_i = din("scs", [NS, 16 + 64 * 16 + 128 * NREP + 16 + 128 + NPG + 16])
        rep_i = din("rep", [128, NREP * 16 + 256 + 64 + 4 * 64 + 1])
        ys_o = dout("ys", [NS, D])
        skv_o = dout("skv6", [NS, 768])
        swk_o = dout("swk", [NS, 512, 128])
        swv_o = dout("swv", [NS, 512, 128])
        sgl_o = dout("sglo", [NS, 4, 64, 128])
        scv_o = dout("scvo", [NS, 2, 2 * DFF])
        oscr_s = nc.dram_tensor("oscr_s", [NS, D], BF16, kind="Internal").ap()
        scmp_d = nc.dram_tensor("scmp_d", [NS * NPG, 16], F32, kind="Internal").ap()
        vc_d = nc.dram_tensor("vc_d", [NS * NPG, 256], F32, kind="Internal").ap()
        idx_d = nc.dram_tensor("idx_d", [2, NS * 16], I32, kind="Internal").ap()
        sct_d = nc.dram_tensor("sct_d", [128, 88 * NS], F32, kind="Internal").ap()
    y_o = dout("y", [OWN * 128, D])
    kv_o = dout("kv6", [OWN * 128, 768])
    gl_o = dout("glast", [4, 64, 128])
    cv_o = dout("convrows", [2, 2 * DFF])
    oscr = nc.dram_tensor("oscr", [NQ * 128, D], BF16, kind="Internal").ap()

    kb = KB(nc)
    dbg_o = dout("dbg", [128, 8192]) if dbg is not None else None
    dbg_items = []

    def dump(name, ap, regs, g=None, n=None):
        if dbg is None or (g, n) != tuple(dbg):
            return
        P_, C_ = ap.shape[0], int(np.prod(ap.shape[1:]))
        c0 = sum(it[2] for it in dbg_items)
        dbg_items.append((name, P_, C_, c0, tuple(ap.shape)))
        kb.dma('sp', REC.dma_start(out=dbg_o[0:P_, c0:c0 + C_], in_=ap), r=regs, w=['dbg_o'])
    nc._dbg_items = dbg_items
    with contextlib.ExitStack() as st0:
        def sb(st, name, shape, dt):
            return st.enter_context(nc.sbuf_tensor("s_" + name, list(shape), dt))

        def pst(st, name, shape, dt):
            return st.enter_context(nc.psum_tensor(name, list(shape), dt))

        PB = [pst(st0, "pb%d" % i, [128, 512], F32) for i in range(8)]

        def pbf(i):
            return PB[i][:].bitcast(BF16)

        identb = sb(st0, "identb", [128, 128], BF16)
        identf = sb(st0, "identf", [128, 128], F32)
        CMst = contextlib.ExitStack()
        CM = sb(CMst, "CM", [128, 2 * NB + NB + 128 + 128 + 5 * 128 + 128 + 128], F32)
        vrep = sb(CMst, "vrep", [128, 1560], F32)
        kb.dma('sp', REC.dma_start(out=CM[:], in_=cmask), w=['CM'])
        kb.dma('sp', REC.dma_start(out=vrep[:], in_=vecsA.partition_broadcast(128)), w=['vrep'])
        valid = CM[:, 0:NB]
        first = CM[:, NB:2 * NB]
        D0 = CM[:, 2 * NB:3 * NB]
        o_ = 3 * NB
        identf_src = CM[:, o_:o_ + 128]
        tri_src = CM[:, o_ + 128:o_ + 256]
        wb_src = CM[:, o_ + 256:o_ + 256 + 640]
        tribd = CM[:, o_ + 896:o_ + 1024]
        rmask = CM[0:64, o_ + 1024:o_ + 1152]
        kb.op('dve', REC.tensor_copy(out=identb[:], in_=identf_src), r=['CM'], w=['identb'])
        kb.op('dve', REC.tensor_copy(out=identf[:], in_=identf_src), r=['CM'], w=['identf'])
        g_attn = vrep[:, 0:1024]
        g_gla = vrep[:, 1024:1536]
        b_ng = vrep[:, 1536:1560]

        with contextlib.ExitStack() as stA:
            w_in_sb = sb(stA, "w_in_sb", [128, 8, DIN], BF16)
            for k in range(8):
                for c0 in (0, 1428):
                    kb.dma('pool', REC.dma_start(
                        out=w_in_sb[:, k, c0:c0 + 1428], in_=w_in[k * 128:(k + 1) * 128, c0:c0 + 1428]), w=['w_in_sb'])
            wgg_sb = sb(stA, "wgg_sb", [17, 256], BF16)
            kb.dma('pool', REC.dma_start(out=wgg_sb[:], in_=wgg), w=['wgg_sb'])
            wc_sb = sb(stA, "wc_sb", [128, 8], BF16)
            kb.dma('pool', REC.dma_start(out=wc_sb[:], in_=wc), w=['wc_sb'])
            tri4 = sb(stA, "tri4", [128, 4, 128], BF16)
            wb4 = sb(stA, "wb4", [128, 5, 4, 128], BF16)
            for h in range(4):
                kb.op('dve', REC.tensor_copy(out=tri4[:, h, :], in_=tri_src), r=['CM'], w=['tri4'])
                kb.op('dve', REC.tensor_copy(out=wb4[:, :, h, :], in_=wb_src.rearrange("p (t f) -> p t f", t=5)),
                      r=['CM'], w=['wb4'])
            ksT = sb(stA, "ksT", [128, S], BF16)
            vsx = sb(stA, "vsx", [128, NT, 65], BF16)
            kwT = sb(stA, "kwT", [65, NW * 128], BF16)
            vwx = sb(stA, "vwx", [128, NW, 65], BF16)
            kcT = sb(stA, "kcT", [64, NB], BF16)
            vcT = sb(stA, "vcT", [64, NB], BF16)
            vcs = sb(stA, "vcs", [BT, NBT, 64], BF16)
            kb.op('pool', REC.memset(kcT[:], 0.0), w=['kcT'])
            kb.op('pool', REC.memset(vcT[:], 0.0), w=['vcT'])
            kb.op('pool', REC.memset(ksT[0:64, :], 0.0), w=['ksT'])
            kb.op('pool', REC.memset(vsx[:], 1.0), w=['vsx'])
            kb.op('pool', REC.memset(vwx[:], 1.0), w=['vwx'])
            rep = S // (min(32, NT) * 128)
            for r_ in range(rep):
                w_ = min(32, NT) * 128
                kb.dma('pool', REC.dma_start(out=ksT[64:128, r_ * w_:(r_ + 1) * w_], in_=ind), w=['ksT'])
            kb.dma('pool', REC.dma_start(out=kwT[64:65, :], in_=kbias[:, W0 * 128:S]), w=['kwT'])
            hst = sb(stA, "hst", [64, 4, 128], F32)
            hbf = sb(stA, "hbf", [64, 4, 128], BF16)
            hbf1 = sb(stA, "hbf1", [64, 4, 128], BF16)
            kb.op('pool', REC.memset(hst[:], 0.0), w=['hst'])
            kb.op('pool', REC.memset(hbf[:], 0.0), w=['hbf'])
            glr_x = sb(stA, "glr_x", [32, 128], BF16)
            kb.op('pool', REC.memset(glr_x[:], 1.0), w=['glr_x'])
            qez = sb(stA, "qez", [64, 2, 4, 128], BF16)
            kb.op('pool', REC.memset(qez[:], 0.0), w=['qez'])
            xt = [sb(stA, "xt%d" % i, [128, D], F32) for i in range(2)]
            cst = [sb(stA, "cst%d" % i, [128, 16], F32) for i in range(2)]
            ss = sb(stA, "ss", [128, 8], F32)
            junk = sb(stA, "junk", [128, D], BF16)
            xn2 = [sb(stA, "xn0", [128, D], BF16)] * 2
            xnT2 = [sb(stA, "xnT%d" % i, [128, 8, 128], BF16) for i in range(2)]
            kvf2 = [sb(stA, "kvf0", [128, 6, 64], F32)] * 2
            ktmp = sb(stA, "ktmp", [128, 4, 4, 8], F32)
            kvb2 = [sb(stA, "kvb%d" % i, [128, 6, 64], BF16) for i in range(2)]
            kv6 = sb(stA, "kv6", [128, 6, 2, 64], F32)
            v_bf2 = [sb(stA, "v_bf%d" % i, [128, 512], BF16) for i in range(2)]
            G4 = sb(stA, "G4", [64, 4, 512], F32)
            gb = sb(stA, "gb", [64, 3, 512], BF16)
            kd_tok = sb(stA, "kd_tok", [128, 4, 64], BF16)
            attm = sb(stA, "attm", [128, 4, 128], BF16)
            sg = sb(stA, "sg", [64, 16], F32)
            rmask4 = sb(stA, "rmask4", [64, 512], F32)
            for hh_ in range(4):
                kb.op('dve', REC.tensor_copy(out=rmask4[:, 128 * hh_:128 * hh_ + 128], in_=rmask), r=['CM'], w=['rmask4'])
            sgr = sb(stA, "sgr", [128, 512], BF16)
            og = sb(stA, "og", [128, 512], F32)
            laT = og[:, 0:256]
            ekT = og[:, 256:512]
            ogb = sb(stA, "ogb", [128, 512], BF16)
            sm8 = sb(stA, "sm8", [128, 16], F32)
            qf = sb(stA, "qf", [128, 4, 64], F32)
            gts = sb(stA, "gts", [128, 12], F32)
            qtmp = sb(stA, "qtmp", [128, 4, 4, 8], F32)
            qpair = sb(stA, "qpair", [128, 4, 128], BF16)
            qT = sb(stA, "qT", [65, 4, 128], BF16)
            kb.op('pool', REC.memset(qpair[:], 0.0), w=['qpair'])
            kb.op('pool', REC.memset(qT[:], 1.0), w=['qT'])
            dd = sb(stA, "dd", [128, NB], F32)
            cb = sb(stA, "cb", [128, NB], F32)
            smx = sb(stA, "smx", [128, 4, NB], F32)
            sc = sb(stA, "sc", [128, NB], F32)
            sc2 = sb(stA, "sc2", [128, NB], F32)
            t1 = sb(stA, "t1", [128, NB], F32)
            t2 = cb
            selb = sb(stA, "selb", [128, NB], F32)
            m8 = sb(stA, "m8", [128, 16], F32)
            pT = sb(stA, "pT", [BT, 4, NBT, 128], BF16)
            NM = max(1, NB // 64)
            rhsm = sb(stA, "rhsm", [128, NM, 512], BF16)
            SBANKS = [0, 1, 7]
            PTbuf = sb(stA, "PTbuf", [128, 3, max(512, 2 * NB)], BF16)
            PT = [PTbuf[:, i, 0:512] for i in range(3)]
            pcb = PTbuf[:].rearrange("p a b -> p (a b)")[:, 0:4 * NB].rearrange("p (h b) -> p h b", b=NB)
            oT = sb(stA, "oT", [65, 2, 512], F32)
            ocm = sb(stA, "ocm", [128, 4, 64], F32)
            onb = sb(stA, "onb", [128, 256], BF16)
            rc = sb(stA, "rc", [128, 16], F32)

            def rmsnorm_tile(src, gvec, dst_bf, srcreg, dstreg):
                kb.op('act', REC.activation(out=junk[:], in_=src, func=AF.Square, accum_out=ss[:, 0:1]),
                      r=[srcreg], w=['junk', 'ss'])
                kb.op('dve', REC.tensor_scalar(out=ss[:, 1:2], in0=ss[:, 0:1], scalar1=1.0 / D, scalar2=EPS,
                                                       op0=ALU.mult, op1=ALU.add), r=['ss'], w=['ss1'])
                kb.op('act', REC.activation(out=ss[:, 3:4], in_=ss[:, 1:2], func=AF.Sqrt), r=['ss1'], w=['ss3'])
                kb.op('dve', REC.reciprocal(out=ss[:, 2:3], in_=ss[:, 3:4]), r=['ss3'], w=['ss2'])
                kb.op('dve', REC.scalar_tensor_tensor(out=dst_bf, in0=src, scalar=ss[:, 2:3], in1=gvec,
                                                              op0=ALU.mult, op1=ALU.mult), r=[srcreg, 'ss2', 'vrep'], w=[dstreg])

            def transpose8(src_bf, dstT, srcreg, dstreg, bank=4):
                for k in range(8):
                    kb.op('pe', REC.transpose(out=pbf(bank)[:, k * 128:(k + 1) * 128],
                                                           in_=src_bf[:, k * 128:(k + 1) * 128], identity=identb[:]),
                          r=[srcreg, 'identb'], w=['pb%d' % bank])
                kb.op('act', REC.copy(out=dstT[:].rearrange("p a b -> p (a b)"), in_=pbf(bank)[:, 0:1024]),
                      r=['pb%d' % bank], w=[dstreg])

            def proj(bank, ncols, rhs_fn, M=128, lhs_fn=None, r=()):
                for k in range(8):
                    kb.op('pe', REC.matmul(PB[bank][0:M, 0:ncols], lhsT=xnT[:, k, 0:M], rhs=rhs_fn(k),
                                                        start=(k == 0), stop=(k == 7)),
                          r=['xnT', 'w_in_sb'] + list(r), w=['pb%d' % bank])

            def rope(dst, src, nh, cs_t, tmp, scale, sreg, dreg, tmpreg):
                c = cs_t[:, 0:8].unsqueeze(1).to_broadcast([128, nh, 8])
                s = cs_t[:, 8:16].unsqueeze(1).to_broadcast([128, nh, 8])
                x1 = src[:, :, 0:8]
                x2 = src[:, :, 8:16]
                kb.op('dve', REC.scalar_tensor_tensor(out=tmp[:, 0:nh, 0, :], in0=x1, scalar=scale, in1=c, op0=ALU.mult, op1=ALU.mult), r=[sreg, 'cst'], w=[tmpreg])
                kb.op('dve', REC.scalar_tensor_tensor(out=tmp[:, 0:nh, 1, :], in0=x2, scalar=scale, in1=s, op0=ALU.mult, op1=ALU.mult), r=[sreg, 'cst'], w=[tmpreg])
                kb.op('dve', REC.scalar_tensor_tensor(out=tmp[:, 0:nh, 2, :], in0=x2, scalar=scale, in1=c, op0=ALU.mult, op1=ALU.mult), r=[sreg, 'cst'], w=[tmpreg])
                kb.op('dve', REC.scalar_tensor_tensor(out=tmp[:, 0:nh, 3, :], in0=x1, scalar=scale, in1=s, op0=ALU.mult, op1=ALU.mult), r=[sreg, 'cst'], w=[tmpreg])
                kb.op('act', REC.mul(out=dst[:, :, 16:64], in_=src[:, :, 16:64], mul=scale), r=[sreg], w=[dreg])
                kb.op('dve', REC.tensor_tensor(out=dst[:, :, 0:8], in0=tmp[:, 0:nh, 0, :], in1=tmp[:, 0:nh, 1, :], op=ALU.subtract), r=[tmpreg], w=[dreg])
                kb.op('dve', REC.tensor_tensor(out=dst[:, :, 8:16], in0=tmp[:, 0:nh, 2, :], in1=tmp[:, 0:nh, 3, :], op=ALU.add), r=[tmpreg], w=[dreg])

            def stageA(n):
                nonlocal xn, xnT
                xn, xnT = xn2[n % 2], xnT2[n % 2]
                kb.alias = {r_: r_ + '_%d' % (n % 2) for r_ in ('xnT', 'kvb', 'v_bf', 'cst')}
                xb_ = xt[n % 2]
                xreg = 'xt%d' % (n % 2)
                kb.dma('sp', REC.dma_start(out=xb_[:], in_=xl[n * 128:(n + 1) * 128, :]), w=[xreg])
                kb.dma('sp', REC.dma_start(out=cst[n % 2][:], in_=cs[n * 128:(n + 1) * 128, :]), w=['cst'])
                rmsnorm_tile(xb_[:], g_attn, xn[:], xreg, 'xn')
                transpose8(xn, xnT, 'xn', 'xnT')

            xn = xnT = None
            stageA(0)
            for g in range(2):
                for n in range(NT):
                    if not (g == 1 and n == NT - 1):
                        stageA((n + 1) % NT)
                    xn, xnT, kvf, kvb, v_bf = xn2[n % 2], xnT2[n % 2], kvf2[n % 2], kvb2[n % 2], v_bf2[n % 2]
                    kb.alias = {r_: r_ + '_%d' % (n % 2) for r_ in ('xnT', 'kvb', 'v_bf', 'cst')}
                    cst_ = cst[n % 2]
                    proj(0, 384, lambda k: w_in_sb[:, k, OFF['kc']:OFF['kc'] + 768].rearrange("p (i c) -> p i c", c=128)[:, :, 64 * g:64 * g + 64])
                    pkv = PB[0][:, 0:384].rearrange("p (i c) -> p i c", c=64)
                    kb.op('act', REC.copy(out=kvf[:], in_=pkv), r=['pb0'], w=['kvf'])
                    kview = kvf[:].rearrange("p (i two) c -> p i two c", two=2)
                    rope(kview[:, :, 0, :], kview[:, :, 0, :], 3, cst_, ktmp, 1.0, 'kvf', 'kvf', 'ktmp')
                    kb.op('act', REC.copy(out=kvb[:], in_=kvf[:]), r=['kvf'], w=['kvb'])
                    if n >= NT - OWN and SKIPKV:
                        pass
                    elif n >= NT - OWN:
                        for i6 in range(6):
                            kb.dma('sp', REC.dma_start(
                                out=kv_o[(n - (NT - OWN)) * 128:(n - (NT - OWN) + 1) * 128, i6 * 128 + g * 64:i6 * 128 + g * 64 + 64],
                                in_=kvf[:, i6, :]), r=['kvf'], w=['kv_o'])
                    kb.op('pe', REC.transpose(out=pbf(4)[0:64, 0:128], in_=kvb[:, 2, :], identity=identb[:]),
                          r=['kvb', 'identb'], w=['pb4'])
                    kb.op('act', REC.copy(out=ksT[0:64, n * 128:(n + 1) * 128], in_=pbf(4)[0:64, 0:128]), r=['pb4'], w=['ksT'])
                    kb.op('pool', REC.tensor_copy(out=vsx[:, n, 0:64], in_=kvb[:, 3, :]), r=['kvb'], w=['vsx'])
                    if n >= W0:
                        kb.op('pe', REC.transpose(out=pbf(4)[0:64, 128:256], in_=kvb[:, 4, :], identity=identb[:]),
                              r=['kvb', 'identb'], w=['pb4'])
                        kb.op('act', REC.copy(out=kwT[0:64, (n - W0) * 128:(n - W0 + 1) * 128], in_=pbf(4)[0:64, 128:256]),
                              r=['pb4'], w=['kwT'])
                        kb.op('pool', REC.tensor_copy(out=vwx[:, n - W0, 0:64], in_=kvb[:, 5, :]), r=['kvb'], w=['vwx'])
                    kb.op('pe', REC.matmul(PB[5][0:64, 0:2], lhsT=kvb[:, 0, :], rhs=wc_sb[:, g * 2:g * 2 + 2], start=True, stop=True),
                          r=['kvb', 'wc_sb'], w=['pb5'])
                    kb.op('pe', REC.matmul(PB[5][0:64, 2:4], lhsT=kvb[:, 1, :], rhs=wc_sb[:, 4 + g * 2:4 + g * 2 + 2], start=True, stop=True),
                          r=['kvb', 'wc_sb'], w=['pb5'])
                    kb.op('dve', REC.tensor_copy(out=kcT[:, 2 * n:2 * n + 2], in_=PB[5][0:64, 0:2]), r=['pb5'], w=['kcT'])
                    kb.op('dve', REC.tensor_copy(out=vcT[:, 2 * n:2 * n + 2], in_=PB[5][0:64, 2:4]), r=['pb5'], w=['vcT'])

                    if g == 0:
                        need_o = n >= HALO
                        proj(1, 512, lambda k: w_in_sb[:, k, OFF['gv']:OFF['gv'] + 512])
                        kb.op('act', REC.copy(out=v_bf[:], in_=PB[1][:, :]), r=['pb1'], w=['v_bf'])
                        if need_o:
                            proj(1, 512, lambda k: w_in_sb[:, k, OFF['gr']:OFF['gr'] + 512])
                            kb.op('act', REC.activation(out=sgr[:], in_=PB[1][:, :], func=AF.Silu), r=['pb1'], w=['sgr'])
                        for k in range(8):
                            kb.op('pe', REC.matmul(PB[5][0:16, 128:256], lhsT=w_in_sb[:, k, OFF['glr']:OFF['glr'] + 16], rhs=xnT[:, k, :],
                                                                start=(k == 0), stop=(k == 7)), r=['xnT', 'w_in_sb'], w=['pb5'])
                        kb.op('dve', REC.tensor_copy(out=glr_x[0:16, :], in_=PB[5][0:16, 128:256]), r=['pb5'], w=['glr_x'])
                        if not need_o:
                            for k in range(8):
                                kb.op('pe', REC.matmul(PB[6][:, 0:256], lhsT=xnT[:, k, :], rhs=w_in_sb[:, k, OFF['gk']:OFF['gk'] + 256], start=(k == 0), stop=(k == 7)),
                                      r=['xnT', 'w_in_sb'], w=['pb6'])
                            kb.op('pe', REC.matmul(PB[2][:, 0:256], lhsT=glr_x[0:17, :], rhs=wgg_sb[0:17, :], start=True, stop=True), r=['glr_x', 'wgg_sb'], w=['pb2'])
                            kb.op('act', REC.activation(out=laT, in_=PB[2][:, 0:256], func=AF.Exp, scale=-1.0), r=['pb2'], w=['og'])
                            kb.op('act', REC.activation(out=laT, in_=laT, func=AF.Ln, bias=1.0), r=['og'], w=['og'])
                            kb.op('pe', REC.matmul(PB[3][:, 0:256], lhsT=tribd, rhs=laT, start=True, stop=True), r=['CM', 'og'], w=['pb3'])
                            kb.op('act', REC.activation(out=ekT, in_=PB[3][:, 0:256], func=AF.Exp, scale=1.0 / 16), r=['pb3'], w=['og'])
                            kb.op('dve', REC.tensor_tensor(out=kd_tok[:].rearrange("p a b -> p (a b)"), in0=PB[6][:, 0:256], in1=ekT, op=ALU.mult), r=['pb6', 'og'], w=['kd_tok'])
                            for hh in range(4):
                                kb.op('pe', REC.matmul(PB[7][0:64, 2 * hh:2 * hh + 2], lhsT=ekT[:, 64 * hh:64 * hh + 64], rhs=identf[:, 63:128:64], start=True, stop=True),
                                      r=['og', 'identf'], w=['pb7'])
                            kb.op('dve', REC.reciprocal(out=sg[:, 8:16], in_=PB[7][0:64, 0:8]), r=['pb7'], w=['sg8'])
                            for c in range(2):
                                for hh in range(4):
                                    kb.op('pe', REC.matmul(PB[0][0:64, 128 * hh:128 * hh + 128], lhsT=kd_tok[64 * c:64 * c + 64, hh, :],
                                                           rhs=v_bf[64 * c:64 * c + 64, 128 * hh:128 * hh + 128], start=True, stop=True), r=['kd_tok', 'v_bf'], w=['pb0'])
                                ebc = sg[:, 8:16].rearrange("p (h c) -> p h c", c=2)[:, :, c:c + 1].to_broadcast([64, 4, 128])
                                kb.op('dve', REC.tensor_tensor(out=hst[:].rearrange("p a b -> p (a b)"), in0=hst[:].rearrange("p a b -> p (a b)"), in1=PB[0][0:64, :], op=ALU.add),
                                      r=['hst', 'pb0'], w=['hst'])
                                kb.op('dve', REC.tensor_tensor(out=hst[:], in0=hst[:], in1=ebc, op=ALU.mult), r=['hst', 'sg8'], w=['hst'])
                            continue
                        for which, bank in ((('gq', 6), ('gk', 7)) if need_o else (('gk', 7),)):
                            for hh in range(4):
                                for k in range(8):
                                    kb.op('pe', REC.matmul(PB[bank][0:64, 128 * hh:128 * hh + 128], lhsT=w_in_sb[:, k, OFF[which] + 64 * hh:OFF[which] + 64 * hh + 64],
                                                           rhs=xnT[:, k, :], start=(k == 0), stop=(k == 7)), r=['xnT', 'w_in_sb'], w=['pb%d' % bank])
                        for hh in range(4):
                            kb.op('pe', REC.matmul(PB[2][0:64, 128 * hh:128 * hh + 128], lhsT=wgg_sb[0:17, 64 * hh:64 * hh + 64], rhs=glr_x[0:17, :],
                                                   start=True, stop=True), r=['glr_x', 'wgg_sb'], w=['pb2'])
                        G = G4
                        kb.op('act', REC.activation(out=G[:, 0, :], in_=PB[2][0:64, :], func=AF.Exp, scale=-1.0), r=['pb2'], w=['G0'])
                        kb.op('act', REC.activation(out=G[:, 0, :], in_=G[:, 0, :], func=AF.Ln, bias=1.0), r=['G0'], w=['G0'])
                        kb.op('dve', REC.tensor_tensor_scan(out=G[:, 1, :], data0=rmask4[:], data1=G[:, 0, :], initial=0.0, op0=ALU.mult, op1=ALU.add),
                              r=['G0', 'rmask4'], w=['G1'])
                        kb.op('act', REC.activation(out=G[:, 2, :], in_=G[:, 1, :], func=AF.Exp, scale=-1.0 / 16), r=['G1'], w=['G2'])
                        kb.op('act', REC.activation(out=G[:, 3, :], in_=G[:, 1, :], func=AF.Exp, scale=1.0 / 16), r=['G1'], w=['G3'])
                        kb.op('dve', REC.tensor_scalar(out=sg[:, 0:8].rearrange("p (h c) -> p h c", c=2), in0=G[:, 1, :].rearrange("p (h t) -> p h t", t=128)[:, :, 63:128:64],
                                                       scalar1=-1.0 / 16, scalar2=None, op0=ALU.mult), r=['G1'], w=['sg0'])
                        kb.op('act', REC.activation(out=sg[:, 8:16], in_=sg[:, 0:8], func=AF.Exp), r=['sg0'], w=['sg8'])
                        kb.op('dve', REC.tensor_tensor(out=G[:, 0, :].rearrange("p (a t) -> p a t", t=64), in0=G[:, 3, :].rearrange("p (a t) -> p a t", t=64),
                                                       in1=sg[:, 8:16].unsqueeze(2).to_broadcast([64, 8, 64]), op=ALU.mult), r=['G3', 'sg8', 'G0'], w=['G0'])
                        kb.op('dve', REC.tensor_tensor(out=gb[:, 2, :], in0=PB[7][0:64, :], in1=G[:, 0, :], op=ALU.mult), r=['pb7', 'G0'], w=['gb2'])
                        for hh in range(4):
                            kb.op('pe', REC.transpose(out=pbf(4)[:, hh * 64:(hh + 1) * 64], in_=gb[:, 2, 128 * hh:128 * hh + 128], identity=identb[0:64, 0:64]),
                                  r=['gb2', 'identb'], w=['pb4'])
                        kb.op('act', REC.copy(out=kd_tok[:].rearrange("p a b -> p (a b)"), in_=pbf(4)[:, 0:256]), r=['pb4'], w=['kd_tok'])
                        if need_o:
                            kb.op('dve', REC.scalar_tensor_tensor(out=gb[:, 0, :], in0=PB[6][0:64, :], scalar=0.125, in1=G[:, 2, :], op0=ALU.mult, op1=ALU.mult),
                                  r=['pb6', 'G2'], w=['gb0'])
                            kb.op('dve', REC.tensor_tensor(out=gb[:, 1, :], in0=PB[7][0:64, :], in1=G[:, 3, :], op=ALU.mult), r=['pb7', 'G3'], w=['gb1'])
                            gb0v = gb[:, 0, :].rearrange("p (h t) -> p h t", t=128)
                            kb.op('pool', REC.tensor_copy(out=qez[:, 0, :, 0:64], in_=gb0v[:, :, 0:64]), r=['gb0'], w=['qez'])
                            kb.op('pool', REC.tensor_copy(out=qez[:, 1, :, 64:128], in_=gb0v[:, :, 64:128]), r=['gb0'], w=['qez'])
                            for hh in range(4):
                                kb.op('pe', REC.matmul(PB[3][:, 128 * hh:128 * hh + 128], lhsT=gb[:, 1, 128 * hh:128 * hh + 128], rhs=gb[:, 0, 128 * hh:128 * hh + 128],
                                                       start=True, stop=True), r=['gb0', 'gb1'], w=['pb3'])
                            kb.op('dve', REC.tensor_tensor(out=attm[:], in0=PB[3][:, :].rearrange("p (h t) -> p h t", t=128),
                                                           in1=tribd.unsqueeze(1).to_broadcast([128, 4, 128]), op=ALU.mult), r=['pb3', 'CM'], w=['attm'])

                        def upd(c):
                            for hh in range(4):
                                kb.op('pe', REC.matmul(PB[0][0:64, 128 * hh:128 * hh + 128], lhsT=kd_tok[64 * c:64 * c + 64, hh, :],
                                                       rhs=v_bf[64 * c:64 * c + 64, 128 * hh:128 * hh + 128], start=True, stop=True), r=['kd_tok', 'v_bf'], w=['pb0'])
                            ebc = sg[:, 8:16].rearrange("p (h c) -> p h c", c=2)[:, :, c:c + 1].to_broadcast([64, 4, 128])
                            kb.op('dve', REC.tensor_tensor(out=hst[:], in0=hst[:], in1=ebc, op=ALU.mult), r=['hst', 'sg8'], w=['hst'])
                            kb.op('dve', REC.tensor_tensor(out=hst[:].rearrange("p a b -> p (a b)"), in0=hst[:].rearrange("p a b -> p (a b)"), in1=PB[0][0:64, :], op=ALU.add),
                                  r=['hst', 'pb0'], w=['hst'])
                        if need_o:
                            upd(0)
                            kb.op('act', REC.copy(out=hbf1[:], in_=hst[:]), r=['hst'], w=['hbf1'])
                            for hh in range(4):
                                kb.op('pe', REC.matmul(PB[2][:, 128 * hh:128 * hh + 128], lhsT=attm[:, hh, :], rhs=v_bf[:, 128 * hh:128 * hh + 128], start=True, stop=False),
                                      r=['attm', 'v_bf'], w=['pb2'])
                                kb.op('pe', REC.matmul(PB[2][:, 128 * hh:128 * hh + 128], lhsT=qez[:, 0, hh, :], rhs=hbf[:, hh, :], start=False, stop=False),
                                      r=['qez', 'hbf'], w=['pb2'])
                                kb.op('pe', REC.matmul(PB[2][:, 128 * hh:128 * hh + 128], lhsT=qez[:, 1, hh, :], rhs=hbf1[:, hh, :], start=False, stop=True),
                                      r=['qez', 'hbf1'], w=['pb2'])
                            upd(1)
                            kb.op('act', REC.copy(out=hbf[:], in_=hst[:]), r=['hst'], w=['hbf'])
                            og4 = og[:].rearrange("p (h v) -> p h v", v=128)
                            kb.op('act', REC.activation(out=og[:], in_=PB[2][:, :], func=AF.Square), r=['pb2'], w=['og'])
                            kb.op('dve', REC.tensor_reduce(out=sm8[:, 0:4], in_=og4, axis=AX.X, op=ALU.add), r=['og'], w=['sm8a'])
                            kb.op('dve', REC.tensor_scalar(out=sm8[:, 0:4], in0=sm8[:, 0:4], scalar1=1.0 / 128, scalar2=EPS, op0=ALU.mult, op1=ALU.add), r=['sm8a'], w=['sm8a'])
                            kb.op('act', REC.activation(out=sm8[:, 4:8], in_=sm8[:, 0:4], func=AF.Sqrt), r=['sm8a'], w=['sm8b'])
                            kb.op('dve', REC.reciprocal(out=sm8[:, 8:12], in_=sm8[:, 4:8]), r=['sm8b'], w=['sm8c'])
                            kb.op('dve', REC.tensor_tensor(out=og4, in0=PB[2][:, :].rearrange("p (h v) -> p h v", v=128),
                                                           in1=sm8[:, 8:12].unsqueeze(2).to_broadcast([128, 4, 128]), op=ALU.mult), r=['pb2', 'sm8c', 'og'], w=['og'])
                            kb.op('dve', REC.tensor_tensor(out=og[:], in0=og[:], in1=g_gla, op=ALU.mult), r=['og', 'vrep'], w=['og'])
                            kb.op('dve', REC.tensor_tensor(out=ogb[:], in0=og[:], in1=sgr[:], op=ALU.mult), r=['og', 'sgr'], w=['ogb'])
                        else:
                            upd(0)
                            upd(1)
                        if need_o:
                            kb.dma('sp', REC.dma_start(out=oscr[(n - HALO) * 128:(n - HALO + 1) * 128, 0:512], in_=ogb[:]),
                                   r=['ogb'], w=['oscr'])
                    if n < HALO:
                        continue
                    proj(1, 256, lambda k: w_in_sb[:, k, OFF['nq'] + 256 * g:OFF['nq'] + 256 * g + 256], r=())
                    for k in range(8):
                        kb.op('pe', REC.matmul(PB[1][:, 256:268], lhsT=xnT[:, k, :], rhs=w_in_sb[:, k, OFF['ng'] + 12 * g:OFF['ng'] + 12 * g + 12],
                                                            start=(k == 0), stop=(k == 7)), r=['xnT', 'w_in_sb'], w=['pb1'])
                    kb.op('dve', REC.tensor_tensor(out=gts[:], in0=PB[1][:, 256:268], in1=b_ng[:, 12 * g:12 * g + 12], op=ALU.add),
                          r=['pb1', 'vrep'], w=['gts'])
                    kb.op('act', REC.activation(out=gts[:], in_=gts[:], func=AF.Sigmoid), r=['gts'], w=['gts'])
                    pq = PB[1][:, 0:256].rearrange("p (h c) -> p h c", c=64)
                    rope(qf[:], pq, 4, cst_, qtmp, 0.125, 'pb1', 'qf', 'qtmp')
                    dump('qf', qf[:].rearrange('p a b -> p (a b)'), ['qf'], g, n)
                    dump('gts', gts[:], ['gts'], g, n)
                    kb.op('act', REC.copy(out=qpair[:, :, 0:64], in_=qf[:]), r=['qf'], w=['qpair'])
                    for h in range(4):
                        kb.op('pe', REC.transpose(out=pbf(4)[0:64, 512 + h * 128:512 + (h + 1) * 128], in_=qpair[:, h, 0:64], identity=identb[:]),
                              r=['qpair', 'identb'], w=['pb4'])
                    kb.op('act', REC.copy(out=qT[0:64, :, :].rearrange("p a b -> p (a b)"), in_=pbf(4)[0:64, 512:1024]), r=['pb4'], w=['qT'])
                    for t in range(NBT):
                        kb.op('pe', REC.transpose(out=pbf(4)[0:BT, t * 64:(t + 1) * 64], in_=vcT[:, t * BT:(t + 1) * BT], identity=identb[0:64, 0:64]),
                              r=['vcT', 'identb'], w=['pb4'])
                    kb.op('act', REC.copy(out=vcs[:].rearrange("p a b -> p (a b)"), in_=pbf(4)[0:BT, 0:NBT * 64]), r=['pb4'], w=['vcs'])
                    hb_ = 512 // NB if NB <= 512 else 1
                    for h in range(4):
                        bank = 5 + (h // hb_) if hb_ < 4 else 5
                        col = (h % hb_) * NB
                        kb.op('pe', REC.matmul(PB[bank][:, col:col + NB], lhsT=qT[0:64, h, :], rhs=kcT[:, :], start=True, stop=True),
                              r=['qT', 'kcT'], w=['pb%d' % bank])
                    kb.op('dve', REC.tensor_scalar(out=dd[:], in0=D0, scalar1=float(128 * n), scalar2=None, op0=ALU.add), r=['CM'], w=['dd'])
                    kb.op('dve', REC.scalar_tensor_tensor(out=cb[:], in0=dd[:], scalar=63.0, in1=valid, op0=ALU.is_ge, op1=ALU.mult), r=['dd', 'CM'], w=['cb'])
                    kb.op('dve', REC.tensor_scalar(out=cb[:], in0=cb[:], scalar1=-1.0, scalar2=-NEG, op0=ALU.add, op1=ALU.mult), r=['cb'], w=['cb'])
                    for h in range(4):
                        bank = 5 + (h // hb_) if hb_ < 4 else 5
                        col = (h % hb_) * NB
                        kb.op('dve', REC.tensor_tensor(out=smx[:, h, :], in0=PB[bank][:, col:col + NB], in1=cb[:], op=ALU.add),
                              r=['pb%d' % bank, 'cb'], w=['smx'])
                    kb.op('dve', REC.tensor_reduce(out=rc[:, 0:4], in_=smx[:], axis=AX.X, op=ALU.max), r=['smx'], w=['rc0'])
                    kb.op('dve', REC.tensor_scalar(out=rc[:, 4:8], in0=rc[:, 0:4], scalar1=-1000.0, scalar2=-1.0, op0=ALU.max, op1=ALU.mult), r=['rc0'], w=['rc1'])
                    for h in range(4):
                        kb.op('act', REC.activation(out=smx[:, h, :], in_=smx[:, h, :], func=AF.Exp, bias=rc[:, 4 + h:5 + h], accum_out=rc[:, 8 + h:9 + h]),
                              r=['smx', 'rc1'], w=['smx', 'rc2'])
                    kb.op('dve', REC.tensor_scalar(out=rc[:, 8:12], in0=rc[:, 8:12], scalar1=1e-30, scalar2=None, op0=ALU.max), r=['rc2'], w=['rc2'])
                    kb.op('dve', REC.reciprocal(out=rc[:, 12:16], in_=rc[:, 8:12]), r=['rc2'], w=['rc3'])
                    kb.op('dve', REC.tensor_tensor(out=smx[:], in0=smx[:], in1=rc[:, 12:16].unsqueeze(2).to_broadcast([128, 4, NB]), op=ALU.mult),
                          r=['smx', 'rc3'], w=['smx'])
                    dump('cb', cb[:], ['cb'], g, n)
                    dump('p', smx[:].rearrange('p a b -> p (a b)'), ['smx'], g, n)
                    kb.op('act', REC.copy(out=pcb, in_=smx[:]), r=['smx'], w=['PT0', 'PT1'])
                    kb.op('dve', REC.tensor_reduce(out=sc[:], in_=smx[:].rearrange("p h b -> p b h"), axis=AX.X, op=ALU.add), r=['smx'], w=['sc'])
                    for h in range(4):
                        for t in range(NBT):
                            kb.op('pe', REC.transpose(out=pbf(4)[0:BT, (h * NBT + t) * 128:(h * NBT + t + 1) * 128],
                                                                        in_=pcb[:, h, t * BT:(t + 1) * BT], identity=identb[:]),
                                  r=['PT0', 'PT1', 'identb'], w=['pb4'])
                    kb.op('act', REC.copy(out=pT[:].rearrange("p a b c -> p (a b c)"), in_=pbf(4)[0:BT, 0:4 * NBT * 128]), r=['pb4'], w=['pT'])
                    for h in range(4):
                        for t in range(NBT):
                            kb.op('pe', REC.matmul(PB[0][:, h * 64:(h + 1) * 64], lhsT=pT[:, h, t, :], rhs=vcs[:, t, :],
                                                                     start=(t == 0), stop=(t == NBT - 1)), r=['pT', 'vcs'], w=['pb0'])
                    kb.op('act', REC.copy(out=ocm[:].rearrange("p a b -> p (a b)"), in_=PB[0][:, 0:256]), r=['pb0'], w=['ocm'])
                    dump('ocmp', ocm[:].rearrange('p a b -> p (a b)'), ['ocm'], g, n)
                    dump('score0', sc[:], ['sc'], g, n)
                    kb.op('dve', REC.tensor_scalar(out=t1[:], in0=dd[:], scalar1=0.0, scalar2=None, op0=ALU.is_ge), r=['dd'], w=['t1'])
                    kb.op('dve', REC.scalar_tensor_tensor(out=t2[:], in0=dd[:], scalar=128.0, in1=t1[:], op0=ALU.is_lt, op1=ALU.mult), r=['dd', 't1'], w=['cb'])
                    kb.op('dve', REC.tensor_tensor(out=t2[:], in0=t2[:], in1=first, op=ALU.max), r=['cb', 'CM'], w=['cb'])
                    kb.op('dve', REC.tensor_tensor(out=t1[:], in0=t1[:], in1=valid, op=ALU.mult), r=['t1', 'CM'], w=['t1'])
                    kb.op('dve', REC.scalar_tensor_tensor(out=sc[:], in0=t2[:], scalar=5.0, in1=sc[:], op0=ALU.mult, op1=ALU.max), r=['cb', 'sc'], w=['sc'])
                    kb.op('dve', REC.scalar_tensor_tensor(out=sc[:], in0=sc[:], scalar=1.0, in1=t1[:], op0=ALU.add, op1=ALU.mult), r=['sc', 't1'], w=['sc'])
                    kb.op('dve', REC.tensor_scalar(out=sc[:], in0=sc[:], scalar1=-1.0, scalar2=None, op0=ALU.add), r=['sc'], w=['sc'])
                    kb.op('dve', REC.max(out=m8[:, 0:8], in_=sc[:]), r=['sc'], w=['m8a'])
                    kb.op('dve', REC.match_replace(out=sc2[:], in_to_replace=m8[:, 0:8], in_values=sc[:], imm_value=-2.0), r=['sc', 'm8a'], w=['sc2'])
                    kb.op('dve', REC.max(out=m8[:, 8:16], in_=sc2[:]), r=['sc2'], w=['m8b'])
                    kb.op('dve', REC.scalar_tensor_tensor(out=selb[:], in0=sc[:], scalar=m8[:, 15:16], in1=t1[:], op0=ALU.is_ge, op1=ALU.mult),
                          r=['sc', 'm8b', 't1'], w=['selb'])
                    kb.op('dve', REC.tensor_scalar(out=selb[:], in0=selb[:], scalar1=-1.0, scalar2=-NEG, op0=ALU.add, op1=ALU.mult), r=['selb'], w=['selb'])
                    dump('score', sc[:], ['sc'], g, n)
                    dump('m8', m8[:], ['m8a', 'm8b'], g, n)
                    dump('selb', selb[:], ['selb'], g, n)
                    nm_need = (2 * n + 1) // 64 + 1
                    for m in range(nm_need):
                        wdt = min(64, NB)
                        kb.op('dve', REC.tensor_copy(out=qpair[:, :, 64:64 + wdt],
                                                                           in_=selb[:, m * 64:m * 64 + wdt].unsqueeze(1).to_broadcast([128, 4, wdt])),
                              r=['selb', 'qpair'], w=['qpair'])
                        for h in range(4):
                            kb.op('pe', REC.transpose(out=pbf(4)[:, h * 128:(h + 1) * 128], in_=qpair[:, h, :], identity=identb[:]),
                                  r=['qpair', 'identb'], w=['pb4'])
                        kb.op('act', REC.copy(out=rhsm[:, m, :], in_=pbf(4)[:, 0:512]), r=['pb4'], w=['rhsm'])
                    tri_flat = tri4[:].rearrange("p a b -> p (a b)")
                    qT_flat = qT[:].rearrange("p a b -> p (a b)")
                    jobs = []
                    for j in range(n + 1):
                        jobs.append(dict(l=ksT[:, j * 128:(j + 1) * 128], r=rhsm[:, (2 * j) // 64, :], rr=['ksT', 'rhsm'],
                                         bias=(tri_flat, 'tri4') if j == n else None, v=vsx[:, j, :], vr='vsx', acc=2, first=(j == 0), last=(j == n)))
                    for t in range(5):
                        j = n - 4 + t
                        jobs.append(dict(l=kwT[:, (j - W0) * 128:(j - W0 + 1) * 128], r=qT_flat, rr=['kwT', 'qT'],
                                         bias=(wb4[:, t, :, :].rearrange("p a b -> p (a b)"), 'wb4'), v=vwx[:, j - W0, :], vr='vwx', acc=3, first=(t == 0), last=(t == 4)))
                    NBUF = len(SBANKS)
                    LOOK = NBUF - 1
                    for idx in range(len(jobs) + LOOK):
                        if idx < len(jobs):
                            jb = jobs[idx]
                            bank = SBANKS[idx % NBUF]
                            breg = 'pb%d' % bank
                            kb.op('pe', REC.matmul(PB[bank][:, :], lhsT=jb['l'], rhs=jb['r'], start=True, stop=(jb['bias'] is None)), r=jb['rr'], w=[breg])
                            if jb['bias'] is not None:
                                kb.op('pe', REC.matmul(PB[bank][:, :], lhsT=identb[:], rhs=jb['bias'][0], start=False, stop=True), r=['identb', jb['bias'][1]], w=[breg])
                        k_ = idx - LOOK
                        if k_ >= 0:
                            jb = jobs[k_]
                            bank = SBANKS[k_ % NBUF]
                            breg = 'pb%d' % bank
                            preg = 'PT%d' % (k_ % NBUF)
                            kb.op('act', REC.activation(out=PT[k_ % NBUF], in_=PB[bank][:, :], func=AF.Exp), r=[breg], w=[preg])
                            kb.op('pe', REC.matmul(PB[jb['acc']][0:65, :], lhsT=jb['v'], rhs=PT[k_ % NBUF], start=jb['first'], stop=jb['last']),
                                  r=[jb['vr'], preg], w=['pb%d' % jb['acc']])
                    kb.op('act', REC.copy(out=oT[:, 0, :], in_=PB[2][0:65, :]), r=['pb2'], w=['oT0'])
                    kb.op('act', REC.copy(out=oT[:, 1, :], in_=PB[3][0:65, :]), r=['pb3'], w=['oT1'])
                    dump('oT0', oT[:, 0, :], ['oT0'], g, n)
                    dump('oT1', oT[:, 1, :], ['oT1'], g, n)
                    for br in range(2):
                        for h in range(4):
                            kb.op('pe', REC.transpose(out=PB[5 + br][:, h * 65:(h + 1) * 65], in_=oT[:, br, h * 128:(h + 1) * 128],
                                                                          identity=identf[0:65, 0:65]), r=['oT%d' % br, 'identf'], w=['pb%d' % (5 + br)])
                    gv = gts[:].rearrange("p (h c) -> p h c", c=3)
                    for br in range(2):
                        pv = PB[5 + br][:, 0:260].rearrange("p (h c) -> p h c", c=65)
                        kb.op('dve', REC.tensor_scalar(out=rc[:, 4 * br:4 * br + 4].unsqueeze(2), in0=pv[:, :, 64:65], scalar1=1e-30, scalar2=None,
                                                                             op0=ALU.max), r=['pb%d' % (5 + br)], w=['rcf%d' % br])
                        kb.op('dve', REC.reciprocal(out=rc[:, 4 * br:4 * br + 4], in_=rc[:, 4 * br:4 * br + 4]), r=['rcf%d' % br], w=['rcf%d' % br])
                        kb.op('dve', REC.tensor_tensor(out=rc[:, 4 * br:4 * br + 4].unsqueeze(2), in0=rc[:, 4 * br:4 * br + 4].unsqueeze(2),
                                                                      in1=gv[:, :, 1 + br:2 + br], op=ALU.mult), r=['rcf%d' % br, 'gts'], w=['rcf%d' % br])
                    kb.op('dve', REC.tensor_tensor(out=ocm[:], in0=ocm[:], in1=gv[:, :, 0:1].to_broadcast([128, 4, 64]), op=ALU.mult), r=['ocm', 'gts'], w=['ocm'])
                    for br in range(2):
                        pv = PB[5 + br][:, 0:260].rearrange("p (h c) -> p h c", c=65)
                        kb.op('dve', REC.tensor_tensor(out=qf[:], in0=pv[:, :, 0:64],
                                                                             in1=rc[:, 4 * br:4 * br + 4].unsqueeze(2).to_broadcast([128, 4, 64]), op=ALU.mult),
                              r=['pb%d' % (5 + br), 'rcf%d' % br, 'qf'], w=['qf'])
                        kb.op('dve', REC.tensor_tensor(out=ocm[:], in0=ocm[:], in1=qf[:], op=ALU.add), r=['ocm', 'qf'], w=['ocm'])
                    dump('ofin', ocm[:].rearrange('p a b -> p (a b)'), ['ocm'], g, n)
                    kb.op('act', REC.copy(out=onb[:, 0:256], in_=ocm[:].rearrange("p a b -> p (a b)")), r=['ocm'], w=['onb'])
                    kb.dma('sp', REC.dma_start(out=oscr[(n - HALO) * 128:(n - HALO + 1) * 128, 512 + 256 * g:768 + 256 * g], in_=onb[:, 0:256]),
                           r=['onb'], w=['oscr'])
            kb.dma('sp', REC.dma_start(out=gl_o.rearrange("h k v -> k h v"), in_=hst[:]), r=['hst'], w=['gl_o'])

        kb.alias = {}
        CMst.close()
        kb.barrier()

        if do_sample:
          kb.barrier()
          with contextlib.ExitStack() as stS:
            cS = sb(stS, "cS", [NS, 16 + 64 * 16 + 128 * NREP + 16 + 128 + NPG + 16], F32)
            kb.dma('sp', REC.dma_start(out=cS[:], in_=scs_i), w=['cS'])
            csS = cS[:, 0:16]
            o_ = 16
            selS = cS[:, o_:o_ + 1024].rearrange("p (s m) -> p s m", m=64)
            o_ += 1024
            RepAll = cS[:, o_:o_ + 128 * NREP].rearrange("p (r m) -> p r m", m=128)
            o_ += 128 * NREP
            id16 = cS[:, o_:o_ + 16]
            o_ += 16
            o_ += 128
            ptabs = cS[:, o_:o_ + NPG]
            o_ += NPG
            slotw = cS[:, o_:o_ + 16]
            cR = sb(stS, "cR", [128, NREP * 16 + 256 + 64 + 4 * 64 + 1], F32)
            kb.dma('sp', REC.dma_start(out=cR[:], in_=rep_i), w=['cR'])
            BAll = cR[:, 0:NREP * 16].rearrange("p (r m) -> p r m", m=16)
            eye16 = cR[0:64, NREP * 16:NREP * 16 + 256].rearrange("p (a b) -> p a b", b=16)
            maskW = cR[:, NREP * 16 + 256:NREP * 16 + 320]
            wrow = cR[:, NREP * 16 + 320:NREP * 16 + 576].rearrange("p (t c g) -> p t c g", t=2, g=2)
            slotm = cR[:, NREP * 16 + 576:NREP * 16 + 577]
            vrS = sb(stS, "vrS", [128, 1560], F32)
            kb.dma('sp', REC.dma_start(out=vrS[:], in_=vecsA.partition_broadcast(128)), w=['vrS'])
            zs = sb(stS, "zs", [NS, DIN], F32)
            qs = sb(stS, "qs", [NS, 8, 64], F32)
            gS = sb(stS, "gS", [NS, 24], F32)
            oall = sb(stS, "oall", [NS, 4, 8, 64], F32)
            ogs = sb(stS, "ogs", [NS, 512], F32)
            smS = sb(stS, "smS", [128, 64], F32)
            tmpS = sb(stS, "tmpS", [128, 8192], F32)
            scS = sb(stS, "scS", [NS, 2, NBP], F32)
            with contextlib.ExitStack() as stSa:
                w_in_sb = sb(stSa, "w_in_sbS", [128, 8, DIN], BF16)
                for k in range(8):
                    for c0 in (0, 1428):
                        kb.dma('pool', REC.dma_start(out=w_in_sb[:, k, c0:c0 + 1428], in_=w_in[k * 128:(k + 1) * 128, c0:c0 + 1428]), w=['w_in_sbS'])
                wgg_sb = sb(stSa, "wgg_sbS", [17, 256], BF16)
                kb.dma('pool', REC.dma_start(out=wgg_sb[:], in_=wgg), w=['wgg_sbS'])
                xsb = sb(stSa, "xsb", [NS, D], F32)
                xnS = sb(stSa, "xnS", [NS, D], BF16)
                xnTS = sb(stSa, "xnTS", [128, 8, NS], BF16)
                junkS = tmpS
                kb.dma('sp', REC.dma_start(out=xsb[:], in_=xs_i), w=['xsb'])
                kb.op('act', REC.activation(out=junkS[0:NS, 0:D], in_=xsb[:], func=AF.Square, accum_out=smS[0:NS, 0:1]), r=['xsb'], w=['tmpS', 'smS0'])
                kb.op('dve', REC.tensor_scalar(out=smS[0:NS, 1:2], in0=smS[0:NS, 0:1], scalar1=1.0 / D, scalar2=EPS, op0=ALU.mult, op1=ALU.add), r=['smS0'], w=['smS1'])
                kb.op('act', REC.activation(out=smS[0:NS, 2:3], in_=smS[0:NS, 1:2], func=AF.Sqrt), r=['smS1'], w=['smS2'])
                kb.op('dve', REC.reciprocal(out=smS[0:NS, 3:4], in_=smS[0:NS, 2:3]), r=['smS2'], w=['smS3'])
                kb.op('dve', REC.scalar_tensor_tensor(out=xnS[:], in0=xsb[:], scalar=smS[0:NS, 3:4], in1=vrS[0:NS, 0:1024], op0=ALU.mult, op1=ALU.mult),
                      r=['xsb', 'smS3', 'vrS'], w=['xnS'])
                for k in range(8):
                    kb.op('pe', REC.transpose(out=pbf(4)[:, k * NS:(k + 1) * NS], in_=xnS[:, k * 128:(k + 1) * 128], identity=identb[0:NS, 0:NS]),
                          r=['xnS', 'identb'], w=['pb4'])
                kb.op('act', REC.copy(out=xnTS[:].rearrange("p a b -> p (a b)"), in_=pbf(4)[:, 0:8 * NS]), r=['pb4'], w=['xnTS'])
                for ci, c0 in enumerate(range(0, DIN, 512)):
                    c1 = min(DIN, c0 + 512)
                    bank = ci % 2
                    for k in range(8):
                        kb.op('pe', REC.matmul(PB[bank][0:NS, 0:c1 - c0], lhsT=xnTS[:, k, :], rhs=w_in_sb[:, k, c0:c1], start=(k == 0), stop=(k == 7)),
                              r=['xnTS', 'w_in_sbS'], w=['pb%d' % bank])
                    kb.op('act', REC.copy(out=zs[:, c0:c1], in_=PB[bank][0:NS, 0:c1 - c0]), r=['pb%d' % bank], w=['zs'])

                def ropeS(dst, src, nh, scale):
                    c = csS[:, 0:8].unsqueeze(1).to_broadcast([NS, nh, 8])
                    sn = csS[:, 8:16].unsqueeze(1).to_broadcast([NS, nh, 8])
                    tmp = tmpS[0:NS, 0:nh * 32].rearrange("p (h a b) -> p h a b", a=4, b=8)
                    x1 = src[:, :, 0:8]
                    x2 = src[:, :, 8:16]
                    kb.op('dve', REC.scalar_tensor_tensor(out=tmp[:, :, 0, :], in0=x1, scalar=scale, in1=c, op0=ALU.mult, op1=ALU.mult), r=['zs', 'cS'], w=['tmpS'])
                    kb.op('dve', REC.scalar_tensor_tensor(out=tmp[:, :, 1, :], in0=x2, scalar=scale, in1=sn, op0=ALU.mult, op1=ALU.mult), r=['zs', 'cS'], w=['tmpS'])
                    kb.op('dve', REC.scalar_tensor_tensor(out=tmp[:, :, 2, :], in0=x2, scalar=scale, in1=c, op0=ALU.mult, op1=ALU.mult), r=['zs', 'cS'], w=['tmpS'])
                    kb.op('dve', REC.scalar_tensor_tensor(out=tmp[:, :, 3, :], in0=x1, scalar=scale, in1=sn, op0=ALU.mult, op1=ALU.mult), r=['zs', 'cS'], w=['tmpS'])
                    kb.op('dve', REC.tensor_scalar(out=dst[:, :, 16:64], in0=src[:, :, 16:64], scalar1=scale, scalar2=None, op0=ALU.mult), r=['zs'], w=['zs', 'qs'])
                    kb.op('dve', REC.tensor_tensor(out=dst[:, :, 0:8], in0=tmp[:, :, 0, :], in1=tmp[:, :, 1, :], op=ALU.subtract), r=['tmpS'], w=['zs', 'qs'])
                    kb.op('dve', REC.tensor_tensor(out=dst[:, :, 8:16], in0=tmp[:, :, 2, :], in1=tmp[:, :, 3, :], op=ALU.add), r=['tmpS'], w=['zs', 'qs'])
                ropeS(qs[:], zs[:, OFF['nq']:OFF['nq'] + 512].rearrange("p (h c) -> p h c", c=64), 8, 0.125)
                for nm in ('kc', 'ks', 'kw'):
                    v_ = zs[:, OFF[nm]:OFF[nm] + 128].rearrange("p (h c) -> p h c", c=64)
                    ropeS(v_, v_, 2, 1.0)
                kb.dma('sp', REC.dma_start(out=skv_o, in_=zs[:, OFF['kc']:OFF['kc'] + 768]), r=['zs'], w=['skv_o'])
                kb.op('dve', REC.tensor_tensor(out=gS[:], in0=zs[:, OFF['ng']:OFF['ng'] + 24], in1=vrS[0:NS, 1536:1560], op=ALU.add), r=['zs', 'vrS'], w=['gS'])
                kb.op('act', REC.activation(out=gS[:], in_=gS[:], func=AF.Sigmoid), r=['gS'], w=['gS'])
                kb.dma('sp', REC.dma_start(out=swk_o[:, 0:511, :], in_=wink_i[:, 1:512, :]), w=['swk_o'])
                kb.dma('sp', REC.dma_start(out=swv_o[:, 0:511, :], in_=winv_i[:, 1:512, :]), w=['swv_o'])
                kb.dma('sp', REC.dma_start(out=swk_o[:, 511, :], in_=zs[:, OFF['kw']:OFF['kw'] + 128]), r=['zs'], w=['swk_o'])
                kb.dma('sp', REC.dma_start(out=swv_o[:, 511, :], in_=zs[:, OFF['vw']:OFF['vw'] + 128]), r=['zs'], w=['swv_o'])
                glrS = sb(stSa, "glrS", [32, NS], BF16)
                kb.op('pool', REC.memset(glrS[:], 1.0), w=['glrS'])
                kb.op('pe', REC.transpose(out=PB[6][0:16, 0:NS], in_=zs[:, OFF['glr']:OFF['glr'] + 16], identity=identf[0:NS, 0:NS]), r=['zs', 'identf'], w=['pb6'])
                kb.op('dve', REC.tensor_copy(out=glrS[0:16, :], in_=PB[6][0:16, 0:NS]), r=['pb6'], w=['glrS'])
                kb.op('pe', REC.matmul(PB[6][0:NS, 256:512], lhsT=glrS[0:17, :], rhs=wgg_sb[0:17, :], start=True, stop=True), r=['glrS', 'wgg_sbS'], w=['pb6'])
                aS = sb(stSa, "aS", [NS, 256], F32)
                kb.op('act', REC.activation(out=aS[:], in_=PB[6][0:NS, 256:512], func=AF.Exp, scale=-1.0), r=['pb6'], w=['aS'])
                kb.op('act', REC.activation(out=aS[:], in_=aS[:], func=AF.Ln, bias=1.0), r=['aS'], w=['aS'])
                kb.op('act', REC.activation(out=aS[:], in_=aS[:], func=AF.Exp, scale=-1.0 / 16), r=['aS'], w=['aS'])
                gT = sb(stSa, "gT", [64, 3, 4, NS], F32)
                for wi, src in enumerate((aS[:], zs[:, OFF['gk']:OFF['gk'] + 256], zs[:, OFF['gq']:OFF['gq'] + 256])):
                    for h in range(4):
                        kb.op('pe', REC.transpose(out=PB[7][0:64, (wi * 4 + h) * NS:(wi * 4 + h + 1) * NS], in_=src[:, 64 * h:64 * h + 64], identity=identf[0:NS, 0:NS]),
                              r=['aS', 'zs', 'identf'], w=['pb7'])
                kb.op('act', REC.copy(out=gT[:].rearrange("p a b c -> p (a b c)"), in_=PB[7][0:64, 0:12 * NS]), r=['pb7'], w=['gT'])
                kb.op('dve', REC.tensor_scalar(out=gT[:, 2, :, :], in0=gT[:, 2, :, :], scalar1=0.125, scalar2=None, op0=ALU.mult), r=['gT'], w=['gT'])
                hS = sb(stSa, "hS", [64, NS * 4, 128], F32)
                kb.dma('sp', REC.dma_start(out=hS[:], in_=sgla_i.rearrange("s h k v -> k (s h) v")), w=['hS'])
                kvt = sb(stSa, "kvt", [64, 128], F32)
                for s_ in range(NS):
                    kb.op('pe', REC.matmul(PB[s_ % 2][0:64, :], lhsT=selS[:, s_, :], rhs=zs[:, OFF['gv']:OFF['gv'] + 512], start=True, stop=True),
                          r=['cS', 'zs'], w=['pb%d' % (s_ % 2)])
                    for h in range(4):
                        kb.op('dve', REC.tensor_scalar(out=kvt[:], in0=PB[s_ % 2][0:64, 128 * h:128 * h + 128], scalar1=gT[:, 1, h, s_:s_ + 1], scalar2=None, op0=ALU.mult),
                              r=['pb%d' % (s_ % 2), 'gT'], w=['kvt'])
                        kb.op('dve', REC.scalar_tensor_tensor(out=hS[:, s_ * 4 + h, :], in0=hS[:, s_ * 4 + h, :], scalar=gT[:, 0, h, s_:s_ + 1], in1=kvt[:],
                                                              op0=ALU.mult, op1=ALU.add), r=['hS', 'gT', 'kvt'], w=['hS'])
                kb.dma('sp', REC.dma_start(out=sgl_o.rearrange("s h k v -> k (s h) v"), in_=hS[:]), r=['hS'], w=['sgl_o'])
                QM = sb(stSa, "QM", [64, 4, NS, 16], F32)
                kb.op('dve', REC.tensor_tensor(out=QM[:], in0=gT[:, 2, :, :].unsqueeze(3).to_broadcast([64, 4, NS, 16]),
                                               in1=eye16.unsqueeze(1).to_broadcast([64, 4, NS, 16]), op=ALU.mult), r=['gT', 'cR'], w=['QM'])
                for h in range(4):
                    for s_ in range(NS):
                        kb.op('pe', REC.matmul(PB[3][0:NS, 128 * h:128 * h + 128], lhsT=QM[:, h, s_, :], rhs=hS[:, s_ * 4 + h, :], start=(s_ == 0), stop=(s_ == NS - 1)),
                              r=['QM', 'hS'], w=['pb3'])
                og4 = ogs[:].rearrange("p (h v) -> p h v", v=128)
                kb.op('act', REC.copy(out=ogs[:], in_=PB[3][0:NS, :]), r=['pb3'], w=['ogs'])
                sq = tmpS[0:NS, 0:512].rearrange("p (h v) -> p h v", v=128)
                kb.op('dve', REC.tensor_tensor(out=sq, in0=og4, in1=og4, op=ALU.mult), r=['ogs'], w=['tmpS'])
                kb.op('dve', REC.tensor_reduce(out=smS[0:NS, 8:12], in_=sq, axis=AX.X, op=ALU.add), r=['tmpS'], w=['smS8'])
                kb.op('dve', REC.tensor_scalar(out=smS[0:NS, 8:12], in0=smS[0:NS, 8:12], scalar1=1.0 / 128, scalar2=EPS, op0=ALU.mult, op1=ALU.add), r=['smS8'], w=['smS8'])
                kb.op('act', REC.activation(out=smS[0:NS, 8:12], in_=smS[0:NS, 8:12], func=AF.Sqrt), r=['smS8'], w=['smS8'])
                kb.op('dve', REC.reciprocal(out=smS[0:NS, 12:16], in_=smS[0:NS, 8:12]), r=['smS8'], w=['smS12'])
                kb.op('dve', REC.tensor_tensor(out=og4, in0=og4, in1=smS[0:NS, 12:16].unsqueeze(2).to_broadcast([NS, 4, 128]), op=ALU.mult), r=['ogs', 'smS12'], w=['ogs'])
                kb.op('dve', REC.tensor_tensor(out=ogs[:], in0=ogs[:], in1=vrS[0:NS, 1024:1536], op=ALU.mult), r=['ogs', 'vrS'], w=['ogs'])
                kb.op('act', REC.activation(out=tmpS[0:NS, 512:1024], in_=zs[:, OFF['gr']:OFF['gr'] + 512], func=AF.Silu), r=['zs'], w=['tmpS'])
                kb.op('dve', REC.tensor_tensor(out=ogs[:], in0=ogs[:], in1=tmpS[0:NS, 512:1024], op=ALU.mult), r=['ogs', 'tmpS'], w=['ogs'])
            kb.barrier()
            with contextlib.ExitStack() as stSa:
                sct = sb(stSa, "sct", [NS, 2 * 2 * DFF], F32)
                scT = sb(stSa, "scT", [128, 88, NS], F32)
                kb.dma('sp', REC.dma_start(out=sct[:], in_=sconv_i), w=['sct'])
                for c8 in range(0, 88, 32):
                    nn_ = min(32, 88 - c8)
                    for u_ in range(nn_):
                        kb.op('pe', REC.transpose(out=PB[5][:, u_ * NS:(u_ + 1) * NS], in_=sct[:, (c8 + u_) * 128:(c8 + u_ + 1) * 128], identity=identf[0:NS, 0:NS]),
                              r=['sct', 'identf'], w=['pb5'])
                    kb.op('act', REC.copy(out=scT[:, c8:c8 + nn_, :].rearrange("p a b -> p (a b)"), in_=PB[5][:, 0:nn_ * NS]), r=['pb5'], w=['scT'])
                kb.dma('sp', REC.dma_start(out=scv_o[:, 0, :], in_=sct[:, 2 * DFF:4 * DFF]), r=['sct'], w=['scv_o'])
                kb.dma('sp', REC.dma_start(out=sct_d, in_=scT[:].rearrange("p a b -> p (a b)")), r=['scT'], w=['sct_d'])
            kb.barrier()

            qsf = qs[:].rearrange("p h c -> p (h c)")

            def qrep(dst, ri):
                kb.op('pe', REC.matmul(PB[6][:, :], lhsT=RepAll[:, ri, :], rhs=qsf, start=True, stop=True), r=['cS', 'qs'], w=['pb6'])
                kb.op('act', REC.copy(out=dst[:], in_=PB[6][:, :]), r=['pb6'], w=['qr'])

            def newkey(br, kname, vname, acc_ps, accreg):
                kk = zs[:, OFF[kname]:OFF[kname] + 128].rearrange("p (g c) -> p g c", c=64).unsqueeze(2).to_broadcast([NS, 2, 4, 64])
                vv = zs[:, OFF[vname]:OFF[vname] + 128].rearrange("p (g c) -> p g c", c=64).unsqueeze(2).to_broadcast([NS, 2, 4, 64])
                t4 = tmpS[0:NS, 0:512].rearrange("p (g h c) -> p g h c", g=2, h=4)
                kb.op('dve', REC.tensor_tensor(out=t4, in0=qs[:].rearrange("p (g h) c -> p g h c", g=2), in1=kk, op=ALU.mult), r=['qs', 'zs'], w=['tmpS'])
                kb.op('dve', REC.tensor_reduce(out=smS[0:NS, 16:24], in_=tmpS[0:NS, 0:512].rearrange("p (a c) -> p a c", c=64), axis=AX.X, op=ALU.add), r=['tmpS'], w=['smS16'])
                kb.op('act', REC.activation(out=smS[0:NS, 16:24], in_=smS[0:NS, 16:24], func=AF.Exp), r=['smS16'], w=['smS16'])
                kb.op('dve', REC.tensor_tensor(out=smS[0:NS, 24:32], in0=smS[0:NS, 16:24], in1=acc_ps[:, 512:520], op=ALU.add), r=['smS16', accreg], w=['smS24'])
                kb.op('dve', REC.reciprocal(out=smS[0:NS, 24:32], in_=smS[0:NS, 24:32]), r=['smS24'], w=['smS24'])
                kb.op('dve', REC.tensor_tensor(out=t4, in0=vv, in1=smS[0:NS, 16:24].rearrange("p (g h) -> p g h", g=2).unsqueeze(3).to_broadcast([NS, 2, 4, 64]), op=ALU.mult),
                      r=['zs', 'smS16'], w=['tmpS'])
                kb.op('dve', REC.tensor_tensor(out=tmpS[0:NS, 0:512], in0=tmpS[0:NS, 0:512], in1=acc_ps[:, 0:512], op=ALU.add), r=['tmpS', accreg], w=['tmpS'])
                kb.op('dve', REC.tensor_tensor(out=oall[:, br, :, :], in0=tmpS[0:NS, 0:512].rearrange("p (a c) -> p a c", c=64),
                                               in1=smS[0:NS, 24:32].unsqueeze(2).to_broadcast([NS, 8, 64]), op=ALU.mult), r=['tmpS', 'smS24'], w=['oall%d' % br])

            def attend(kbuf, vbuf, qr, nrow, rowstride_g, mask, bsel_list, acc_bank, first, last, regs, ghs=range(8)):
                e = tmpS[:, 0:8 * nrow].rearrange("p (a r) -> p a r", r=nrow)
                pr = tmpS[:, 4096:4096 + nrow * 64]
                for gh in ghs:
                    g_ = gh // 4
                    kb.op('dve', REC.tensor_tensor(out=pr.rearrange("p (r c) -> p r c", c=64), in0=kbuf[:, :, 64 * g_:64 * g_ + 64],
                                                   in1=qr[:, 64 * gh:64 * gh + 64].unsqueeze(1).to_broadcast([128, nrow, 64]), op=ALU.mult), r=regs + ['qr'], w=['tmpS'])
                    kb.op('dve', REC.tensor_reduce(out=e[:, gh, :], in_=pr.rearrange("p (r c) -> p r c", c=64), axis=AX.X, op=ALU.add), r=['tmpS'], w=['tmpS'])
                g0_, g1_ = ghs[0], ghs[-1] + 1
                esl = e[:, g0_:g1_, :]
                if mask is not None:
                    kb.op('dve', REC.tensor_tensor(out=esl, in0=esl, in1=mask.unsqueeze(1).to_broadcast([128, g1_ - g0_, nrow]), op=ALU.add), r=['tmpS', 'cR'], w=['tmpS'])
                kb.op('act', REC.activation(out=esl, in_=esl, func=AF.Exp), r=['tmpS'], w=['tmpS'])
                ov = sb_ov
                for gh in ghs:
                    g_ = gh // 4
                    kb.op('dve', REC.tensor_tensor(out=pr.rearrange("p (r c) -> p r c", c=64), in0=vbuf[:, :, 64 * g_:64 * g_ + 64],
                                                   in1=e[:, gh, :].unsqueeze(2).to_broadcast([128, nrow, 64]), op=ALU.mult), r=regs + ['tmpS'], w=['tmpS'])
                    kb.op('dve', REC.tensor_reduce(out=ov[:, 64 * gh:64 * gh + 64], in_=pr.rearrange("p (r c) -> p c r", c=64), axis=AX.X, op=ALU.add), r=['tmpS'], w=['ovS'])
                kb.op('dve', REC.tensor_reduce(out=ov[:, 512 + g0_:512 + g1_], in_=esl, axis=AX.X, op=ALU.add), r=['tmpS'], w=['ovS'])
                return ov

            sb_ov = sb(stS, "sb_ov", [128, 520], F32)
            kb.op('pool', REC.memset(sb_ov[:], 0.0), w=['ovS'])
            qr = sb(stS, "qr", [128, 512], F32)
            with contextlib.ExitStack() as stSb:
                wk = sb(stSb, "wk", [128, 64, 128], F32)
                wv = sb(stSb, "wv", [128, 64, 128], F32)
                kb.dma('sp', REC.dma_start(out=wk[:], in_=wink_i.rearrange("s (c r) f -> (s c) r f", c=8)), w=['wk'])
                kb.dma('sp', REC.dma_start(out=wv[:], in_=winv_i.rearrange("s (c r) f -> (s c) r f", c=8)), w=['wv'])
                qrep(qr, 0)
                ov = attend(wk, wv, qr, 64, None, maskW, None, 3, True, True, ['wk', 'wv'])
                kb.op('pe', REC.matmul(PB[3][0:NS, 0:512], lhsT=BAll[:, 0, :], rhs=ov[:, 0:512], start=True, stop=True), r=['ovS', 'cR'], w=['pb3'])
                kb.op('pe', REC.matmul(PB[2][0:NS, 0:8], lhsT=BAll[:, 0, :], rhs=ov[:, 512:520], start=True, stop=True), r=['ovS', 'cR'], w=['pb2'])
                accw = sb(stSb, "accw", [NS, 520], F32)
                kb.op('act', REC.copy(out=accw[:, 0:512], in_=PB[3][0:NS, 0:512]), r=['pb3'], w=['accw'])
                kb.op('act', REC.copy(out=accw[:, 512:520], in_=PB[2][0:NS, 0:8]), r=['pb2'], w=['accw'])
                newkey(2, 'kw', 'vw', accw, 'accw')
            kb.barrier()
            with contextlib.ExitStack() as stSc:
                ptab_sb = sb(stSc, "ptab_sb", [128, NGR], I32)
                idx4 = sb(stSc, "idx4", [128, NGR], I32)
                kb.dma('sp', REC.dma_start(out=ptab_sb[:], in_=ptab_i), w=['ptab_sb'])
                kb.op('dve', REC.tensor_scalar(out=idx4[:], in0=ptab_sb[:], scalar1=4.0, scalar2=None, op0=ALU.mult), r=['ptab_sb'], w=['idx4'])
                pg = [sb(stSc, "pg%d" % i, [128, 32, 2, 64], F32) for i in range(2)]
                kcP = sb(stSc, "kcP", [128, 2, 2, 128], F32)
                scP = sb(stSc, "scP", [128, 2, 8], F32)
                part = sb(stSc, "part", [128, 128], F32)
                it = 0
                for r_ in range(NGR):
                    qrep(qr, 1 + r_)
                    for ti in range(2):
                        pool_ap = pools[ti].ap().rearrange("n (c f) -> (n c) f", c=4)
                        for c in range(4):
                            b_ = pg[it % 2]
                            breg = 'pg%d' % (it % 2)
                            it += 1
                            kb.dma('pool', REC.indirect_dma_start(out=b_[:].rearrange("p a b c -> p (a b c)"), out_offset=None, in_=pool_ap, element_offset=c * 4096,
                                                                  in_offset=bass.IndirectOffsetOnAxis(ap=idx4[:, r_:r_ + 1], axis=0)), r=['idx4'], w=[breg])
                            m_ = c // 2
                            wsl = wrow[:, ti, (c % 2) * 32:(c % 2) * 32 + 32, :].unsqueeze(3).to_broadcast([128, 32, 2, 64])
                            pr4 = tmpS[:, 0:4096].rearrange("p (a b c) -> p a b c", b=2, c=64)
                            kb.op('dve', REC.tensor_tensor(out=pr4, in0=b_[:], in1=wsl, op=ALU.mult), r=[breg, 'cR'], w=['tmpS'])
                            dst = kcP[:, ti, m_, :] if c % 2 == 0 else part[:]
                            kb.op('dve', REC.tensor_reduce(out=dst, in_=tmpS[:, 0:4096].rearrange("p (a f) -> p f a", f=128), axis=AX.X, op=ALU.add), r=['tmpS'], w=['kcP' if c % 2 == 0 else 'part'])
                            if c % 2 == 1:
                                kb.op('dve', REC.tensor_tensor(out=kcP[:, ti, m_, :], in0=kcP[:, ti, m_, :], in1=part[:], op=ALU.add), r=['kcP', 'part'], w=['kcP'])
                    for gh in range(8):
                        g_ = gh // 4
                        pr3 = tmpS[:, 0:128].rearrange("p (m c) -> p m c", c=64)
                        kb.op('dve', REC.tensor_tensor(out=pr3, in0=kcP[:, 0, :, 64 * g_:64 * g_ + 64], in1=qr[:, 64 * gh:64 * gh + 64].unsqueeze(1).to_broadcast([128, 2, 64]), op=ALU.mult),
                              r=['kcP', 'qr'], w=['tmpS'])
                        kb.op('dve', REC.tensor_reduce(out=scP[:, :, gh], in_=pr3, axis=AX.X, op=ALU.add), r=['tmpS'], w=['scP'])
                    kb.dma('sp', REC.dma_start(out=scmp_d[r_ * 128:(r_ + 1) * 128, :], in_=scP[:].rearrange("p a b -> p (a b)")), r=['scP'], w=['scmp_d'])
                    kb.dma('sp', REC.dma_start(out=vc_d[r_ * 128:(r_ + 1) * 128, :], in_=kcP[:, 1, :, :].rearrange("p a b -> p (a b)")), r=['kcP'], w=['vc_d'])
                scm = sb(stSc, "scm", [NS, NPG * 16], F32)
                vcs_ = sb(stSc, "vcs_", [NS, NPG * 256], F32)
                kb.dma('sp', REC.dma_start(out=scm[:], in_=scmp_d.rearrange("(s p) f -> s (p f)", p=NPG)), r=['scmp_d'], w=['scm'])
                kb.dma('sp', REC.dma_start(out=vcs_[:], in_=vc_d.rearrange("(s p) f -> s (p f)", p=NPG)), r=['vc_d'], w=['vcs_'])
                sview = scm[:].rearrange("p (b a) -> p a b", a=8)
                pS = sb(stSc, "pS", [NS, 8, NBP], F32)
                kb.op('dve', REC.tensor_reduce(out=smS[0:NS, 32:40], in_=sview, axis=AX.X, op=ALU.max), r=['scm'], w=['smS32'])
                kb.op('dve', REC.tensor_scalar(out=smS[0:NS, 32:40], in0=smS[0:NS, 32:40], scalar1=-1.0, scalar2=None, op0=ALU.mult), r=['smS32'], w=['smS32'])
                for gh in range(8):
                    kb.op('act', REC.activation(out=pS[:, gh, :], in_=sview[:, gh, :], func=AF.Exp, bias=smS[0:NS, 32 + gh:33 + gh], accum_out=smS[0:NS, 40 + gh:41 + gh]),
                          r=['scm', 'smS32'], w=['pS', 'smS40'])
                kb.op('dve', REC.reciprocal(out=smS[0:NS, 48:56], in_=smS[0:NS, 40:48]), r=['smS40'], w=['smS48'])
                kb.op('dve', REC.tensor_tensor(out=pS[:], in0=pS[:], in1=smS[0:NS, 48:56].unsqueeze(2).to_broadcast([NS, 8, NBP]), op=ALU.mult), r=['pS', 'smS48'], w=['pS'])
                kb.op('dve', REC.tensor_reduce(out=scS[:], in_=pS[:].rearrange("p (g h) b -> p g b h", g=2), axis=AX.X, op=ALU.add), r=['pS'], w=['scS'])
                vview = vcs_[:].rearrange("p (b g c) -> p b g c", g=2, c=64)
                for gh in range(8):
                    g_ = gh // 4
                    pr3 = tmpS[0:NS, 0:NBP * 64].rearrange("p (b c) -> p b c", c=64)
                    kb.op('dve', REC.tensor_tensor(out=pr3, in0=vview[:, :, g_, :], in1=pS[:, gh, :].unsqueeze(2).to_broadcast([NS, NBP, 64]), op=ALU.mult), r=['vcs_', 'pS'], w=['tmpS'])
                    kb.op('dve', REC.tensor_reduce(out=oall[:, 0, gh, :], in_=tmpS[0:NS, 0:NBP * 64].rearrange("p (b c) -> p c b", c=64), axis=AX.X, op=ALU.add), r=['tmpS'], w=['oall0'])
            kb.barrier()
            with contextlib.ExitStack() as stSd:
                ksg = sb(stSd, "ksg", [128, 64, 128], F32)
                vsg = sb(stSd, "vsg", [128, 64, 128], F32)
                accs = sb(stSd, "accs", [NS, 520], F32)
                m8s = sb(stSd, "m8s", [NS, 16], F32)
                i8s = sb(stSd, "i8s", [NS, 16], mybir.dt.uint32)
                idf = sb(stSd, "idf", [NS, 16], F32)
                mm_ = sb(stSd, "mm_", [NS, 16], F32)
                ohs = sb(stSd, "ohs", [NS, 16, NBP], F32)
                ptb = sb(stSd, "ptb", [NS, NPG, 2], F32)
                idi = sb(stSd, "idi", [NS, 16], I32)
                idp = sb(stSd, "idp", [128, 2], I32)
                sc2s = sb(stSd, "sc2s", [NS, NBP], F32)
                pti = sb(stSd, "pti", [NS, NPG], I32)
                ptf = sb(stSd, "ptf", [NS, NPG], F32)
                kb.dma('sp', REC.dma_start(out=pti[:], in_=ptabs_i), w=['pti'])
                kb.op('dve', REC.tensor_copy(out=ptf[:], in_=pti[:]), r=['pti'], w=['ptf'])
                kb.op('dve', REC.tensor_scalar(out=ptb[:, :, 0], in0=ptf[:], scalar1=2.0, scalar2=None, op0=ALU.mult), r=['ptf'], w=['ptb'])
                kb.op('dve', REC.tensor_scalar(out=ptb[:, :, 1], in0=ptf[:], scalar1=2.0, scalar2=1.0, op0=ALU.mult, op1=ALU.add), r=['ptf'], w=['ptb'])
                iotp = cS[:, 16 + 1024 + 128 * NREP + 16:16 + 1024 + 128 * NREP + 16 + 128]
                for g_ in range(2):
                    kb.op('pool', REC.memset(scS[:, g_, 0:1], 5.0), r=['scS'], w=['scS'])
                    kb.op('pool', REC.memset(scS[:, g_, NBP - 1:NBP], 5.0), r=['scS'], w=['scS'])
                    kb.op('dve', REC.max(out=m8s[:, 0:8], in_=scS[:, g_, :]), r=['scS'], w=['m8s'])
                    kb.op('dve', REC.max_index(out=i8s[:, 0:8], in_max=m8s[:, 0:8], in_values=scS[:, g_, :]), r=['scS', 'm8s'], w=['i8s'])
                    kb.op('dve', REC.match_replace(out=sc2s[:], in_to_replace=m8s[:, 0:8], in_values=scS[:, g_, :], imm_value=-2.0), r=['scS', 'm8s'], w=['sc2s'])
                    kb.op('dve', REC.max(out=m8s[:, 8:16], in_=sc2s[:]), r=['sc2s'], w=['m8s'])
                    kb.op('dve', REC.max_index(out=i8s[:, 8:16], in_max=m8s[:, 8:16], in_values=sc2s[:]), r=['sc2s', 'm8s'], w=['i8s'])
                    kb.op('dve', REC.tensor_copy(out=idf[:], in_=i8s[:]), r=['i8s'], w=['idf'])
                    kb.op('dve', REC.tensor_tensor(out=ohs[:], in0=idf[:].unsqueeze(2).to_broadcast([NS, 16, NBP]), in1=iotp[:, 0:NBP].unsqueeze(1).to_broadcast([NS, 16, NBP]), op=ALU.is_equal),
                          r=['idf', 'cS'], w=['ohs'])
                    kb.op('dve', REC.tensor_tensor(out=ohs[:], in0=ohs[:], in1=ptb[:].rearrange("p a b -> p (a b)").unsqueeze(1).to_broadcast([NS, 16, NBP]), op=ALU.mult), r=['ohs', 'ptb'], w=['ohs'])
                    kb.op('dve', REC.tensor_reduce(out=idf[:], in_=ohs[:], axis=AX.X, op=ALU.add), r=['ohs'], w=['idf'])
                    kb.op('dve', REC.tensor_copy(out=idi[:], in_=idf[:]), r=['idf'], w=['idi'])
                    kb.dma('sp', REC.dma_start(out=idx_d[g_:g_ + 1, :].rearrange("o (s k) -> (o s) k", k=16), in_=idi[:]), r=['idi'], w=['idx_d'])
                    kb.dma('sp', REC.dma_start(out=idp[:], in_=idx_d[g_:g_ + 1, :].rearrange("o (h s k) -> (o s k) h", h=2, k=16), allow_slow_non_contiguous=True), r=['idx_d'], w=['idp'])
                    for half in range(2):
                        for buf, pl_, breg in ((ksg, pools[2], 'ksg'), (vsg, pools[3], 'vsg')):
                            kb.dma('pool', REC.indirect_dma_start(out=buf[:].rearrange("p a b -> p (a b)"), out_offset=None, in_=pl_.ap().rearrange("n (c f) -> (n c) f", c=2),
                                                                  in_offset=bass.IndirectOffsetOnAxis(ap=idp[:, half:half + 1], axis=0)), r=['idp'], w=[breg])
                        qrep(qr, 1 + NGR + half)
                        ov = attend(ksg, vsg, qr, 64, None, None, None, 3, True, True, ['ksg', 'vsg'], ghs=range(4 * g_, 4 * g_ + 4))
                        kb.op('dve', REC.tensor_scalar(out=ov[:], in0=ov[:], scalar1=slotm, scalar2=None, op0=ALU.mult), r=['ovS', 'cR'], w=['ovS'])
                        kb.op('pe', REC.matmul(PB[3][0:NS, 0:512], lhsT=BAll[:, 1 + NGR + half, :], rhs=ov[:, 0:512], start=(half == 0), stop=(half == 1)), r=['ovS', 'cR'], w=['pb3'])
                        kb.op('pe', REC.matmul(PB[2][0:NS, 0:8], lhsT=BAll[:, 1 + NGR + half, :], rhs=ov[:, 512:520], start=(half == 0), stop=(half == 1)), r=['ovS', 'cR'], w=['pb2'])
                    kb.op('act', REC.copy(out=accs[:, 256 * g_:256 * g_ + 256], in_=PB[3][0:NS, 256 * g_:256 * g_ + 256]), r=['pb3'], w=['accs'])
                    kb.op('act', REC.copy(out=accs[:, 512 + 4 * g_:516 + 4 * g_], in_=PB[2][0:NS, 4 * g_:4 * g_ + 4]), r=['pb2'], w=['accs'])
                newkey(1, 'ks', 'vs', accs, 'accs')
            kb.barrier()
            gv3 = gS[:].rearrange("p (h c) -> p h c", c=3)
            for br in range(3):
                kb.op('dve', REC.tensor_tensor(out=oall[:, br, :, :], in0=oall[:, br, :, :], in1=gv3[:, :, br:br + 1].to_broadcast([NS, 8, 64]), op=ALU.mult),
                      r=['oall%d' % br, 'gS'], w=['oall%d' % br])
            kb.op('dve', REC.tensor_tensor(out=oall[:, 0, :, :], in0=oall[:, 0, :, :], in1=oall[:, 1, :, :], op=ALU.add), r=['oall0', 'oall1'], w=['oall0'])
            kb.op('dve', REC.tensor_tensor(out=oall[:, 0, :, :], in0=oall[:, 0, :, :], in1=oall[:, 2, :, :], op=ALU.add), r=['oall0', 'oall2'], w=['oall0'])
            ocs = sb(stS, "ocs", [NS, D], BF16)
            kb.op('act', REC.copy(out=ocs[:, 0:512], in_=ogs[:]), r=['ogs'], w=['ocs'])
            kb.op('act', REC.copy(out=ocs[:, 512:1024], in_=oall[:, 0, :, :].rearrange("p a b -> p (a b)")), r=['oall0'], w=['ocs'])
            kb.dma('sp', REC.dma_start(out=oscr_s, in_=ocs[:]), r=['ocs'], w=['oscr_s'])
          kb.barrier()
        NTOK = NQ * 128 + (NS if do_sample else 0)
        h1_d = nc.dram_tensor("h1_d", [NTOK, D], F32, kind="Internal").ap()
        hnT_d = nc.dram_tensor("hnT_d", [128, 8, NTOK], BF16, kind="Internal").ap()
        tiles = [(tq * 128, 128, 'halo' if tq == 0 else 'own', tq) for tq in range(NQ)]
        if do_sample:
            tiles.append((NQ * 128, NS, 'sample', None))

        def castload(st_, name, src_ap, rows_k, ncols):
            dst = sb(st_, name, [128, rows_k, ncols], BF16)
            for k in range(rows_k):
                for c0 in range(0, ncols, 2048):
                    c1 = min(ncols, c0 + 2048)
                    kb.dma('pool', REC.dma_start(out=dst[:, k, c0:c1], in_=src_ap[k * 128:(k + 1) * 128, c0:c1]), w=[name])
            return dst

        def rmsn(ss, junk, src, gvec, dst, srcreg, dstreg, P):
            kb.op('act', REC.activation(out=junk[0:P, :], in_=src, func=AF.Square, accum_out=ss[0:P, 0:1]), r=[srcreg], w=['junkB', 'ssB'])
            kb.op('dve', REC.tensor_scalar(out=ss[0:P, 1:2], in0=ss[0:P, 0:1], scalar1=1.0 / D, scalar2=EPS, op0=ALU.mult, op1=ALU.add), r=['ssB'], w=['ssB1'])
            kb.op('act', REC.activation(out=ss[0:P, 3:4], in_=ss[0:P, 1:2], func=AF.Sqrt), r=['ssB1'], w=['ssB3'])
            kb.op('dve', REC.reciprocal(out=ss[0:P, 2:3], in_=ss[0:P, 3:4]), r=['ssB3'], w=['ssB2'])
            kb.op('dve', REC.scalar_tensor_tensor(out=dst, in0=src, scalar=ss[0:P, 2:3], in1=gvec[0:P, :], op0=ALU.mult, op1=ALU.mult),
                  r=[srcreg, 'ssB2', 'gv'], w=[dstreg])

        def tr8q(src_bf, dstT, srcreg, dstreg, P, nk=8, par=0):
            for k in range(nk):
                kb.op('pe', REC.transpose(out=pbf(4)[:, k * P:(k + 1) * P], in_=src_bf[0:P, k * 128:(k + 1) * 128], identity=identb[0:P, 0:P]),
                      r=[srcreg, 'identb'], w=['pb4'])
            kb.op('act', REC.copy(out=dstT[:, 0:nk, 0:P], in_=pbf(4)[:, 0:nk * P].rearrange("p (a b) -> p a b", b=P)), r=['pb4'], w=[dstreg])

        with contextlib.ExitStack() as stB:
            w_o_sb = castload(stB, 'w_o_sb', w_o, 8, D)
            gv1 = sb(stB, "gv1", [128, D], F32)
            kb.dma('sp', REC.dma_start(out=gv1[:], in_=vecsB[:, 0:1024].partition_broadcast(128)), w=['gv'])
            ss = sb(stB, "ssB", [128, 8], F32)
            junk = sb(stB, "junkB", [128, D], BF16)
            xb2 = [sb(stB, "xb2_%d" % i, [128, D], F32) for i in range(2)]
            ocat = [sb(stB, "ocat%d" % i, [128, D], BF16) for i in range(2)]
            ocT = [sb(stB, "ocT%d" % i, [128, 8, 128], BF16) for i in range(2)]
            h1 = [sb(stB, "h1_%d" % i, [128, D], F32) for i in range(2)]
            hnb = [sb(stB, "hnb%d" % i, [128, D], BF16) for i in range(2)]
            hnT = [sb(stB, "hnT%d" % i, [128, 8, 128], BF16) for i in range(2)]
            for ti, (t0, P, mode, tq) in enumerate(tiles):
                pr_ = ti % 2
                kb.alias = {r_: r_ + '_%d' % pr_ for r_ in ('xb2', 'ocat', 'ocT', 'h1', 'hnb', 'hnT')}
                x_ap = xs_i if mode == 'sample' else xl[(HALO + tq) * 128:(HALO + tq + 1) * 128, :]
                oc_ap = oscr_s if mode == 'sample' else oscr[tq * 128:(tq + 1) * 128, :]
                kb.dma('sp', REC.dma_start(out=xb2[pr_][0:P, :], in_=x_ap), w=['xb2'])
                kb.dma('sp', REC.dma_start(out=ocat[pr_][0:P, :], in_=oc_ap), r=['oscr', 'oscr_s'], w=['ocat'])
                tr8q(ocat[pr_], ocT[pr_], 'ocat', 'ocT', P)
                for half in range(2):
                    for k in range(8):
                        kb.op('pe', REC.matmul(PB[half][0:P, :], lhsT=ocT[pr_][:, k, 0:P], rhs=w_o_sb[:, k, half * 512:(half + 1) * 512],
                                               start=(k == 0), stop=(k == 7)), r=['ocT', 'w_o_sb'], w=['pb%d' % half])
                    kb.op('dve', REC.tensor_tensor(out=h1[pr_][0:P, half * 512:(half + 1) * 512], in0=PB[half][0:P, :], in1=xb2[pr_][0:P, half * 512:(half + 1) * 512], op=ALU.add),
                          r=['pb%d' % half, 'xb2'], w=['h1'])
                kb.dma('sp', REC.dma_start(out=h1_d[t0:t0 + P, :], in_=h1[pr_][0:P, :]), r=['h1'], w=['h1_d'])
                rmsn(ss, junk, h1[pr_][0:P, :], gv1, hnb[pr_][0:P, :], 'h1', 'hnb', P)
                tr8q(hnb[pr_], hnT[pr_], 'hnb', 'hnT', P)
                kb.dma('sp', REC.dma_start(out=hnT_d[:, :, t0:t0 + P], in_=hnT[pr_][:, :, 0:P]), r=['hnT'], w=['hnT_d'])
            kb.alias = {}
        kb.barrier()
        with contextlib.ExitStack() as stB:
            w_up_sb = castload(stB, 'w_up_sb', w_up, 8, 2 * DFF)
            w_dn_sb = castload(stB, 'w_dn_sb', w_down, 22, D)
            cw = sb(stB, "cw", [128, 44, 4], F32)
            kb.dma('sp', REC.dma_start(out=cw[:].rearrange("p a b -> p (a b)"), in_=convw), w=['cw'])
            carry = sb(stB, "carry", [128, 44, 2], F32)
            kb.op('pool', REC.memset(carry[:], 0.0), w=['carry'])
            GT = 512
            hg = sb(stB, "hg", [128, 8, GT], BF16)
            uext2 = [sb(stB, "uext%d" % i, [128, 2, GT + 2], F32) for i in range(2)]
            ca2 = [sb(stB, "ca%d" % i, [128, 2, GT], F32) for i in range(2)]
            actT = sb(stB, "actT", [128, 22, GT], BF16)
            hio = [sb(stB, "hio%d" % i, [128, D], F32) for i in range(2)]
            sc0T = sb(stB, "sc0T", [128, 44, NS], F32) if do_sample else None
            sc1T = sb(stB, "sc1T", [128, 44, NS], F32) if do_sample else None
            if do_sample:
                kb.dma('sp', REC.dma_start(out=sc0T[:].rearrange("p a b -> p (a b)"), in_=sct_d[:, 0:44 * NS]), r=['sct_d'], w=['sc0T'])
                kb.dma('sp', REC.dma_start(out=sc1T[:].rearrange("p a b -> p (a b)"), in_=sct_d[:, 44 * NS:88 * NS]), r=['sct_d'], w=['sc1T'])
            groups = [(0, 128, 'halo')] + [(128 + gi * GT, min(GT, NQ * 128 - 128 - gi * GT), 'own') for gi in range((NQ * 128 - 128 + GT - 1) // GT)]
            if do_sample:
                groups.append((NQ * 128, NS, 'sample'))
            for (t0, W, mode) in groups:
                is_halo = (mode == 'halo')
                c0 = W - 2 if is_halo else 0
                ncol = W - c0
                kb.dma('sp', REC.dma_start(out=hg[:, :, 0:W], in_=hnT_d[:, :, t0:t0 + W]), r=['hnT_d'], w=['hg'])
                for i in range(22):
                    bk = (2, 3) if i % 2 == 0 else (5, 6)
                    uext = uext2[i % 2]
                    ca = ca2[i % 2]
                    UR = 'uext%d' % (i % 2)
                    CR = 'ca%d' % (i % 2)
                    for ab in range(2):
                        col = ab * DFF + i * 128
                        for k in range(8):
                            kb.op('pe', REC.matmul(PB[bk[ab]][:, 0:ncol], lhsT=w_up_sb[:, k, col:col + 128], rhs=hg[:, k, c0:W],
                                                   start=(k == 0), stop=(k == 7)), r=['hg', 'w_up_sb'], w=['pb%d' % bk[ab]])
                    if is_halo:
                        for ab in range(2):
                            kb.op('act', REC.copy(out=carry[:, ab * 22 + i, :], in_=PB[bk[ab]][:, 0:2]), r=['pb%d' % bk[ab]], w=['carry'])
                        continue
                    for ab in range(2):
                        ci = ab * 22 + i
                        if mode == 'sample':
                            kb.op('act', REC.activation(out=ca[:, ab, 0:W], in_=sc0T[:, ci, :], func=AF.Identity, scale=cw[:, ci, 0:1], bias=cw[:, ci, 3:4]),
                                  r=['sc0T', 'cw', 'actT'], w=[CR])
                            kb.op('dve', REC.scalar_tensor_tensor(out=ca[:, ab, 0:W], in0=sc1T[:, ci, :], scalar=cw[:, ci, 1:2], in1=ca[:, ab, 0:W],
                                                                  op0=ALU.mult, op1=ALU.add), r=['sc1T', 'cw', CR], w=[CR])
                            kb.op('dve', REC.scalar_tensor_tensor(out=ca[:, ab, 0:W], in0=PB[bk[ab]][:, 0:W], scalar=cw[:, ci, 2:3], in1=ca[:, ab, 0:W],
                                                                  op0=ALU.mult, op1=ALU.add), r=['pb%d' % bk[ab], 'cw', CR], w=[CR])
                            continue
                        kb.op('act', REC.copy(out=uext[:, ab, 0:2], in_=carry[:, ci, :]), r=['carry', CR], w=[UR])
                        kb.op('act', REC.copy(out=uext[:, ab, 2:W + 2], in_=PB[bk[ab]][:, 0:W]), r=['pb%d' % bk[ab], CR], w=[UR])
                        kb.op('pool', REC.tensor_copy(out=carry[:, ci, :], in_=uext[:, ab, W:W + 2]), r=[UR], w=['carry'])
                        kb.op('act', REC.activation(out=ca[:, ab, 0:W], in_=uext[:, ab, 0:W], func=AF.Identity, scale=cw[:, ci, 0:1], bias=cw[:, ci, 3:4]),
                              r=[UR, 'cw', 'actT'], w=[CR])
                        kb.op('dve', REC.scalar_tensor_tensor(out=ca[:, ab, 0:W], in0=uext[:, ab, 1:W + 1], scalar=cw[:, ci, 1:2], in1=ca[:, ab, 0:W],
                                                              op0=ALU.mult, op1=ALU.add), r=[UR, 'cw', CR], w=[CR])
                        kb.op('dve', REC.scalar_tensor_tensor(out=ca[:, ab, 0:W], in0=uext[:, ab, 2:W + 2], scalar=cw[:, ci, 2:3], in1=ca[:, ab, 0:W],
                                                              op0=ALU.mult, op1=ALU.add), r=[UR, 'cw', CR], w=[CR])
                    kb.op('act', REC.activation(out=ca[:, 0, 0:W], in_=ca[:, 0, 0:W], func=AF.Silu), r=[CR], w=[CR])
                    kb.op('dve', REC.tensor_tensor(out=actT[:, i, 0:W], in0=ca[:, 0, 0:W], in1=ca[:, 1, 0:W], op=ALU.mult), r=[CR], w=['actT'])
                if is_halo:
                    continue
                if mode == 'sample':
                    for ci_, c0_ in enumerate(range(0, 2 * DFF, 512)):
                        for k in range(8):
                            kb.op('pe', REC.matmul(PB[2][0:W, :], lhsT=hg[:, k, 0:W], rhs=w_up_sb[:, k, c0_:c0_ + 512], start=(k == 0), stop=(k == 7)),
                                  r=['hg', 'w_up_sb'], w=['pb2'])
                        kb.op('act', REC.copy(out=hio[ci_ % 2][0:W, 0:512], in_=PB[2][0:W, :]), r=['pb2'], w=['hio%d' % (ci_ % 2)])
                        kb.dma('sp', REC.dma_start(out=scv_o[:, 1, c0_:c0_ + 512], in_=hio[ci_ % 2][0:W, 0:512]), r=['hio%d' % (ci_ % 2)], w=['scv_o'])
                for st_ in range(0, W, 128):
                    P = min(128, W - st_)
                    hb = hio[(st_ // 128) % 2]
                    hr = 'hio%d' % ((st_ // 128) % 2)
                    kb.dma('sp', REC.dma_start(out=hb[0:P, :], in_=h1_d[t0 + st_:t0 + st_ + P, :]), r=['h1_d'], w=[hr])
                    for half in range(2):
                        for i in range(22):
                            kb.op('pe', REC.matmul(PB[half][0:P, :], lhsT=actT[:, i, st_:st_ + P], rhs=w_dn_sb[:, i, half * 512:(half + 1) * 512],
                                                   start=(i == 0), stop=(i == 21)), r=['actT', 'w_dn_sb'], w=['pb%d' % half])
                        kb.op('dve', REC.tensor_tensor(out=hb[0:P, half * 512:(half + 1) * 512], in0=PB[half][0:P, :], in1=hb[0:P, half * 512:(half + 1) * 512], op=ALU.add),
                              r=['pb%d' % half, hr], w=[hr])
                    kb.dma('sp', REC.dma_start(out=h1_d[t0 + st_:t0 + st_ + P, :], in_=hb[0:P, :]), r=[hr], w=['h1_d'])
            for t_ in range(2):
                kb.dma('sp', REC.dma_start(out=cv_o[t_:t_ + 1, :].rearrange("o (c p) -> p (o c)", p=128), in_=carry[:, :, t_],
                                           allow_slow_non_contiguous=True), r=['carry'], w=['cv_o'])
        kb.barrier()
        with contextlib.ExitStack() as stB:
            w_pg_sb = castload(stB, 'w_pg_sb', w_pg, 8, D)
            w_ple_sb = castload(stB, 'w_ple_sb', w_ple, 2, D)
            gv2 = sb(stB, "gv2", [128, 2048], F32)
            kb.dma('sp', REC.dma_start(out=gv2[:], in_=vecsB[:, 1024:3072].partition_broadcast(128)), w=['gv'])
            ss = sb(stB, "ssB3", [128, 8], F32)
            junk = sb(stB, "junkB3", [128, D], BF16)
            h2 = [sb(stB, "h2_%d" % i, [128, D], F32) for i in range(2)]
            hnb = [sb(stB, "hnc%d" % i, [128, D], BF16) for i in range(2)]
            hnT = [sb(stB, "hnU%d" % i, [128, 8, 128], BF16) for i in range(2)]
            pbb = [sb(stB, "pbb%d" % i, [128, 256], BF16) for i in range(2)]
            peT = [sb(stB, "peT%d" % i, [128, 2, 128], BF16) for i in range(2)]
            gsg = [sb(stB, "gsg%d" % i, [128, D], F32) for i in range(2)]
            yb = [sb(stB, "yb%d" % i, [128, D], F32) for i in range(2)]
            for ti, (t0, P, mode, tq) in enumerate(tiles):
                if mode == 'halo':
                    continue
                pr_ = ti % 2
                kb.alias = {r_: r_ + '_%d' % pr_ for r_ in ('h2', 'hnb', 'hnT', 'pbb', 'peT', 'gsg', 'yb')}
                pe_ap = ps_i if mode == 'sample' else pl[(tq - 1) * 128:tq * 128, :]
                y_ap = ys_o if mode == 'sample' else y_o[(tq - 1) * 128:tq * 128, :]
                kb.dma('sp', REC.dma_start(out=h2[pr_][0:P, :], in_=h1_d[t0:t0 + P, :]), r=['h1_d'], w=['h2'])
                kb.dma('pool', REC.dma_start(out=pbb[pr_][0:P, :], in_=pe_ap), w=['pbb'])
                rmsn(ss, junk, h2[pr_][0:P, :], gv2[:, 0:1024], hnb[pr_][0:P, :], 'h2', 'hnb', P)
                tr8q(hnb[pr_], hnT[pr_], 'hnb', 'hnT', P)
                tr8q(pbb[pr_], peT[pr_], 'pbb', 'peT', P, nk=2)
                for half in range(2):
                    for k in range(8):
                        kb.op('pe', REC.matmul(PB[half][0:P, :], lhsT=hnT[pr_][:, k, 0:P], rhs=w_pg_sb[:, k, half * 512:(half + 1) * 512],
                                               start=(k == 0), stop=(k == 7)), r=['hnT', 'w_pg_sb'], w=['pb%d' % half])
                    kb.op('act', REC.activation(out=gsg[pr_][0:P, half * 512:(half + 1) * 512], in_=PB[half][0:P, :], func=AF.Sigmoid), r=['pb%d' % half], w=['gsg'])
                    for k in range(2):
                        kb.op('pe', REC.matmul(PB[2 + half][0:P, :], lhsT=peT[pr_][:, k, 0:P], rhs=w_ple_sb[:, k, half * 512:(half + 1) * 512],
                                               start=(k == 0), stop=(k == 1)), r=['peT', 'w_ple_sb'], w=['pb%d' % (2 + half)])
                    kb.op('dve', REC.tensor_tensor(out=gsg[pr_][0:P, half * 512:(half + 1) * 512], in0=PB[2 + half][0:P, :], in1=gsg[pr_][0:P, half * 512:(half + 1) * 512], op=ALU.mult),
                          r=['pb%d' % (2 + half), 'gsg'], w=['gsg'])
                kb.op('pool', REC.tensor_tensor(out=h2[pr_][0:P, :], in0=h2[pr_][0:P, :], in1=gsg[pr_][0:P, :], op=ALU.add), r=['h2', 'gsg'], w=['h2'])
                rmsn(ss, junk, h2[pr_][0:P, :], gv2[:, 1024:2048], yb[pr_][0:P, :], 'h2', 'yb', P)
                kb.dma('sp', REC.dma_start(out=y_ap, in_=yb[pr_][0:P, :]), r=['yb'], w=['y_o'])
            kb.alias = {}
        kb.emit()
    return nc


def _consts(S, pad):
    NB = S // 64
    p = np.arange(128, dtype=np.float32)[:, None]
    f = np.arange(128, dtype=np.float32)[None, :]
    blk = np.arange(NB, dtype=np.float32)[None, :]
    padblk = pad // 64
    valid = np.broadcast_to((blk >= padblk).astype(np.float32), (128, NB))
    first = np.broadcast_to((blk == padblk).astype(np.float32), (128, NB))
    D0 = p - 64.0 * blk
    ident = np.eye(128, dtype=np.float32)
    tri = np.where(p <= f, 0.0, NEG).astype(np.float32)
    wbs = []
    for t in range(5):
        dlt = (t - 4) * 128 + p - f
        wbs.append(np.where((dlt <= 0) & (dlt > -512), 0.0, NEG).astype(np.float32))
    tribd = ((p <= f) & ((p // 64) == (f // 64))).astype(np.float32)
    rmask = np.broadcast_to(((np.arange(128) % 64) != 0).astype(np.float32)[None, :], (128, 128))
    return np.ascontiguousarray(np.concatenate([valid, first, D0, ident, tri] + wbs + [tribd, rmask], axis=1).astype(np.float32))


def _rope_table(S, pad):
    pos = (np.arange(S) - pad).astype(np.float32)
    inv = (500000.0 ** (-np.arange(8, dtype=np.float32) / 8)).astype(np.float32)
    ang = (pos[:, None] * inv[None, :]).astype(np.float32)
    return np.concatenate([np.cos(ang), np.sin(ang)], axis=1).astype(np.float32)


_NC_CACHE = {}


def kernel(x_prompt, x_sample, p_prompt, p_sample, cache_k_cmp, cache_v_cmp, cache_k_slc,
           cache_v_slc, cache_k_win, cache_v_win, state_gla, state_conv, page_table,
           g_attn, w_in, w_gla_gate, b_gla_gate, g_gla_out, b_nsa_gate, w_cmp_k, w_cmp_v,
           w_o, g_ffn, w_up, w_conv, b_conv, w_down, g_ple, w_ple, w_ple_gate, g_final, _prompt_only=False, _dbg=None, _do_sample=True):
    x_prompt = np.asarray(x_prompt)
    B, S, _ = x_prompt.shape
    NSEQ = x_sample.shape[0]
    PAST = page_table.shape[1] * 128
    chunk = S // 4
    NT = S // 128
    OWN = NT // 4
    NPOOLP = np.asarray(cache_k_cmp).shape[1]
    key = (S, NSEQ // 8, PAST, _dbg, _do_sample)
    if key not in _NC_CACHE:
        _NC_CACHE[key] = build(S, NSEQ // 8, PAST, dbg=_dbg, do_sample=_do_sample, NPOOLP=NPOOLP)
    nc = _NC_CACHE[key]
    f32 = lambda a: np.ascontiguousarray(np.asarray(a, dtype=np.float32))
    vecsA = np.concatenate([f32(g_attn)[0], f32(g_gla_out)[0], f32(b_nsa_gate)[0]])[None, :]
    vecsB = np.concatenate([f32(g_ffn)[0], f32(g_ple)[0], f32(g_final)])[None, :]
    wgg = np.concatenate([f32(w_gla_gate)[0], f32(b_gla_gate)], axis=0)
    wcm = np.zeros((128, 8), np.float32)
    for ti, wsrc in enumerate((f32(w_cmp_k)[0], f32(w_cmp_v)[0])):
        for g in range(2):
            for m in range(2):
                wcm[64 * m:64 * m + 64, ti * 4 + g * 2 + m] = wsrc[:, g]
    convw = np.zeros((128, 44, 4), np.float32)
    wcv = f32(w_conv)[0]
    convw[:, :, 0:3] = wcv.T.reshape(44, 128, 3).transpose(1, 0, 2)
    convw[:, :, 3] = f32(b_conv)[0].reshape(44, 128).T
    nti = min(32, NT)
    indp = np.zeros((64, nti * 128), np.float32)
    for m in range(nti):
        for k_ in range(128):
            indp[2 * m + k_ // 64, m * 128 + k_] = 1.0
    in_maps = []
    for c in range(8):
        b, j = c // 4, c % 4
        pad = (3 - j) * chunk
        xl = np.zeros((S, D), np.float32)
        xl[pad:] = x_prompt[b, 0:(j + 1) * chunk]
        kbias = np.zeros((1, S), np.float32)
        kbias[0, :pad] = NEG
        in_maps.append(dict(
            xl=xl, pl=f32(p_prompt[0, b, j * chunk:(j + 1) * chunk]), cs=_rope_table(S, pad), cmask=_consts(S, pad), kbias=kbias,
            ind=indp, wc=wcm, w_in=f32(w_in)[0], wgg=wgg, vecsA=vecsA, vecsB=vecsB, w_o=f32(w_o)[0], w_up=f32(w_up)[0],
            convw=np.ascontiguousarray(convw.reshape(128, 176)), w_down=f32(w_down)[0], w_ple=f32(w_ple)[0], w_pg=f32(w_ple_gate)[0]))

    if _do_sample:
        NS = NSEQ // 8
        NPG = PAST // 128
        SPG = 128 // NPG
        NGR = NS // SPG
        NREP = 1 + NGR + 2
        pools_h = [np.ascontiguousarray(np.asarray(a, dtype=np.float32)[0].reshape(NPOOLP, 16384)) for a in (cache_k_cmp, cache_v_cmp, cache_k_slc, cache_v_slc)]
        inv = (500000.0 ** (-np.arange(8, dtype=np.float32) / 8)).astype(np.float32)
        ang = (np.float32(PAST) * inv).astype(np.float32)
        csrow = np.concatenate([np.cos(ang), np.sin(ang)]).astype(np.float32)
        pidx = np.arange(128)
        Rep = np.zeros((NS, NREP, 128), np.float32)
        for s_ in range(NS):
            Rep[s_, 0, :] = (pidx // 8 == s_)
            for r_ in range(NGR):
                Rep[s_, 1 + r_, :] = (r_ * SPG + pidx // NPG == s_)
            for hf in range(2):
                Rep[s_, 1 + NGR + hf, :] = (8 * hf + pidx // 16 == s_)
        selS = np.zeros((NS, NS, 64), np.float32)
        for s_ in range(NS):
            selS[s_, s_, :] = 1.0
        eye = np.eye(16, dtype=np.float32)
        maskW = np.zeros((128, 64), np.float32)
        maskW[pidx % 8 == 0, 0] = NEG
        wrow = np.stack([f32(w_cmp_k)[0], f32(w_cmp_v)[0]], axis=0)
        repc = np.concatenate([Rep.transpose(2, 1, 0).reshape(128, NREP * 16), np.broadcast_to(eye.reshape(1, 256), (128, 256)),
                               maskW, np.broadcast_to(wrow.reshape(1, 256), (128, 256)), (pidx % 16 != 15).astype(np.float32)[:, None]], axis=1)
        ptab = np.asarray(page_table).astype(np.int32)
        for c in range(8):
            sl = slice(c * NS, (c + 1) * NS)
            scs = np.concatenate([np.broadcast_to(csrow[None, :], (NS, 16)), selS.reshape(NS, 1024), Rep.reshape(NS, NREP * 128), eye[:NS],
                                  np.broadcast_to(np.arange(128, dtype=np.float32)[None, :], (NS, 128)), np.zeros((NS, NPG), np.float32),
                                  np.zeros((NS, 16), np.float32)], axis=1)
            pt_c = ptab[sl]
            ptab_pg = np.ascontiguousarray(pt_c.reshape(NGR, SPG * NPG).T)
            in_maps[c].update(dict(
                xs=f32(x_sample[sl, 0]), ps=f32(p_sample[0, sl, 0]), ptab=ptab_pg, ptabs=np.ascontiguousarray(pt_c),
                pool0=pools_h[0], pool1=pools_h[1], pool2=pools_h[2], pool3=pools_h[3],
                wink=np.ascontiguousarray(f32(cache_k_win)[0, sl].reshape(NS, 512, 128)), winv=np.ascontiguousarray(f32(cache_v_win)[0, sl].reshape(NS, 512, 128)),
                sgla=f32(state_gla)[0, sl], sconv=np.ascontiguousarray(f32(state_conv)[0, sl].reshape(NS, -1)),
                scs=np.ascontiguousarray(scs.astype(np.float32)), rep=np.ascontiguousarray(repc.astype(np.float32))))
    res = run_bass_kernel_spmd(nc, in_maps, core_ids=list(range(8)))
    R = res.results
    global _LAST, _LASTNC
    _LAST = R
    _LASTNC = nc
    y_prompt = np.zeros((B, S, D), np.float32)
    kv = np.zeros((B, S, 768), np.float32)
    for c in range(8):
        b, j = c // 4, c % 4
        y_prompt[b, j * chunk:(j + 1) * chunk] = R[c]["y"]
        kv[b, j * chunk:(j + 1) * chunk] = R[c]["kv6"]
    kv = kv.reshape(B, S, 6, 2, 64)
    nw = min(512, S)
    outs_p = [kv[None, :, :, i] for i in range(4)] + [kv[None, :, S - nw:, 4], kv[None, :, S - nw:, 5]]
    gl = np.stack([R[3]["glast"], R[7]["glast"]])[None]
    cv = np.stack([R[3]["convrows"], R[7]["convrows"]])[None]
    outs_p = [np.ascontiguousarray(o) for o in outs_p] + [gl, cv]
    if _prompt_only:
        return (y_prompt, None, *outs_p)
    cat = lambda k: np.concatenate([np.asarray(R[c][k]) for c in range(8)], axis=0)
    y_s = cat("ys")[:, None, :]
    skv = cat("skv6").reshape(NSEQ, 1, 6, 2, 64)
    outs_s = [np.ascontiguousarray(skv[None, :, :, i]) for i in range(4)]
    outs_s += [cat("swk").reshape(1, NSEQ, 512, 2, 64), cat("swv").reshape(1, NSEQ, 512, 2, 64)]
    outs_s += [cat("sglo")[None], cat("scvo")[None]]
    return (y_prompt, y_s, *outs_p, *outs_s)
```

```python
import contextlib
import numpy as np
import ml_dtypes
import concourse.bass as bass
import concourse.mybir as mybir
from concourse.bass_utils import run_bass_kernel_spmd

import os
SKIPKV = bool(os.environ.get('SKIPKV'))
F32 = mybir.dt.float32
BF16 = mybir.dt.bfloat16
I32 = mybir.dt.int32
ALU = mybir.AluOpType
AF = mybir.ActivationFunctionType
AX = mybir.AxisListType

D = 1024
DFF = 2816
DIN = 2856
OFF = dict(gq=0, gk=256, gv=512, gr=1024, glr=1536, nq=1552, kc=2064, vc=2192, ks=2320, vs=2448,
           kw=2576, vw=2704, ng=2832)
NEG = -30000.0
EPS = 1e-6


class _Rec:
    def __getattr__(self, name):
        def mk(*a, **k):
            return lambda eng: getattr(eng, name)(*a, **k)
        return mk


REC = _Rec()


class KB:
    ENG = ['pe', 'act', 'dve', 'pool', 'sp']
    NDS = 40

    def __init__(self, nc):
        self.nc = nc
        self.q = {e: [] for e in self.ENG}
        self.cnt = {e: 0 for e in self.ENG}
        self.seen = {e: {} for e in self.ENG}
        self.lastw = {}
        self.readers = {}
        self.dtot = [0] * self.NDS
        self.dnext = 0
        self.dnext_sw = 0
        self.pend = {e: [] for e in self.ENG}
        self.alias = {}

    def _need(self, E, tok, waits):
        if tok is None:
            return
        key, val = tok
        if key == 'pe' and E == 'pe':
            return
        if self.seen[E].get(key, 0) >= val:
            return
        self.seen[E][key] = val
        waits.append(tok)

    def _deps(self, E, r, w):
        waits = []
        r = [self.alias.get(x, x) for x in r]
        w = [self.alias.get(x, x) for x in w]
        for reg in r:
            self._need(E, self.lastw.get(reg), waits)
        for reg in w:
            self._need(E, self.lastw.get(reg), waits)
            for key, val in self.readers.get(reg, {}).items():
                self._need(E, (key, val), waits)
        return waits

    def _commit(self, tok, r, w):
        r = [self.alias.get(x, x) for x in r]
        w = [self.alias.get(x, x) for x in w]
        for reg in w:
            self.lastw[reg] = tok
            self.readers[reg] = {}
        for reg in r:
            d = self.readers.setdefault(reg, {})
            if d.get(tok[0], 0) < tok[1]:
                d[tok[0]] = tok[1]

    def barrier(self):
        for E in self.ENG:
            waits = self.pend[E]
            for e2 in ['pe', 'act', 'dve', 'pool']:
                if e2 != E and self.cnt[e2]:
                    self._need(E, (e2, self.cnt[e2]), waits)
            if E != 'pe' and self.cnt.get(E) and E != 'sp':
                self._need(E, (E, self.cnt[E]), waits)
            for k in range(self.NDS):
                if self.dtot[k]:
                    self._need(E, (('d', k), self.dtot[k]), waits)

    def op(self, E, fn, r=(), w=()):
        waits = self.pend[E] + self._deps(E, r, w)
        self.pend[E] = []
        self.cnt[E] += 1
        tok = (E, self.cnt[E])
        self.q[E].append((waits, fn, (E, 1)))
        self._commit(tok, r, w)

    def dma(self, E, fn, r=(), w=()):
        if E == 'pool':
            k = 32 + self.dnext_sw
            self.dnext_sw = (self.dnext_sw + 1) % 8
        else:
            k = self.dnext
            self.dnext = (self.dnext + 1) % 32
        waits = self.pend[E] + self._deps(E, r, w)
        self.pend[E] = []
        if self.dtot[k]:
            self._need(E, (('d', k), self.dtot[k]), waits)
        self.dtot[k] += 16
        tok = (('d', k), self.dtot[k])
        self.q[E].append((waits, fn, (('d', k), 16)))
        self._commit(tok, r, w)

    def emit(self):
        nc = self.nc
        waits = []
        for k in range(self.NDS):
            if self.dtot[k]:
                self._need('sp', (('d', k), self.dtot[k]), waits)
        for e in ['pe', 'act', 'dve', 'pool']:
            if self.cnt[e]:
                self._need('sp', (e, self.cnt[e]), waits)
        self.q['sp'].append((waits, None, None))
        with contextlib.ExitStack() as st:
            sems = {}
            for e in ['pe', 'act', 'dve', 'pool']:
                sems[e] = st.enter_context(nc.semaphore('c_' + e))
            for k in range(self.NDS):
                sems[('d', k)] = st.enter_context(nc.semaphore('d%d' % k))
            block = st.enter_context(nc.Block())

            def run(eng, items):
                for waits, fn, inc in items:
                    for key, val in waits:
                        eng.wait_ge(sems[key], val)
                    if fn is None:
                        continue
                    fn(eng).then_inc(sems[inc[0]], inc[1])

            @block.tensor
            def _(eng):
                run(eng, self.q['pe'])

            @block.scalar
            def _(eng):
                run(eng, self.q['act'])

            @block.vector
            def _(eng):
                run(eng, self.q['dve'])

            @block.gpsimd
            def _(eng):
                run(eng, self.q['pool'])

            @block.sync
            def _(eng):
                run(eng, self.q['sp'])


def build(S, NSMP, PAST, do_sample=True, dbg=None, NPOOLP=0):
    NT = S // 128
    NB = S // 64
    OWN = NT // 4
    HALO = NT - OWN - 1
    NQ = OWN + 1
    W0 = HALO - 4
    NW = NT - W0
    BT = min(128, NB)
    NBT = NB // BT
    assert W0 >= 0

    nc = bass.Bass("TRN2", target_bir_lowering=False)

    def din(name, shape, dt=F32):
        return nc.dram_tensor(name, list(shape), dt, kind="ExternalInput").ap()

    def dout(name, shape, dt=F32):
        return nc.dram_tensor(name, list(shape), dt, kind="ExternalOutput").ap()

    xl = din("xl", [S, D])
    pl = din("pl", [OWN * 128, 256])
    cs = din("cs", [S, 16])
    cmask = din("cmask", [128, 2 * NB + NB + 128 + 128 + 5 * 128 + 128 + 128])
    kbias = din("kbias", [1, S])
    ind = din("ind", [64, min(32, NT) * 128])
    wc = din("wc", [128, 8])
    w_in = din("w_in", [D, DIN])
    wgg = din("wgg", [17, 256])
    vecsA = din("vecsA", [1, 1560])
    vecsB = din("vecsB", [1, 3072])
    w_o = din("w_o", [D, D])
    w_up = din("w_up", [D, 2 * DFF])
    convw = din("convw", [128, 44 * 4])
    w_down = din("w_down", [DFF, D])
    w_ple = din("w_ple", [256, D])
    w_pg = din("w_pg", [D, D])

    NS = NSMP
    NPG = PAST // 128
    NBP = PAST // 64
    SPG = 128 // NPG
    NGR = NS // SPG
    NREP = 1 + NGR + 2
    if do_sample:
        xs_i = din("xs", [NS, D])
        ps_i = din("ps", [NS, 256])
        ptab_i = din("ptab", [128, NGR], I32)
        ptabs_i = din("ptabs", [NS, NPG], I32)
        pools = [nc.dram_tensor("pool%d" % i, [NPOOLP, 16384], F32, kind="ExternalInput") for i in range(4)]
        wink_i = din("wink", [NS, 512, 128])
        winv_i = din("winv", [NS, 512, 128])
        sgla_i = din("sgla", [NS, 4, 64, 128])
        sconv_i = din("sconv", [NS, 2 * 2 * DFF])
        scs_i = din("scs", [NS, 16 + 64 * 16 + 128 * NREP + 16 + 128 + NPG + 16])
        rep_i = din("rep", [128, NREP * 16 + 256 + 64 + 4 * 64 + 1])
        ys_o = dout("ys", [NS, D])
        skv_o = dout("skv6", [NS, 768])
        swk_o = dout("swk", [NS, 512, 128])
        swv_o = dout("swv", [NS, 512, 128])
        sgl_o = dout("sglo", [NS, 4, 64, 128])
        scv_o = dout("scvo", [NS, 2, 2 * DFF])
        oscr_s = nc.dram_tensor("oscr_s", [NS, D], BF16, kind="Internal").ap()
        scmp_d = nc.dram_tensor("scmp_d", [NS * NPG, 16], F32, kind="Internal").ap()
        vc_d = nc.dram_tensor("vc_d", [NS * NPG, 256], F32, kind="Internal").ap()
        idx_d = nc.dram_tensor("idx_d", [2, NS * 16], I32, kind="Internal").ap()
        sct_d = nc.dram_tensor("sct_d", [128, 88 * NS], F32, kind="Internal").ap()
    y_o = dout("y", [OWN * 128, D])
    kv_o = dout("kv6", [OWN * 128, 768])
    gl_o = dout("glast", [4, 64, 128])
    cv_o = dout("convrows", [2, 2 * DFF])
    oscr = nc.dram_tensor("oscr", [NQ * 128, D], BF16, kind="Internal").ap()

    kb = KB(nc)
    dbg_o = dout("dbg", [128, 8192]) if dbg is not None else None
    dbg_items = []

    def dump(name, ap, regs, g=None, n=None):
        if dbg is None or (g, n) != tuple(dbg):
            return
        P_, C_ = ap.shape[0], int(np.prod(ap.shape[1:]))
        c0 = sum(it[2] for it in dbg_items)
        dbg_items.append((name, P_, C_, c0, tuple(ap.shape)))
        kb.dma('sp', REC.dma_start(out=dbg_o[0:P_, c0:c0 + C_], in_=ap), r=regs, w=['dbg_o'])
    nc._dbg_items = dbg_items
    with contextlib.ExitStack() as st0:
        def sb(st, name, shape, dt):
            return st.enter_context(nc.sbuf_tensor("s_" + name, list(shape), dt))

        def pst(st, name, shape, dt):
            return st.enter_context(nc.psum_tensor(name, list(shape), dt))

        PB = [pst(st0, "pb%d" % i, [128, 512], F32) for i in range(8)]

        def pbf(i):
            return PB[i][:].bitcast(BF16)

        identb = sb(st0, "identb", [128, 128], BF16)
        identf = sb(st0, "identf", [128, 128], F32)
        CMst = contextlib.ExitStack()
        CM = sb(CMst, "CM", [128, 2 * NB + NB + 128 + 128 + 5 * 128 + 128 + 128], F32)
        vrep = sb(CMst, "vrep", [128, 1560], F32)
        kb.dma('sp', REC.dma_start(out=CM[:], in_=cmask), w=['CM'])
        kb.dma('sp', REC.dma_start(out=vrep[:], in_=vecsA.partition_broadcast(128)), w=['vrep'])
        valid = CM[:, 0:NB]
        first = CM[:, NB:2 * NB]
        D0 = CM[:, 2 * NB:3 * NB]
        o_ = 3 * NB
        identf_src = CM[:, o_:o_ + 128]
        tri_src = CM[:, o_ + 128:o_ + 256]
        wb_src = CM[:, o_ + 256:o_ + 256 + 640]
        tribd = CM[:, o_ + 896:o_ + 1024]
        rmask = CM[0:64, o_ + 1024:o_ + 1152]
        kb.op('dve', REC.tensor_copy(out=identb[:], in_=identf_src), r=['CM'], w=['identb'])
        kb.op('dve', REC.tensor_copy(out=identf[:], in_=identf_src), r=['CM'], w=['identf'])
        g_attn = vrep[:, 0:1024]
        g_gla = vrep[:, 1024:1536]
        b_ng = vrep[:, 1536:1560]

        with contextlib.ExitStack() as stA:
            w_in_sb = sb(stA, "w_in_sb", [128, 8, DIN], BF16)
            for k in range(8):
                for c0 in (0, 1428):
                    kb.dma('pool', REC.dma_start(
                        out=w_in_sb[:, k, c0:c0 + 1428], in_=w_in[k * 128:(k + 1) * 128, c0:c0 + 1428]), w=['w_in_sb'])
            wgg_sb = sb(stA, "wgg_sb", [17, 256], BF16)
            kb.dma('pool', REC.dma_start(out=wgg_sb[:], in_=wgg), w=['wgg_sb'])
            wc_sb = sb(stA, "wc_sb", [128, 8], BF16)
            kb.dma('pool', REC.dma_start(out=wc_sb[:], in_=wc), w=['wc_sb'])
            tri4 = sb(stA, "tri4", [128, 4, 128], BF16)
            wb4 = sb(stA, "wb4", [128, 5, 4, 128], BF16)
            for h in range(4):
                kb.op('dve', REC.tensor_copy(out=tri4[:, h, :], in_=tri_src), r=['CM'], w=['tri4'])
                kb.op('dve', REC.tensor_copy(out=wb4[:, :, h, :], in_=wb_src.rearrange("p (t f) -> p t f", t=5)),
                      r=['CM'], w=['wb4'])
            ksT = sb(stA, "ksT", [128, S], BF16)
            vsx = sb(stA, "vsx", [128, NT, 65], BF16)
            kwT = sb(stA, "kwT", [65, NW * 128], BF16)
            vwx = sb(stA, "vwx", [128, NW, 65], BF16)
            kcT = sb(stA, "kcT", [64, NB], BF16)
            vcT = sb(stA, "vcT", [64, NB], BF16)
            vcs = sb(stA, "vcs", [BT, NBT, 64], BF16)
            kb.op('pool', REC.memset(kcT[:], 0.0), w=['kcT'])
            kb.op('pool', REC.memset(vcT[:], 0.0), w=['vcT'])
            kb.op('pool', REC.memset(ksT[0:64, :], 0.0), w=['ksT'])
            kb.op('pool', REC.memset(vsx[:], 1.0), w=['vsx'])
            kb.op('pool', REC.memset(vwx[:], 1.0), w=['vwx'])
            rep = S // (min(32, NT) * 128)
            for r_ in range(rep):
                w_ = min(32, NT) * 128
                kb.dma('pool', REC.dma_start(out=ksT[64:128, r_ * w_:(r_ + 1) * w_], in_=ind), w=['ksT'])
            kb.dma('pool', REC.dma_start(out=kwT[64:65, :], in_=kbias[:, W0 * 128:S]), w=['kwT'])
            hst = sb(stA, "hst", [64, 4, 128], F32)
            hbf = sb(stA, "hbf", [64, 4, 128], BF16)
            hbf1 = sb(stA, "hbf1", [64, 4, 128], BF16)
            kb.op('pool', REC.memset(hst[:], 0.0), w=['hst'])
            kb.op('pool', REC.memset(hbf[:], 0.0), w=['hbf'])
            glr_x = sb(stA, "glr_x", [32, 128], BF16)
            kb.op('pool', REC.memset(glr_x[:], 1.0), w=['glr_x'])
            qez = sb(stA, "qez", [64, 2, 4, 128], BF16)
            kb.op('pool', REC.memset(qez[:], 0.0), w=['qez'])
            xt = [sb(stA, "xt%d" % i, [128, D], F32) for i in range(2)]
            cst = [sb(stA, "cst%d" % i, [128, 16], F32) for i in range(2)]
            ss = sb(stA, "ss", [128, 8], F32)
            junk = sb(stA, "junk", [128, D], BF16)
            xn2 = [sb(stA, "xn0", [128, D], BF16)] * 2
            xnT2 = [sb(stA, "xnT%d" % i, [128, 8, 128], BF16) for i in range(2)]
            kvf2 = [sb(stA, "kvf0", [128, 6, 64], F32)] * 2
            ktmp = sb(stA, "ktmp", [128, 4, 4, 8], F32)
            kvb2 = [sb(stA, "kvb%d" % i, [128, 6, 64], BF16) for i in range(2)]
            kv6 = sb(stA, "kv6", [128, 6, 2, 64], F32)
            v_bf2 = [sb(stA, "v_bf%d" % i, [128, 512], BF16) for i in range(2)]
            G4 = sb(stA, "G4", [64, 4, 512], F32)
            gb = sb(stA, "gb", [64, 3, 512], BF16)
            kd_tok = sb(stA, "kd_tok", [128, 4, 64], BF16)
            attm = sb(stA, "attm", [128, 4, 128], BF16)
            sg = sb(stA, "sg", [64, 16], F32)
            rmask4 = sb(stA, "rmask4", [64, 512], F32)
            for hh_ in range(4):
                kb.op('dve', REC.tensor_copy(out=rmask4[:, 128 * hh_:128 * hh_ + 128], in_=rmask), r=['CM'], w=['rmask4'])
            sgr = sb(stA, "sgr", [128, 512], BF16)
            og = sb(stA, "og", [128, 512], F32)
            laT = og[:, 0:256]
            ekT = og[:, 256:512]
            ogb = sb(stA, "ogb", [128, 512], BF16)
            sm8 = sb(stA, "sm8", [128, 16], F32)
            qf = sb(stA, "qf", [128, 4, 64], F32)
            gts = sb(stA, "gts", [128, 12], F32)
            qtmp = sb(stA, "qtmp", [128, 4, 4, 8], F32)
            qpair = sb(stA, "qpair", [128, 4, 128], BF16)
            qT = sb(stA, "qT", [65, 4, 128], BF16)
            kb.op('pool', REC.memset(qpair[:], 0.0), w=['qpair'])
            kb.op('pool', REC.memset(qT[:], 1.0), w=['qT'])
            dd = sb(stA, "dd", [128, NB], F32)
            cb = sb(stA, "cb", [128, NB], F32)
            smx = sb(stA, "smx", [128, 4, NB], F32)
            sc = sb(stA, "sc", [128, NB], F32)
            sc2 = sb(stA, "sc2", [128, NB], F32)
            t1 = sb(stA, "t1", [128, NB], F32)
            t2 = cb
            selb = sb(stA, "selb", [128, NB], F32)
            m8 = sb(stA, "m8", [128, 16], F32)
            pT = sb(stA, "pT", [BT, 4, NBT, 128], BF16)
            NM = max(1, NB // 64)
            rhsm = sb(stA, "rhsm", [128, NM, 512], BF16)
            SBANKS = [0, 1, 7]
            PTbuf = sb(stA, "PTbuf", [128, 3, max(512, 2 * NB)], BF16)
            PT = [PTbuf[:, i, 0:512] for i in range(3)]
            pcb = PTbuf[:].rearrange("p a b -> p (a b)")[:, 0:4 * NB].rearrange("p (h b) -> p h b", b=NB)
            oT = sb(stA, "oT", [65, 2, 512], F32)
            ocm = sb(stA, "ocm", [128, 4, 64], F32)
            onb = sb(stA, "onb", [128, 256], BF16)
            rc = sb(stA, "rc", [128, 16], F32)

            def rmsnorm_tile(src, gvec, dst_bf, srcreg, dstreg):
                kb.op('dve', REC.scalar_tensor_tensor(out=junk[:], in0=src, scalar=1.0, in1=src, op0=ALU.mult, op1=ALU.mult, accum_out=ss[:, 0:1]),
                      r=[srcreg], w=['junk', 'ss'])
                kb.op('dve', REC.tensor_scalar(out=ss[:, 1:2], in0=ss[:, 0:1], scalar1=1.0 / D, scalar2=EPS,
                                                       op0=ALU.mult, op1=ALU.add), r=['ss'], w=['ss1'])
                kb.op('act', REC.activation(out=ss[:, 3:4], in_=ss[:, 1:2], func=AF.Ln), r=['ss1'], w=['ss3'])
                kb.op('act', REC.activation(out=ss[:, 2:3], in_=ss[:, 3:4], func=AF.Exp, scale=-0.5), r=['ss3'], w=['ss2'])
                kb.op('dve', REC.scalar_tensor_tensor(out=dst_bf, in0=src, scalar=ss[:, 2:3], in1=gvec,
                                                              op0=ALU.mult, op1=ALU.mult), r=[srcreg, 'ss2', 'vrep'], w=[dstreg])

            def transpose8(src_bf, dstT, srcreg, dstreg, bank=4):
                for k in range(8):
                    kb.op('pe', REC.transpose(out=pbf(bank)[:, k * 128:(k + 1) * 128],
                                                           in_=src_bf[:, k * 128:(k + 1) * 128], identity=identb[:]),
                          r=[srcreg, 'identb'], w=['pb%d' % bank])
                kb.op('act', REC.copy(out=dstT[:].rearrange("p a b -> p (a b)"), in_=pbf(bank)[:, 0:1024]),
                      r=['pb%d' % bank], w=[dstreg])

            def proj(bank, ncols, rhs_fn, M=128, lhs_fn=None, r=()):
                for k in range(8):
                    kb.op('pe', REC.matmul(PB[bank][0:M, 0:ncols], lhsT=xnT[:, k, 0:M], rhs=rhs_fn(k),
                                                        start=(k == 0), stop=(k == 7)),
                          r=['xnT', 'w_in_sb'] + list(r), w=['pb%d' % bank])

            def rope(dst, src, nh, cs_t, tmp, scale, sreg, dreg, tmpreg):
                c = cs_t[:, 0:8].unsqueeze(1).to_broadcast([128, nh, 8])
                s = cs_t[:, 8:16].unsqueeze(1).to_broadcast([128, nh, 8])
                x1 = src[:, :, 0:8]
                x2 = src[:, :, 8:16]
                kb.op('dve', REC.scalar_tensor_tensor(out=tmp[:, 0:nh, 0, :], in0=x1, scalar=scale, in1=c, op0=ALU.mult, op1=ALU.mult), r=[sreg, 'cst'], w=[tmpreg])
                kb.op('dve', REC.scalar_tensor_tensor(out=tmp[:, 0:nh, 1, :], in0=x2, scalar=scale, in1=s, op0=ALU.mult, op1=ALU.mult), r=[sreg, 'cst'], w=[tmpreg])
                kb.op('dve', REC.scalar_tensor_tensor(out=tmp[:, 0:nh, 2, :], in0=x2, scalar=scale, in1=c, op0=ALU.mult, op1=ALU.mult), r=[sreg, 'cst'], w=[tmpreg])
                kb.op('dve', REC.scalar_tensor_tensor(out=tmp[:, 0:nh, 3, :], in0=x1, scalar=scale, in1=s, op0=ALU.mult, op1=ALU.mult), r=[sreg, 'cst'], w=[tmpreg])
                kb.op('act', REC.mul(out=dst[:, :, 16:64], in_=src[:, :, 16:64], mul=scale), r=[sreg], w=[dreg])
                kb.op('dve', REC.tensor_tensor(out=dst[:, :, 0:8], in0=tmp[:, 0:nh, 0, :], in1=tmp[:, 0:nh, 1, :], op=ALU.subtract), r=[tmpreg], w=[dreg])
                kb.op('dve', REC.tensor_tensor(out=dst[:, :, 8:16], in0=tmp[:, 0:nh, 2, :], in1=tmp[:, 0:nh, 3, :], op=ALU.add), r=[tmpreg], w=[dreg])

            def stageA(n):
                nonlocal xn, xnT
                xn, xnT = xn2[n % 2], xnT2[n % 2]
                kb.alias = {r_: r_ + '_%d' % (n % 2) for r_ in ('xnT', 'kvb', 'v_bf', 'cst')}
                xb_ = xt[n % 2]
                xreg = 'xt%d' % (n % 2)
                kb.dma('sp', REC.dma_start(out=xb_[:], in_=xl[n * 128:(n + 1) * 128, :]), w=[xreg])
                kb.dma('sp', REC.dma_start(out=cst[n % 2][:], in_=cs[n * 128:(n + 1) * 128, :]), w=['cst'])
                rmsnorm_tile(xb_[:], g_attn, xn[:], xreg, 'xn')
                transpose8(xn, xnT, 'xn', 'xnT')

            xn = xnT = None
            stageA(0)
            for g in range(2):
                for n in range(NT):
                    if not (g == 1 and n == NT - 1):
                        stageA((n + 1) % NT)
                    xn, xnT, kvf, kvb, v_bf = xn2[n % 2], xnT2[n % 2], kvf2[n % 2], kvb2[n % 2], v_bf2[n % 2]
                    kb.alias = {r_: r_ + '_%d' % (n % 2) for r_ in ('xnT', 'kvb', 'v_bf', 'cst')}
                    cst_ = cst[n % 2]
                    proj(0, 384, lambda k: w_in_sb[:, k, OFF['kc']:OFF['kc'] + 768].rearrange("p (i c) -> p i c", c=128)[:, :, 64 * g:64 * g + 64])
                    pkv = PB[0][:, 0:384].rearrange("p (i c) -> p i c", c=64)
                    kb.op('act', REC.copy(out=kvf[:], in_=pkv), r=['pb0'], w=['kvf'])
                    kview = kvf[:].rearrange("p (i two) c -> p i two c", two=2)
                    rope(kview[:, :, 0, :], kview[:, :, 0, :], 3, cst_, ktmp, 1.0, 'kvf', 'kvf', 'ktmp')
                    kb.op('act', REC.copy(out=kvb[:], in_=kvf[:]), r=['kvf'], w=['kvb'])
                    if n >= NT - OWN and SKIPKV:
                        pass
                    elif n >= NT - OWN:
                        for i6 in range(6):
                            kb.dma('sp', REC.dma_start(
                                out=kv_o[(n - (NT - OWN)) * 128:(n - (NT - OWN) + 1) * 128, i6 * 128 + g * 64:i6 * 128 + g * 64 + 64],
                                in_=kvf[:, i6, :]), r=['kvf'], w=['kv_o'])
                    kb.op('pe', REC.transpose(out=pbf(4)[0:64, 0:128], in_=kvb[:, 2, :], identity=identb[:]),
                          r=['kvb', 'identb'], w=['pb4'])
                    kb.op('act', REC.copy(out=ksT[0:64, n * 128:(n + 1) * 128], in_=pbf(4)[0:64, 0:128]), r=['pb4'], w=['ksT'])
                    kb.op('pool', REC.tensor_copy(out=vsx[:, n, 0:64], in_=kvb[:, 3, :]), r=['kvb'], w=['vsx'])
                    if n >= W0:
                        kb.op('pe', REC.transpose(out=pbf(4)[0:64, 128:256], in_=kvb[:, 4, :], identity=identb[:]),
                              r=['kvb', 'identb'], w=['pb4'])
                        kb.op('act', REC.copy(out=kwT[0:64, (n - W0) * 128:(n - W0 + 1) * 128], in_=pbf(4)[0:64, 128:256]),
                              r=['pb4'], w=['kwT'])
                        kb.op('pool', REC.tensor_copy(out=vwx[:, n - W0, 0:64], in_=kvb[:, 5, :]), r=['kvb'], w=['vwx'])
                    kb.op('pe', REC.matmul(PB[5][0:64, 0:2], lhsT=kvb[:, 0, :], rhs=wc_sb[:, g * 2:g * 2 + 2], start=True, stop=True),
                          r=['kvb', 'wc_sb'], w=['pb5'])
                    kb.op('pe', REC.matmul(PB[5][0:64, 2:4], lhsT=kvb[:, 1, :], rhs=wc_sb[:, 4 + g * 2:4 + g * 2 + 2], start=True, stop=True),
                          r=['kvb', 'wc_sb'], w=['pb5'])
                    kb.op('dve', REC.tensor_copy(out=kcT[:, 2 * n:2 * n + 2], in_=PB[5][0:64, 0:2]), r=['pb5'], w=['kcT'])
                    kb.op('dve', REC.tensor_copy(out=vcT[:, 2 * n:2 * n + 2], in_=PB[5][0:64, 2:4]), r=['pb5'], w=['vcT'])

                    if g == 0:
                        need_o = n >= HALO
                        proj(1, 512, lambda k: w_in_sb[:, k, OFF['gv']:OFF['gv'] + 512])
                        kb.op('act', REC.copy(out=v_bf[:], in_=PB[1][:, :]), r=['pb1'], w=['v_bf'])
                        if need_o:
                            proj(1, 512, lambda k: w_in_sb[:, k, OFF['gr']:OFF['gr'] + 512])
                            kb.op('act', REC.activation(out=sgr[:], in_=PB[1][:, :], func=AF.Silu), r=['pb1'], w=['sgr'])
                        for k in range(8):
                            kb.op('pe', REC.matmul(PB[5][0:16, 128:256], lhsT=w_in_sb[:, k, OFF['glr']:OFF['glr'] + 16], rhs=xnT[:, k, :],
                                                                start=(k == 0), stop=(k == 7)), r=['xnT', 'w_in_sb'], w=['pb5'])
                        kb.op('dve', REC.tensor_copy(out=glr_x[0:16, :], in_=PB[5][0:16, 128:256]), r=['pb5'], w=['glr_x'])
                        if not need_o:
                            for k in range(8):
                                kb.op('pe', REC.matmul(PB[6][:, 0:256], lhsT=xnT[:, k, :], rhs=w_in_sb[:, k, OFF['gk']:OFF['gk'] + 256], start=(k == 0), stop=(k == 7)),
                                      r=['xnT', 'w_in_sb'], w=['pb6'])
                            kb.op('pe', REC.matmul(PB[2][:, 0:256], lhsT=glr_x[0:17, :], rhs=wgg_sb[0:17, :], start=True, stop=True), r=['glr_x', 'wgg_sb'], w=['pb2'])
                            kb.op('act', REC.activation(out=laT, in_=PB[2][:, 0:256], func=AF.Exp, scale=-1.0), r=['pb2'], w=['og'])
                            kb.op('act', REC.activation(out=laT, in_=laT, func=AF.Ln, bias=1.0), r=['og'], w=['og'])
                            kb.op('pe', REC.matmul(PB[3][:, 0:256], lhsT=tribd, rhs=laT, start=True, stop=True), r=['CM', 'og'], w=['pb3'])
                            kb.op('act', REC.activation(out=ekT, in_=PB[3][:, 0:256], func=AF.Exp, scale=1.0 / 16), r=['pb3'], w=['og'])
                            kb.op('dve', REC.tensor_tensor(out=kd_tok[:].rearrange("p a b -> p (a b)"), in0=PB[6][:, 0:256], in1=ekT, op=ALU.mult), r=['pb6', 'og'], w=['kd_tok'])
                            for hh in range(4):
                                kb.op('pe', REC.matmul(PB[7][0:64, 2 * hh:2 * hh + 2], lhsT=ekT[:, 64 * hh:64 * hh + 64], rhs=identf[:, 63:128:64], start=True, stop=True),
                                      r=['og', 'identf'], w=['pb7'])
                            kb.op('dve', REC.reciprocal(out=sg[:, 8:16], in_=PB[7][0:64, 0:8]), r=['pb7'], w=['sg8'])
                            for c in range(2):
                                for hh in range(4):
                                    kb.op('pe', REC.matmul(PB[0][0:64, 128 * hh:128 * hh + 128], lhsT=kd_tok[64 * c:64 * c + 64, hh, :],
                                                           rhs=v_bf[64 * c:64 * c + 64, 128 * hh:128 * hh + 128], start=True, stop=True), r=['kd_tok', 'v_bf'], w=['pb0'])
                                ebc = sg[:, 8:16].rearrange("p (h c) -> p h c", c=2)[:, :, c:c + 1].to_broadcast([64, 4, 128])
                                kb.op('dve', REC.tensor_tensor(out=hst[:].rearrange("p a b -> p (a b)"), in0=hst[:].rearrange("p a b -> p (a b)"), in1=PB[0][0:64, :], op=ALU.add),
                                      r=['hst', 'pb0'], w=['hst'])
                                kb.op('dve', REC.tensor_tensor(out=hst[:], in0=hst[:], in1=ebc, op=ALU.mult), r=['hst', 'sg8'], w=['hst'])
                            continue
                        for which, bank in ((('gq', 6), ('gk', 7)) if need_o else (('gk', 7),)):
                            for hh in range(4):
                                for k in range(8):
                                    kb.op('pe', REC.matmul(PB[bank][0:64, 128 * hh:128 * hh + 128], lhsT=w_in_sb[:, k, OFF[which] + 64 * hh:OFF[which] + 64 * hh + 64],
                                                           rhs=xnT[:, k, :], start=(k == 0), stop=(k == 7)), r=['xnT', 'w_in_sb'], w=['pb%d' % bank])
                        for hh in range(4):
                            kb.op('pe', REC.matmul(PB[2][0:64, 128 * hh:128 * hh + 128], lhsT=wgg_sb[0:17, 64 * hh:64 * hh + 64], rhs=glr_x[0:17, :],
                                                   start=True, stop=True), r=['glr_x', 'wgg_sb'], w=['pb2'])
                        G = G4
                        kb.op('act', REC.activation(out=G[:, 0, :], in_=PB[2][0:64, :], func=AF.Exp, scale=-1.0), r=['pb2'], w=['G0'])
                        kb.op('act', REC.activation(out=G[:, 0, :], in_=G[:, 0, :], func=AF.Ln, bias=1.0), r=['G0'], w=['G0'])
                        kb.op('dve', REC.tensor_tensor_scan(out=G[:, 1, :], data0=rmask4[:], data1=G[:, 0, :], initial=0.0, op0=ALU.mult, op1=ALU.add),
                              r=['G0', 'rmask4'], w=['G1'])
                        kb.op('act', REC.activation(out=G[:, 2, :], in_=G[:, 1, :], func=AF.Exp, scale=-1.0 / 16), r=['G1'], w=['G2'])
                        kb.op('act', REC.activation(out=G[:, 3, :], in_=G[:, 1, :], func=AF.Exp, scale=1.0 / 16), r=['G1'], w=['G3'])
                        kb.op('dve', REC.tensor_scalar(out=sg[:, 0:8].rearrange("p (h c) -> p h c", c=2), in0=G[:, 1, :].rearrange("p (h t) -> p h t", t=128)[:, :, 63:128:64],
                                                       scalar1=-1.0 / 16, scalar2=None, op0=ALU.mult), r=['G1'], w=['sg0'])
                        kb.op('act', REC.activation(out=sg[:, 8:16], in_=sg[:, 0:8], func=AF.Exp), r=['sg0'], w=['sg8'])
                        kb.op('dve', REC.tensor_tensor(out=G[:, 0, :].rearrange("p (a t) -> p a t", t=64), in0=G[:, 3, :].rearrange("p (a t) -> p a t", t=64),
                                                       in1=sg[:, 8:16].unsqueeze(2).to_broadcast([64, 8, 64]), op=ALU.mult), r=['G3', 'sg8', 'G0'], w=['G0'])
                        kb.op('dve', REC.tensor_tensor(out=gb[:, 2, :], in0=PB[7][0:64, :], in1=G[:, 0, :], op=ALU.mult), r=['pb7', 'G0'], w=['gb2'])
                        for hh in range(4):
                            kb.op('pe', REC.transpose(out=pbf(4)[:, hh * 64:(hh + 1) * 64], in_=gb[:, 2, 128 * hh:128 * hh + 128], identity=identb[0:64, 0:64]),
                                  r=['gb2', 'identb'], w=['pb4'])
                        kb.op('act', REC.copy(out=kd_tok[:].rearrange("p a b -> p (a b)"), in_=pbf(4)[:, 0:256]), r=['pb4'], w=['kd_tok'])
                        if need_o:
                            kb.op('dve', REC.scalar_tensor_tensor(out=gb[:, 0, :], in0=PB[6][0:64, :], scalar=0.125, in1=G[:, 2, :], op0=ALU.mult, op1=ALU.mult),
                                  r=['pb6', 'G2'], w=['gb0'])
                            kb.op('dve', REC.tensor_tensor(out=gb[:, 1, :], in0=PB[7][0:64, :], in1=G[:, 3, :], op=ALU.mult), r=['pb7', 'G3'], w=['gb1'])
                            gb0v = gb[:, 0, :].rearrange("p (h t) -> p h t", t=128)
                            kb.op('pool', REC.tensor_copy(out=qez[:, 0, :, 0:64], in_=gb0v[:, :, 0:64]), r=['gb0'], w=['qez'])
                            kb.op('pool', REC.tensor_copy(out=qez[:, 1, :, 64:128], in_=gb0v[:, :, 64:128]), r=['gb0'], w=['qez'])
                            for hh in range(4):
                                kb.op('pe', REC.matmul(PB[3][:, 128 * hh:128 * hh + 128], lhsT=gb[:, 1, 128 * hh:128 * hh + 128], rhs=gb[:, 0, 128 * hh:128 * hh + 128],
                                                       start=True, stop=True), r=['gb0', 'gb1'], w=['pb3'])
                            kb.op('dve', REC.tensor_tensor(out=attm[:], in0=PB[3][:, :].rearrange("p (h t) -> p h t", t=128),
                                                           in1=tribd.unsqueeze(1).to_broadcast([128, 4, 128]), op=ALU.mult), r=['pb3', 'CM'], w=['attm'])

                        def upd(c):
                            for hh in range(4):
                                kb.op('pe', REC.matmul(PB[0][0:64, 128 * hh:128 * hh + 128], lhsT=kd_tok[64 * c:64 * c + 64, hh, :],
                                                       rhs=v_bf[64 * c:64 * c + 64, 128 * hh:128 * hh + 128], start=True, stop=True), r=['kd_tok', 'v_bf'], w=['pb0'])
                            ebc = sg[:, 8:16].rearrange("p (h c) -> p h c", c=2)[:, :, c:c + 1].to_broadcast([64, 4, 128])
                            kb.op('dve', REC.tensor_tensor(out=hst[:], in0=hst[:], in1=ebc, op=ALU.mult), r=['hst', 'sg8'], w=['hst'])
                            kb.op('dve', REC.tensor_tensor(out=hst[:].rearrange("p a b -> p (a b)"), in0=hst[:].rearrange("p a b -> p (a b)"), in1=PB[0][0:64, :], op=ALU.add),
                                  r=['hst', 'pb0'], w=['hst'])
                        if need_o:
                            upd(0)
                            kb.op('act', REC.copy(out=hbf1[:], in_=hst[:]), r=['hst'], w=['hbf1'])
                            for hh in range(4):
                                kb.op('pe', REC.matmul(PB[2][:, 128 * hh:128 * hh + 128], lhsT=attm[:, hh, :], rhs=v_bf[:, 128 * hh:128 * hh + 128], start=True, stop=False),
                                      r=['attm', 'v_bf'], w=['pb2'])
                                kb.op('pe', REC.matmul(PB[2][:, 128 * hh:128 * hh + 128], lhsT=qez[:, 0, hh, :], rhs=hbf[:, hh, :], start=False, stop=False),
                                      r=['qez', 'hbf'], w=['pb2'])
                                kb.op('pe', REC.matmul(PB[2][:, 128 * hh:128 * hh + 128], lhsT=qez[:, 1, hh, :], rhs=hbf1[:, hh, :], start=False, stop=True),
                                      r=['qez', 'hbf1'], w=['pb2'])
                            upd(1)
                            kb.op('act', REC.copy(out=hbf[:], in_=hst[:]), r=['hst'], w=['hbf'])
                            og4 = og[:].rearrange("p (h v) -> p h v", v=128)
                            kb.op('act', REC.activation(out=og[:], in_=PB[2][:, :], func=AF.Square), r=['pb2'], w=['og'])
                            kb.op('dve', REC.tensor_reduce(out=sm8[:, 0:4], in_=og4, axis=AX.X, op=ALU.add), r=['og'], w=['sm8a'])
                            kb.op('dve', REC.tensor_scalar(out=sm8[:, 0:4], in0=sm8[:, 0:4], scalar1=1.0 / 128, scalar2=EPS, op0=ALU.mult, op1=ALU.add), r=['sm8a'], w=['sm8a'])
                            kb.op('act', REC.activation(out=sm8[:, 4:8], in_=sm8[:, 0:4], func=AF.Sqrt), r=['sm8a'], w=['sm8b'])
                            kb.op('dve', REC.reciprocal(out=sm8[:, 8:12], in_=sm8[:, 4:8]), r=['sm8b'], w=['sm8c'])
                            kb.op('dve', REC.tensor_tensor(out=og4, in0=PB[2][:, :].rearrange("p (h v) -> p h v", v=128),
                                                           in1=sm8[:, 8:12].unsqueeze(2).to_broadcast([128, 4, 128]), op=ALU.mult), r=['pb2', 'sm8c', 'og'], w=['og'])
                            kb.op('dve', REC.tensor_tensor(out=og[:], in0=og[:], in1=g_gla, op=ALU.mult), r=['og', 'vrep'], w=['og'])
                            kb.op('dve', REC.tensor_tensor(out=ogb[:], in0=og[:], in1=sgr[:], op=ALU.mult), r=['og', 'sgr'], w=['ogb'])
                        else:
                            upd(0)
                            upd(1)
                        if need_o:
                            kb.dma('sp', REC.dma_start(out=oscr[(n - HALO) * 128:(n - HALO + 1) * 128, 0:512], in_=ogb[:]),
                                   r=['ogb'], w=['oscr'])
                    if n < HALO:
                        continue
                    proj(1, 256, lambda k: w_in_sb[:, k, OFF['nq'] + 256 * g:OFF['nq'] + 256 * g + 256], r=())
                    for k in range(8):
                        kb.op('pe', REC.matmul(PB[1][:, 256:268], lhsT=xnT[:, k, :], rhs=w_in_sb[:, k, OFF['ng'] + 12 * g:OFF['ng'] + 12 * g + 12],
                                                            start=(k == 0), stop=(k == 7)), r=['xnT', 'w_in_sb'], w=['pb1'])
                    kb.op('dve', REC.tensor_tensor(out=gts[:], in0=PB[1][:, 256:268], in1=b_ng[:, 12 * g:12 * g + 12], op=ALU.add),
                          r=['pb1', 'vrep'], w=['gts'])
                    kb.op('act', REC.activation(out=gts[:], in_=gts[:], func=AF.Sigmoid), r=['gts'], w=['gts'])
                    pq = PB[1][:, 0:256].rearrange("p (h c) -> p h c", c=64)
                    rope(qf[:], pq, 4, cst_, qtmp, 0.125, 'pb1', 'qf', 'qtmp')
                    dump('qf', qf[:].rearrange('p a b -> p (a b)'), ['qf'], g, n)
                    dump('gts', gts[:], ['gts'], g, n)
                    kb.op('act', REC.copy(out=qpair[:, :, 0:64], in_=qf[:]), r=['qf'], w=['qpair'])
                    for h in range(4):
                        kb.op('pe', REC.transpose(out=pbf(4)[0:64, 512 + h * 128:512 + (h + 1) * 128], in_=qpair[:, h, 0:64], identity=identb[:]),
                              r=['qpair', 'identb'], w=['pb4'])
                    kb.op('act', REC.copy(out=qT[0:64, :, :].rearrange("p a b -> p (a b)"), in_=pbf(4)[0:64, 512:1024]), r=['pb4'], w=['qT'])
                    for t in range(NBT):
                        kb.op('pe', REC.transpose(out=pbf(4)[0:BT, t * 64:(t + 1) * 64], in_=vcT[:, t * BT:(t + 1) * BT], identity=identb[0:64, 0:64]),
                              r=['vcT', 'identb'], w=['pb4'])
                    kb.op('act', REC.copy(out=vcs[:].rearrange("p a b -> p (a b)"), in_=pbf(4)[0:BT, 0:NBT * 64]), r=['pb4'], w=['vcs'])
                    hb_ = 512 // NB if NB <= 512 else 1
                    for h in range(4):
                        bank = 5 + (h // hb_) if hb_ < 4 else 5
                        col = (h % hb_) * NB
                        kb.op('pe', REC.matmul(PB[bank][:, col:col + NB], lhsT=qT[0:64, h, :], rhs=kcT[:, :], start=True, stop=True),
                              r=['qT', 'kcT'], w=['pb%d' % bank])
                    kb.op('dve', REC.tensor_scalar(out=dd[:], in0=D0, scalar1=float(128 * n), scalar2=None, op0=ALU.add), r=['CM'], w=['dd'])
                    kb.op('dve', REC.scalar_tensor_tensor(out=cb[:], in0=dd[:], scalar=63.0, in1=valid, op0=ALU.is_ge, op1=ALU.mult), r=['dd', 'CM'], w=['cb'])
                    kb.op('dve', REC.tensor_scalar(out=cb[:], in0=cb[:], scalar1=-1.0, scalar2=-NEG, op0=ALU.add, op1=ALU.mult), r=['cb'], w=['cb'])
                    for h in range(4):
                        bank = 5 + (h // hb_) if hb_ < 4 else 5
                        col = (h % hb_) * NB
                        kb.op('dve', REC.tensor_tensor(out=smx[:, h, :], in0=PB[bank][:, col:col + NB], in1=cb[:], op=ALU.add),
                              r=['pb%d' % bank, 'cb'], w=['smx'])
                    kb.op('dve', REC.tensor_reduce(out=rc[:, 0:4], in_=smx[:], axis=AX.X, op=ALU.max), r=['smx'], w=['rc0'])
                    kb.op('dve', REC.tensor_scalar(out=rc[:, 4:8], in0=rc[:, 0:4], scalar1=-1000.0, scalar2=-1.0, op0=ALU.max, op1=ALU.mult), r=['rc0'], w=['rc1'])
                    for h in range(4):
                        kb.op('act', REC.activation(out=smx[:, h, :], in_=smx[:, h, :], func=AF.Exp, bias=rc[:, 4 + h:5 + h], accum_out=rc[:, 8 + h:9 + h]),
                              r=['smx', 'rc1'], w=['smx', 'rc2'])
                    kb.op('dve', REC.tensor_scalar(out=rc[:, 8:12], in0=rc[:, 8:12], scalar1=1e-30, scalar2=None, op0=ALU.max), r=['rc2'], w=['rc2'])
                    kb.op('dve', REC.reciprocal(out=rc[:, 12:16], in_=rc[:, 8:12]), r=['rc2'], w=['rc3'])
                    kb.op('dve', REC.tensor_tensor(out=smx[:], in0=smx[:], in1=rc[:, 12:16].unsqueeze(2).to_broadcast([128, 4, NB]), op=ALU.mult),
                          r=['smx', 'rc3'], w=['smx'])
                    dump('cb', cb[:], ['cb'], g, n)
                    dump('p', smx[:].rearrange('p a b -> p (a b)'), ['smx'], g, n)
                    kb.op('act', REC.copy(out=pcb, in_=smx[:]), r=['smx'], w=['PT0', 'PT1'])
                    kb.op('dve', REC.tensor_reduce(out=sc[:], in_=smx[:].rearrange("p h b -> p b h"), axis=AX.X, op=ALU.add), r=['smx'], w=['sc'])
                    for h in range(4):
                        for t in range(NBT):
                            kb.op('pe', REC.transpose(out=pbf(4)[0:BT, (h * NBT + t) * 128:(h * NBT + t + 1) * 128],
                                                                        in_=pcb[:, h, t * BT:(t + 1) * BT], identity=identb[:]),
                                  r=['PT0', 'PT1', 'identb'], w=['pb4'])
                    kb.op('act', REC.copy(out=pT[:].rearrange("p a b c -> p (a b c)"), in_=pbf(4)[0:BT, 0:4 * NBT * 128]), r=['pb4'], w=['pT'])
                    for h in range(4):
                        for t in range(NBT):
                            kb.op('pe', REC.matmul(PB[0][:, h * 64:(h + 1) * 64], lhsT=pT[:, h, t, :], rhs=vcs[:, t, :],
                                                                     start=(t == 0), stop=(t == NBT - 1)), r=['pT', 'vcs'], w=['pb0'])
                    kb.op('act', REC.copy(out=ocm[:].rearrange("p a b -> p (a b)"), in_=PB[0][:, 0:256]), r=['pb0'], w=['ocm'])
                    dump('ocmp', ocm[:].rearrange('p a b -> p (a b)'), ['ocm'], g, n)
                    dump('score0', sc[:], ['sc'], g, n)
                    kb.op('dve', REC.tensor_scalar(out=t1[:], in0=dd[:], scalar1=0.0, scalar2=None, op0=ALU.is_ge), r=['dd'], w=['t1'])
                    kb.op('dve', REC.scalar_tensor_tensor(out=t2[:], in0=dd[:], scalar=128.0, in1=t1[:], op0=ALU.is_lt, op1=ALU.mult), r=['dd', 't1'], w=['cb'])
                    kb.op('dve', REC.tensor_tensor(out=t2[:], in0=t2[:], in1=first, op=ALU.max), r=['cb', 'CM'], w=['cb'])
                    kb.op('dve', REC.tensor_tensor(out=t1[:], in0=t1[:], in1=valid, op=ALU.mult), r=['t1', 'CM'], w=['t1'])
                    kb.op('dve', REC.scalar_tensor_tensor(out=sc[:], in0=t2[:], scalar=5.0, in1=sc[:], op0=ALU.mult, op1=ALU.max), r=['cb', 'sc'], w=['sc'])
                    kb.op('dve', REC.scalar_tensor_tensor(out=sc[:], in0=sc[:], scalar=1.0, in1=t1[:], op0=ALU.add, op1=ALU.mult), r=['sc', 't1'], w=['sc'])
                    kb.op('dve', REC.tensor_scalar(out=sc[:], in0=sc[:], scalar1=-1.0, scalar2=None, op0=ALU.add), r=['sc'], w=['sc'])
                    kb.op('dve', REC.max(out=m8[:, 0:8], in_=sc[:]), r=['sc'], w=['m8a'])
                    kb.op('dve', REC.match_replace(out=sc2[:], in_to_replace=m8[:, 0:8], in_values=sc[:], imm_value=-2.0), r=['sc', 'm8a'], w=['sc2'])
                    kb.op('dve', REC.max(out=m8[:, 8:16], in_=sc2[:]), r=['sc2'], w=['m8b'])
                    kb.op('dve', REC.scalar_tensor_tensor(out=selb[:], in0=sc[:], scalar=m8[:, 15:16], in1=t1[:], op0=ALU.is_ge, op1=ALU.mult),
                          r=['sc', 'm8b', 't1'], w=['selb'])
                    kb.op('dve', REC.tensor_scalar(out=selb[:], in0=selb[:], scalar1=-1.0, scalar2=-NEG, op0=ALU.add, op1=ALU.mult), r=['selb'], w=['selb'])
                    dump('score', sc[:], ['sc'], g, n)
                    dump('m8', m8[:], ['m8a', 'm8b'], g, n)
                    dump('selb', selb[:], ['selb'], g, n)
                    nm_need = (2 * n + 1) // 64 + 1
                    for m in range(nm_need):
                        wdt = min(64, NB)
                        kb.op('dve', REC.tensor_copy(out=qpair[:, :, 64:64 + wdt],
                                                                           in_=selb[:, m * 64:m * 64 + wdt].unsqueeze(1).to_broadcast([128, 4, wdt])),
                              r=['selb', 'qpair'], w=['qpair'])
                        for h in range(4):
                            kb.op('pe', REC.transpose(out=pbf(4)[:, h * 128:(h + 1) * 128], in_=qpair[:, h, :], identity=identb[:]),
                                  r=['qpair', 'identb'], w=['pb4'])
                        kb.op('act', REC.copy(out=rhsm[:, m, :], in_=pbf(4)[:, 0:512]), r=['pb4'], w=['rhsm'])
                    tri_flat = tri4[:].rearrange("p a b -> p (a b)")
                    qT_flat = qT[:].rearrange("p a b -> p (a b)")
                    jobs = []
                    for j in range(n + 1):
                        jobs.append(dict(l=ksT[:, j * 128:(j + 1) * 128], r=rhsm[:, (2 * j) // 64, :], rr=['ksT', 'rhsm'],
                                         bias=(tri_flat, 'tri4') if j == n else None, v=vsx[:, j, :], vr='vsx', acc=2, first=(j == 0), last=(j == n)))
                    for t in range(5):
                        j = n - 4 + t
                        jobs.append(dict(l=kwT[:, (j - W0) * 128:(j - W0 + 1) * 128], r=qT_flat, rr=['kwT', 'qT'],
                                         bias=(wb4[:, t, :, :].rearrange("p a b -> p (a b)"), 'wb4'), v=vwx[:, j - W0, :], vr='vwx', acc=3, first=(t == 0), last=(t == 4)))
                    NBUF = len(SBANKS)
                    LOOK = NBUF - 1
                    for idx in range(len(jobs) + LOOK):
                        if idx < len(jobs):
                            jb = jobs[idx]
                            bank = SBANKS[idx % NBUF]
                            breg = 'pb%d' % bank
                            kb.op('pe', REC.matmul(PB[bank][:, :], lhsT=jb['l'], rhs=jb['r'], start=True, stop=(jb['bias'] is None)), r=jb['rr'], w=[breg])
                            if jb['bias'] is not None:
                                kb.op('pe', REC.matmul(PB[bank][:, :], lhsT=identb[:], rhs=jb['bias'][0], start=False, stop=True), r=['identb', jb['bias'][1]], w=[breg])
                        k_ = idx - LOOK
                        if k_ >= 0:
                            jb = jobs[k_]
                            bank = SBANKS[k_ % NBUF]
                            breg = 'pb%d' % bank
                            preg = 'PT%d' % (k_ % NBUF)
                            kb.op('act', REC.activation(out=PT[k_ % NBUF], in_=PB[bank][:, :], func=AF.Exp), r=[breg], w=[preg])
                            kb.op('pe', REC.matmul(PB[jb['acc']][0:65, :], lhsT=jb['v'], rhs=PT[k_ % NBUF], start=jb['first'], stop=jb['last']),
                                  r=[jb['vr'], preg], w=['pb%d' % jb['acc']])
                    kb.op('act', REC.copy(out=oT[:, 0, :], in_=PB[2][0:65, :]), r=['pb2'], w=['oT0'])
                    kb.op('act', REC.copy(out=oT[:, 1, :], in_=PB[3][0:65, :]), r=['pb3'], w=['oT1'])
                    dump('oT0', oT[:, 0, :], ['oT0'], g, n)
                    dump('oT1', oT[:, 1, :], ['oT1'], g, n)
                    for br in range(2):
                        for h in range(4):
                            kb.op('pe', REC.transpose(out=PB[5 + br][:, h * 65:(h + 1) * 65], in_=oT[:, br, h * 128:(h + 1) * 128],
                                                                          identity=identf[0:65, 0:65]), r=['oT%d' % br, 'identf'], w=['pb%d' % (5 + br)])
                    gv = gts[:].rearrange("p (h c) -> p h c", c=3)
                    for br in range(2):
                        pv = PB[5 + br][:, 0:260].rearrange("p (h c) -> p h c", c=65)
                        kb.op('dve', REC.tensor_scalar(out=rc[:, 4 * br:4 * br + 4].unsqueeze(2), in0=pv[:, :, 64:65], scalar1=1e-30, scalar2=None,
                                                                             op0=ALU.max), r=['pb%d' % (5 + br)], w=['rcf%d' % br])
                        kb.op('dve', REC.reciprocal(out=rc[:, 4 * br:4 * br + 4], in_=rc[:, 4 * br:4 * br + 4]), r=['rcf%d' % br], w=['rcf%d' % br])
                        kb.op('dve', REC.tensor_tensor(out=rc[:, 4 * br:4 * br + 4].unsqueeze(2), in0=rc[:, 4 * br:4 * br + 4].unsqueeze(2),
                                                                      in1=gv[:, :, 1 + br:2 + br], op=ALU.mult), r=['rcf%d' % br, 'gts'], w=['rcf%d' % br])
                    kb.op('dve', REC.tensor_tensor(out=ocm[:], in0=ocm[:], in1=gv[:, :, 0:1].to_broadcast([128, 4, 64]), op=ALU.mult), r=['ocm', 'gts'], w=['ocm'])
                    for br in range(2):
                        pv = PB[5 + br][:, 0:260].rearrange("p (h c) -> p h c", c=65)
                        kb.op('dve', REC.tensor_tensor(out=qf[:], in0=pv[:, :, 0:64],
                                                                             in1=rc[:, 4 * br:4 * br + 4].unsqueeze(2).to_broadcast([128, 4, 64]), op=ALU.mult),
                              r=['pb%d' % (5 + br), 'rcf%d' % br, 'qf'], w=['qf'])
                        kb.op('dve', REC.tensor_tensor(out=ocm[:], in0=ocm[:], in1=qf[:], op=ALU.add), r=['ocm', 'qf'], w=['ocm'])
                    dump('ofin', ocm[:].rearrange('p a b -> p (a b)'), ['ocm'], g, n)
                    kb.op('act', REC.copy(out=onb[:, 0:256], in_=ocm[:].rearrange("p a b -> p (a b)")), r=['ocm'], w=['onb'])
                    kb.dma('sp', REC.dma_start(out=oscr[(n - HALO) * 128:(n - HALO + 1) * 128, 512 + 256 * g:768 + 256 * g], in_=onb[:, 0:256]),
                           r=['onb'], w=['oscr'])
            kb.dma('sp', REC.dma_start(out=gl_o.rearrange("h k v -> k h v"), in_=hst[:]), r=['hst'], w=['gl_o'])

        kb.alias = {}
        CMst.close()
        kb.barrier()

        if do_sample:
          kb.barrier()
          with contextlib.ExitStack() as stS:
            cS = sb(stS, "cS", [NS, 16 + 64 * 16 + 128 * NREP + 16 + 128 + NPG + 16], F32)
            kb.dma('sp', REC.dma_start(out=cS[:], in_=scs_i), w=['cS'])
            csS = cS[:, 0:16]
            o_ = 16
            selS = cS[:, o_:o_ + 1024].rearrange("p (s m) -> p s m", m=64)
            o_ += 1024
            RepAll = cS[:, o_:o_ + 128 * NREP].rearrange("p (r m) -> p r m", m=128)
            o_ += 128 * NREP
            id16 = cS[:, o_:o_ + 16]
            o_ += 16
            o_ += 128
            ptabs = cS[:, o_:o_ + NPG]
            o_ += NPG
            slotw = cS[:, o_:o_ + 16]
            cR = sb(stS, "cR", [128, NREP * 16 + 256 + 64 + 4 * 64 + 1], F32)
            kb.dma('sp', REC.dma_start(out=cR[:], in_=rep_i), w=['cR'])
            BAll = cR[:, 0:NREP * 16].rearrange("p (r m) -> p r m", m=16)
            eye16 = cR[0:64, NREP * 16:NREP * 16 + 256].rearrange("p (a b) -> p a b", b=16)
            maskW = cR[:, NREP * 16 + 256:NREP * 16 + 320]
            wrow = cR[:, NREP * 16 + 320:NREP * 16 + 576].rearrange("p (t c g) -> p t c g", t=2, g=2)
            slotm = cR[:, NREP * 16 + 576:NREP * 16 + 577]
            vrS = sb(stS, "vrS", [128, 1560], F32)
            kb.dma('sp', REC.dma_start(out=vrS[:], in_=vecsA.partition_broadcast(128)), w=['vrS'])
            zs = sb(stS, "zs", [NS, DIN], F32)
            qs = sb(stS, "qs", [NS, 8, 64], F32)
            gS = sb(stS, "gS", [NS, 24], F32)
            oall = sb(stS, "oall", [NS, 4, 8, 64], F32)
            ogs = sb(stS, "ogs", [NS, 512], F32)
            smS = sb(stS, "smS", [128, 64], F32)
            tmpS = sb(stS, "tmpS", [128, 8192], F32)
            scS = sb(stS, "scS", [NS, 2, NBP], F32)
            with contextlib.ExitStack() as stSa:
                w_in_sb = sb(stSa, "w_in_sbS", [128, 8, DIN], BF16)
                for k in range(8):
                    for c0 in (0, 1428):
                        kb.dma('pool', REC.dma_start(out=w_in_sb[:, k, c0:c0 + 1428], in_=w_in[k * 128:(k + 1) * 128, c0:c0 + 1428]), w=['w_in_sbS'])
                wgg_sb = sb(stSa, "wgg_sbS", [17, 256], BF16)
                kb.dma('pool', REC.dma_start(out=wgg_sb[:], in_=wgg), w=['wgg_sbS'])
                xsb = sb(stSa, "xsb", [NS, D], F32)
                xnS = sb(stSa, "xnS", [NS, D], BF16)
                xnTS = sb(stSa, "xnTS", [128, 8, NS], BF16)
                junkS = tmpS
                kb.dma('sp', REC.dma_start(out=xsb[:], in_=xs_i), w=['xsb'])
                kb.op('act', REC.activation(out=junkS[0:NS, 0:D], in_=xsb[:], func=AF.Square, accum_out=smS[0:NS, 0:1]), r=['xsb'], w=['tmpS', 'smS0'])
                kb.op('dve', REC.tensor_scalar(out=smS[0:NS, 1:2], in0=smS[0:NS, 0:1], scalar1=1.0 / D, scalar2=EPS, op0=ALU.mult, op1=ALU.add), r=['smS0'], w=['smS1'])
                kb.op('act', REC.activation(out=smS[0:NS, 2:3], in_=smS[0:NS, 1:2], func=AF.Sqrt), r=['smS1'], w=['smS2'])
                kb.op('dve', REC.reciprocal(out=smS[0:NS, 3:4], in_=smS[0:NS, 2:3]), r=['smS2'], w=['smS3'])
                kb.op('dve', REC.scalar_tensor_tensor(out=xnS[:], in0=xsb[:], scalar=smS[0:NS, 3:4], in1=vrS[0:NS, 0:1024], op0=ALU.mult, op1=ALU.mult),
                      r=['xsb', 'smS3', 'vrS'], w=['xnS'])
                for k in range(8):
                    kb.op('pe', REC.transpose(out=pbf(4)[:, k * NS:(k + 1) * NS], in_=xnS[:, k * 128:(k + 1) * 128], identity=identb[0:NS, 0:NS]),
                          r=['xnS', 'identb'], w=['pb4'])
                kb.op('act', REC.copy(out=xnTS[:].rearrange("p a b -> p (a b)"), in_=pbf(4)[:, 0:8 * NS]), r=['pb4'], w=['xnTS'])
                for ci, c0 in enumerate(range(0, DIN, 512)):
                    c1 = min(DIN, c0 + 512)
                    bank = ci % 2
                    for k in range(8):
                        kb.op('pe', REC.matmul(PB[bank][0:NS, 0:c1 - c0], lhsT=xnTS[:, k, :], rhs=w_in_sb[:, k, c0:c1], start=(k == 0), stop=(k == 7)),
                              r=['xnTS', 'w_in_sbS'], w=['pb%d' % bank])
                    kb.op('act', REC.copy(out=zs[:, c0:c1], in_=PB[bank][0:NS, 0:c1 - c0]), r=['pb%d' % bank], w=['zs'])

                def ropeS(dst, src, nh, scale):
                    c = csS[:, 0:8].unsqueeze(1).to_broadcast([NS, nh, 8])
                    sn = csS[:, 8:16].unsqueeze(1).to_broadcast([NS, nh, 8])
                    tmp = tmpS[0:NS, 0:nh * 32].rearrange("p (h a b) -> p h a b", a=4, b=8)
                    x1 = src[:, :, 0:8]
                    x2 = src[:, :, 8:16]
                    kb.op('dve', REC.scalar_tensor_tensor(out=tmp[:, :, 0, :], in0=x1, scalar=scale, in1=c, op0=ALU.mult, op1=ALU.mult), r=['zs', 'cS'], w=['tmpS'])
                    kb.op('dve', REC.scalar_tensor_tensor(out=tmp[:, :, 1, :], in0=x2, scalar=scale, in1=sn, op0=ALU.mult, op1=ALU.mult), r=['zs', 'cS'], w=['tmpS'])
                    kb.op('dve', REC.scalar_tensor_tensor(out=tmp[:, :, 2, :], in0=x2, scalar=scale, in1=c, op0=ALU.mult, op1=ALU.mult), r=['zs', 'cS'], w=['tmpS'])
                    kb.op('dve', REC.scalar_tensor_tensor(out=tmp[:, :, 3, :], in0=x1, scalar=scale, in1=sn, op0=ALU.mult, op1=ALU.mult), r=['zs', 'cS'], w=['tmpS'])
                    kb.op('dve', REC.tensor_scalar(out=dst[:, :, 16:64], in0=src[:, :, 16:64], scalar1=scale, scalar2=None, op0=ALU.mult), r=['zs'], w=['zs', 'qs'])
                    kb.op('dve', REC.tensor_tensor(out=dst[:, :, 0:8], in0=tmp[:, :, 0, :], in1=tmp[:, :, 1, :], op=ALU.subtract), r=['tmpS'], w=['zs', 'qs'])
                    kb.op('dve', REC.tensor_tensor(out=dst[:, :, 8:16], in0=tmp[:, :, 2, :], in1=tmp[:, :, 3, :], op=ALU.add), r=['tmpS'], w=['zs', 'qs'])
                ropeS(qs[:], zs[:, OFF['nq']:OFF['nq'] + 512].rearrange("p (h c) -> p h c", c=64), 8, 0.125)
                for nm in ('kc', 'ks', 'kw'):
                    v_ = zs[:, OFF[nm]:OFF[nm] + 128].rearrange("p (h c) -> p h c", c=64)
                    ropeS(v_, v_, 2, 1.0)
                kb.dma('sp', REC.dma_start(out=skv_o, in_=zs[:, OFF['kc']:OFF['kc'] + 768]), r=['zs'], w=['skv_o'])
                kb.op('dve', REC.tensor_tensor(out=gS[:], in0=zs[:, OFF['ng']:OFF['ng'] + 24], in1=vrS[0:NS, 1536:1560], op=ALU.add), r=['zs', 'vrS'], w=['gS'])
                kb.op('act', REC.activation(out=gS[:], in_=gS[:], func=AF.Sigmoid), r=['gS'], w=['gS'])
                kb.dma('sp', REC.dma_start(out=swk_o[:, 0:511, :], in_=wink_i[:, 1:512, :]), w=['swk_o'])
                kb.dma('sp', REC.dma_start(out=swv_o[:, 0:511, :], in_=winv_i[:, 1:512, :]), w=['swv_o'])
                kb.dma('sp', REC.dma_start(out=swk_o[:, 511, :], in_=zs[:, OFF['kw']:OFF['kw'] + 128]), r=['zs'], w=['swk_o'])
                kb.dma('sp', REC.dma_start(out=swv_o[:, 511, :], in_=zs[:, OFF['vw']:OFF['vw'] + 128]), r=['zs'], w=['swv_o'])
                glrS = sb(stSa, "glrS", [32, NS], BF16)
                kb.op('pool', REC.memset(glrS[:], 1.0), w=['glrS'])
                kb.op('pe', REC.transpose(out=PB[6][0:16, 0:NS], in_=zs[:, OFF['glr']:OFF['glr'] + 16], identity=identf[0:NS, 0:NS]), r=['zs', 'identf'], w=['pb6'])
                kb.op('dve', REC.tensor_copy(out=glrS[0:16, :], in_=PB[6][0:16, 0:NS]), r=['pb6'], w=['glrS'])
                kb.op('pe', REC.matmul(PB[6][0:NS, 256:512], lhsT=glrS[0:17, :], rhs=wgg_sb[0:17, :], start=True, stop=True), r=['glrS', 'wgg_sbS'], w=['pb6'])
                aS = sb(stSa, "aS", [NS, 256], F32)
                kb.op('act', REC.activation(out=aS[:], in_=PB[6][0:NS, 256:512], func=AF.Exp, scale=-1.0), r=['pb6'], w=['aS'])
                kb.op('act', REC.activation(out=aS[:], in_=aS[:], func=AF.Ln, bias=1.0), r=['aS'], w=['aS'])
                kb.op('act', REC.activation(out=aS[:], in_=aS[:], func=AF.Exp, scale=-1.0 / 16), r=['aS'], w=['aS'])
                gT = sb(stSa, "gT", [64, 3, 4, NS], F32)
                for wi, src in enumerate((aS[:], zs[:, OFF['gk']:OFF['gk'] + 256], zs[:, OFF['gq']:OFF['gq'] + 256])):
                    for h in range(4):
                        kb.op('pe', REC.transpose(out=PB[7][0:64, (wi * 4 + h) * NS:(wi * 4 + h + 1) * NS], in_=src[:, 64 * h:64 * h + 64], identity=identf[0:NS, 0:NS]),
                              r=['aS', 'zs', 'identf'], w=['pb7'])
                kb.op('act', REC.copy(out=gT[:].rearrange("p a b c -> p (a b c)"), in_=PB[7][0:64, 0:12 * NS]), r=['pb7'], w=['gT'])
                kb.op('dve', REC.tensor_scalar(out=gT[:, 2, :, :], in0=gT[:, 2, :, :], scalar1=0.125, scalar2=None, op0=ALU.mult), r=['gT'], w=['gT'])
                hS = sb(stSa, "hS", [64, NS * 4, 128], F32)
                kb.dma('sp', REC.dma_start(out=hS[:], in_=sgla_i.rearrange("s h k v -> k (s h) v")), w=['hS'])
                kvt = sb(stSa, "kvt", [64, 128], F32)
                for s_ in range(NS):
                    kb.op('pe', REC.matmul(PB[s_ % 2][0:64, :], lhsT=selS[:, s_, :], rhs=zs[:, OFF['gv']:OFF['gv'] + 512], start=True, stop=True),
                          r=['cS', 'zs'], w=['pb%d' % (s_ % 2)])
                    for h in range(4):
                        kb.op('dve', REC.tensor_scalar(out=kvt[:], in0=PB[s_ % 2][0:64, 128 * h:128 * h + 128], scalar1=gT[:, 1, h, s_:s_ + 1], scalar2=None, op0=ALU.mult),
                              r=['pb%d' % (s_ % 2), 'gT'], w=['kvt'])
                        kb.op('dve', REC.scalar_tensor_tensor(out=hS[:, s_ * 4 + h, :], in0=hS[:, s_ * 4 + h, :], scalar=gT[:, 0, h, s_:s_ + 1], in1=kvt[:],
                                                              op0=ALU.mult, op1=ALU.add), r=['hS', 'gT', 'kvt'], w=['hS'])
                kb.dma('sp', REC.dma_start(out=sgl_o.rearrange("s h k v -> k (s h) v"), in_=hS[:]), r=['hS'], w=['sgl_o'])
                QM = sb(stSa, "QM", [64, 4, NS, 16], F32)
                kb.op('dve', REC.tensor_tensor(out=QM[:], in0=gT[:, 2, :, :].unsqueeze(3).to_broadcast([64, 4, NS, 16]),
                                               in1=eye16.unsqueeze(1).to_broadcast([64, 4, NS, 16]), op=ALU.mult), r=['gT', 'cR'], w=['QM'])
                for h in range(4):
                    for s_ in range(NS):
                        kb.op('pe', REC.matmul(PB[3][0:NS, 128 * h:128 * h + 128], lhsT=QM[:, h, s_, :], rhs=hS[:, s_ * 4 + h, :], start=(s_ == 0), stop=(s_ == NS - 1)),
                              r=['QM', 'hS'], w=['pb3'])
                og4 = ogs[:].rearrange("p (h v) -> p h v", v=128)
                kb.op('act', REC.copy(out=ogs[:], in_=PB[3][0:NS, :]), r=['pb3'], w=['ogs'])
                sq = tmpS[0:NS, 0:512].rearrange("p (h v) -> p h v", v=128)
                kb.op('dve', REC.tensor_tensor(out=sq, in0=og4, in1=og4, op=ALU.mult), r=['ogs'], w=['tmpS'])
                kb.op('dve', REC.tensor_reduce(out=smS[0:NS, 8:12], in_=sq, axis=AX.X, op=ALU.add), r=['tmpS'], w=['smS8'])
                kb.op('dve', REC.tensor_scalar(out=smS[0:NS, 8:12], in0=smS[0:NS, 8:12], scalar1=1.0 / 128, scalar2=EPS, op0=ALU.mult, op1=ALU.add), r=['smS8'], w=['smS8'])
                kb.op('act', REC.activation(out=smS[0:NS, 8:12], in_=smS[0:NS, 8:12], func=AF.Sqrt), r=['smS8'], w=['smS8'])
                kb.op('dve', REC.reciprocal(out=smS[0:NS, 12:16], in_=smS[0:NS, 8:12]), r=['smS8'], w=['smS12'])
                kb.op('dve', REC.tensor_tensor(out=og4, in0=og4, in1=smS[0:NS, 12:16].unsqueeze(2).to_broadcast([NS, 4, 128]), op=ALU.mult), r=['ogs', 'smS12'], w=['ogs'])
                kb.op('dve', REC.tensor_tensor(out=ogs[:], in0=ogs[:], in1=vrS[0:NS, 1024:1536], op=ALU.mult), r=['ogs', 'vrS'], w=['ogs'])
                kb.op('act', REC.activation(out=tmpS[0:NS, 512:1024], in_=zs[:, OFF['gr']:OFF['gr'] + 512], func=AF.Silu), r=['zs'], w=['tmpS'])
                kb.op('dve', REC.tensor_tensor(out=ogs[:], in0=ogs[:], in1=tmpS[0:NS, 512:1024], op=ALU.mult), r=['ogs', 'tmpS'], w=['ogs'])
            kb.barrier()
            with contextlib.ExitStack() as stSa:
                sct = sb(stSa, "sct", [NS, 2 * 2 * DFF], F32)
                scT = sb(stSa, "scT", [128, 88, NS], F32)
                kb.dma('sp', REC.dma_start(out=sct[:], in_=sconv_i), w=['sct'])
                for c8 in range(0, 88, 32):
                    nn_ = min(32, 88 - c8)
                    for u_ in range(nn_):
                        kb.op('pe', REC.transpose(out=PB[5][:, u_ * NS:(u_ + 1) * NS], in_=sct[:, (c8 + u_) * 128:(c8 + u_ + 1) * 128], identity=identf[0:NS, 0:NS]),
                              r=['sct', 'identf'], w=['pb5'])
                    kb.op('act', REC.copy(out=scT[:, c8:c8 + nn_, :].rearrange("p a b -> p (a b)"), in_=PB[5][:, 0:nn_ * NS]), r=['pb5'], w=['scT'])
                kb.dma('sp', REC.dma_start(out=scv_o[:, 0, :], in_=sct[:, 2 * DFF:4 * DFF]), r=['sct'], w=['scv_o'])
                kb.dma('sp', REC.dma_start(out=sct_d, in_=scT[:].rearrange("p a b -> p (a b)")), r=['scT'], w=['sct_d'])
            kb.barrier()

            qsf = qs[:].rearrange("p h c -> p (h c)")

            def qrep(dst, ri):
                kb.op('pe', REC.matmul(PB[6][:, :], lhsT=RepAll[:, ri, :], rhs=qsf, start=True, stop=True), r=['cS', 'qs'], w=['pb6'])
                kb.op('act', REC.copy(out=dst[:], in_=PB[6][:, :]), r=['pb6'], w=['qr'])

            def newkey(br, kname, vname, acc_ps, accreg):
                kk = zs[:, OFF[kname]:OFF[kname] + 128].rearrange("p (g c) -> p g c", c=64).unsqueeze(2).to_broadcast([NS, 2, 4, 64])
                vv = zs[:, OFF[vname]:OFF[vname] + 128].rearrange("p (g c) -> p g c", c=64).unsqueeze(2).to_broadcast([NS, 2, 4, 64])
                t4 = tmpS[0:NS, 0:512].rearrange("p (g h c) -> p g h c", g=2, h=4)
                kb.op('dve', REC.tensor_tensor(out=t4, in0=qs[:].rearrange("p (g h) c -> p g h c", g=2), in1=kk, op=ALU.mult), r=['qs', 'zs'], w=['tmpS'])
                kb.op('dve', REC.tensor_reduce(out=smS[0:NS, 16:24], in_=tmpS[0:NS, 0:512].rearrange("p (a c) -> p a c", c=64), axis=AX.X, op=ALU.add), r=['tmpS'], w=['smS16'])
                kb.op('act', REC.activation(out=smS[0:NS, 16:24], in_=smS[0:NS, 16:24], func=AF.Exp), r=['smS16'], w=['smS16'])
                kb.op('dve', REC.tensor_tensor(out=smS[0:NS, 24:32], in0=smS[0:NS, 16:24], in1=acc_ps[:, 512:520], op=ALU.add), r=['smS16', accreg], w=['smS24'])
                kb.op('dve', REC.reciprocal(out=smS[0:NS, 24:32], in_=smS[0:NS, 24:32]), r=['smS24'], w=['smS24'])
                kb.op('dve', REC.tensor_tensor(out=t4, in0=vv, in1=smS[0:NS, 16:24].rearrange("p (g h) -> p g h", g=2).unsqueeze(3).to_broadcast([NS, 2, 4, 64]), op=ALU.mult),
                      r=['zs', 'smS16'], w=['tmpS'])
                kb.op('dve', REC.tensor_tensor(out=tmpS[0:NS, 0:512], in0=tmpS[0:NS, 0:512], in1=acc_ps[:, 0:512], op=ALU.add), r=['tmpS', accreg], w=['tmpS'])
                kb.op('dve', REC.tensor_tensor(out=oall[:, br, :, :], in0=tmpS[0:NS, 0:512].rearrange("p (a c) -> p a c", c=64),
                                               in1=smS[0:NS, 24:32].unsqueeze(2).to_broadcast([NS, 8, 64]), op=ALU.mult), r=['tmpS', 'smS24'], w=['oall%d' % br])

            def attend(kbuf, vbuf, qr, nrow, rowstride_g, mask, bsel_list, acc_bank, first, last, regs, ghs=range(8)):
                e = tmpS[:, 0:8 * nrow].rearrange("p (a r) -> p a r", r=nrow)
                pr = tmpS[:, 4096:4096 + nrow * 64]
                for gh in ghs:
                    g_ = gh // 4
                    kb.op('dve', REC.tensor_tensor(out=pr.rearrange("p (r c) -> p r c", c=64), in0=kbuf[:, :, 64 * g_:64 * g_ + 64],
                                                   in1=qr[:, 64 * gh:64 * gh + 64].unsqueeze(1).to_broadcast([128, nrow, 64]), op=ALU.mult), r=regs + ['qr'], w=['tmpS'])
                    kb.op('dve', REC.tensor_reduce(out=e[:, gh, :], in_=pr.rearrange("p (r c) -> p r c", c=64), axis=AX.X, op=ALU.add), r=['tmpS'], w=['tmpS'])
                g0_, g1_ = ghs[0], ghs[-1] + 1
                esl = e[:, g0_:g1_, :]
                if mask is not None:
                    kb.op('dve', REC.tensor_tensor(out=esl, in0=esl, in1=mask.unsqueeze(1).to_broadcast([128, g1_ - g0_, nrow]), op=ALU.add), r=['tmpS', 'cR'], w=['tmpS'])
                kb.op('act', REC.activation(out=esl, in_=esl, func=AF.Exp), r=['tmpS'], w=['tmpS'])
                ov = sb_ov
                for gh in ghs:
                    g_ = gh // 4
                    kb.op('dve', REC.tensor_tensor(out=pr.rearrange("p (r c) -> p r c", c=64), in0=vbuf[:, :, 64 * g_:64 * g_ + 64],
                                                   in1=e[:, gh, :].unsqueeze(2).to_broadcast([128, nrow, 64]), op=ALU.mult), r=regs + ['tmpS'], w=['tmpS'])
                    kb.op('dve', REC.tensor_reduce(out=ov[:, 64 * gh:64 * gh + 64], in_=pr.rearrange("p (r c) -> p c r", c=64), axis=AX.X, op=ALU.add), r=['tmpS'], w=['ovS'])
                kb.op('dve', REC.tensor_reduce(out=ov[:, 512 + g0_:512 + g1_], in_=esl, axis=AX.X, op=ALU.add), r=['tmpS'], w=['ovS'])
                return ov

            sb_ov = sb(stS, "sb_ov", [128, 520], F32)
            kb.op('pool', REC.memset(sb_ov[:], 0.0), w=['ovS'])
            qr = sb(stS, "qr", [128, 512], F32)
            with contextlib.ExitStack() as stSb:
                wk = sb(stSb, "wk", [128, 64, 128], F32)
                wv = sb(stSb, "wv", [128, 64, 128], F32)
                kb.dma('sp', REC.dma_start(out=wk[:], in_=wink_i.rearrange("s (c r) f -> (s c) r f", c=8)), w=['wk'])
                kb.dma('sp', REC.dma_start(out=wv[:], in_=winv_i.rearrange("s (c r) f -> (s c) r f", c=8)), w=['wv'])
                qrep(qr, 0)
                ov = attend(wk, wv, qr, 64, None, maskW, None, 3, True, True, ['wk', 'wv'])
                kb.op('pe', REC.matmul(PB[3][0:NS, 0:512], lhsT=BAll[:, 0, :], rhs=ov[:, 0:512], start=True, stop=True), r=['ovS', 'cR'], w=['pb3'])
                kb.op('pe', REC.matmul(PB[2][0:NS, 0:8], lhsT=BAll[:, 0, :], rhs=ov[:, 512:520], start=True, stop=True), r=['ovS', 'cR'], w=['pb2'])
                accw = sb(stSb, "accw", [NS, 520], F32)
                kb.op('act', REC.copy(out=accw[:, 0:512], in_=PB[3][0:NS, 0:512]), r=['pb3'], w=['accw'])
                kb.op('act', REC.copy(out=accw[:, 512:520], in_=PB[2][0:NS, 0:8]), r=['pb2'], w=['accw'])
                newkey(2, 'kw', 'vw', accw, 'accw')
            kb.barrier()
            with contextlib.ExitStack() as stSc:
                ptab_sb = sb(stSc, "ptab_sb", [128, NGR], I32)
                idx4 = sb(stSc, "idx4", [128, NGR], I32)
                kb.dma('sp', REC.dma_start(out=ptab_sb[:], in_=ptab_i), w=['ptab_sb'])
                kb.op('dve', REC.tensor_scalar(out=idx4[:], in0=ptab_sb[:], scalar1=4.0, scalar2=None, op0=ALU.mult), r=['ptab_sb'], w=['idx4'])
                pg = [sb(stSc, "pg%d" % i, [128, 32, 2, 64], F32) for i in range(2)]
                kcP = sb(stSc, "kcP", [128, 2, 2, 128], F32)
                scP = sb(stSc, "scP", [128, 2, 8], F32)
                part = sb(stSc, "part", [128, 128], F32)
                it = 0
                for r_ in range(NGR):
                    qrep(qr, 1 + r_)
                    for ti in range(2):
                        pool_ap = pools[ti].ap().rearrange("n (c f) -> (n c) f", c=4)
                        for c in range(4):
                            b_ = pg[it % 2]
                            breg = 'pg%d' % (it % 2)
                            it += 1
                            kb.dma('pool', REC.indirect_dma_start(out=b_[:].rearrange("p a b c -> p (a b c)"), out_offset=None, in_=pool_ap, element_offset=c * 4096,
                                                                  in_offset=bass.IndirectOffsetOnAxis(ap=idx4[:, r_:r_ + 1], axis=0)), r=['idx4'], w=[breg])
                            m_ = c // 2
                            wsl = wrow[:, ti, (c % 2) * 32:(c % 2) * 32 + 32, :].unsqueeze(3).to_broadcast([128, 32, 2, 64])
                            pr4 = tmpS[:, 0:4096].rearrange("p (a b c) -> p a b c", b=2, c=64)
                            kb.op('dve', REC.tensor_tensor(out=pr4, in0=b_[:], in1=wsl, op=ALU.mult), r=[breg, 'cR'], w=['tmpS'])
                            dst = kcP[:, ti, m_, :] if c % 2 == 0 else part[:]
                            kb.op('dve', REC.tensor_reduce(out=dst, in_=tmpS[:, 0:4096].rearrange("p (a f) -> p f a", f=128), axis=AX.X, op=ALU.add), r=['tmpS'], w=['kcP' if c % 2 == 0 else 'part'])
                            if c % 2 == 1:
                                kb.op('dve', REC.tensor_tensor(out=kcP[:, ti, m_, :], in0=kcP[:, ti, m_, :], in1=part[:], op=ALU.add), r=['kcP', 'part'], w=['kcP'])
                    for gh in range(8):
                        g_ = gh // 4
                        pr3 = tmpS[:, 0:128].rearrange("p (m c) -> p m c", c=64)
                        kb.op('dve', REC.tensor_tensor(out=pr3, in0=kcP[:, 0, :, 64 * g_:64 * g_ + 64], in1=qr[:, 64 * gh:64 * gh + 64].unsqueeze(1).to_broadcast([128, 2, 64]), op=ALU.mult),
                              r=['kcP', 'qr'], w=['tmpS'])
                        kb.op('dve', REC.tensor_reduce(out=scP[:, :, gh], in_=pr3, axis=AX.X, op=ALU.add), r=['tmpS'], w=['scP'])
                    kb.dma('sp', REC.dma_start(out=scmp_d[r_ * 128:(r_ + 1) * 128, :], in_=scP[:].rearrange("p a b -> p (a b)")), r=['scP'], w=['scmp_d'])
                    kb.dma('sp', REC.dma_start(out=vc_d[r_ * 128:(r_ + 1) * 128, :], in_=kcP[:, 1, :, :].rearrange("p a b -> p (a b)")), r=['kcP'], w=['vc_d'])
                scm = sb(stSc, "scm", [NS, NPG * 16], F32)
                vcs_ = sb(stSc, "vcs_", [NS, NPG * 256], F32)
                kb.dma('sp', REC.dma_start(out=scm[:], in_=scmp_d.rearrange("(s p) f -> s (p f)", p=NPG)), r=['scmp_d'], w=['scm'])
                kb.dma('sp', REC.dma_start(out=vcs_[:], in_=vc_d.rearrange("(s p) f -> s (p f)", p=NPG)), r=['vc_d'], w=['vcs_'])
                sview = scm[:].rearrange("p (b a) -> p a b", a=8)
                pS = sb(stSc, "pS", [NS, 8, NBP], F32)
                kb.op('dve', REC.tensor_reduce(out=smS[0:NS, 32:40], in_=sview, axis=AX.X, op=ALU.max), r=['scm'], w=['smS32'])
                kb.op('dve', REC.tensor_scalar(out=smS[0:NS, 32:40], in0=smS[0:NS, 32:40], scalar1=-1.0, scalar2=None, op0=ALU.mult), r=['smS32'], w=['smS32'])
                for gh in range(8):
                    kb.op('act', REC.activation(out=pS[:, gh, :], in_=sview[:, gh, :], func=AF.Exp, bias=smS[0:NS, 32 + gh:33 + gh], accum_out=smS[0:NS, 40 + gh:41 + gh]),
                          r=['scm', 'smS32'], w=['pS', 'smS40'])
                kb.op('dve', REC.reciprocal(out=smS[0:NS, 48:56], in_=smS[0:NS, 40:48]), r=['smS40'], w=['smS48'])
                kb.op('dve', REC.tensor_tensor(out=pS[:], in0=pS[:], in1=smS[0:NS, 48:56].unsqueeze(2).to_broadcast([NS, 8, NBP]), op=ALU.mult), r=['pS', 'smS48'], w=['pS'])
                kb.op('dve', REC.tensor_reduce(out=scS[:], in_=pS[:].rearrange("p (g h) b -> p g b h", g=2), axis=AX.X, op=ALU.add), r=['pS'], w=['scS'])
                vview = vcs_[:].rearrange("p (b g c) -> p b g c", g=2, c=64)
                for gh in range(8):
                    g_ = gh // 4
                    pr3 = tmpS[0:NS, 0:NBP * 64].rearrange("p (b c) -> p b c", c=64)
                    kb.op('dve', REC.tensor_tensor(out=pr3, in0=vview[:, :, g_, :], in1=pS[:, gh, :].unsqueeze(2).to_broadcast([NS, NBP, 64]), op=ALU.mult), r=['vcs_', 'pS'], w=['tmpS'])
                    kb.op('dve', REC.tensor_reduce(out=oall[:, 0, gh, :], in_=tmpS[0:NS, 0:NBP * 64].rearrange("p (b c) -> p c b", c=64), axis=AX.X, op=ALU.add), r=['tmpS'], w=['oall0'])
            kb.barrier()
            with contextlib.ExitStack() as stSd:
                ksg = sb(stSd, "ksg", [128, 64, 128], F32)
                vsg = sb(stSd, "vsg", [128, 64, 128], F32)
                accs = sb(stSd, "accs", [NS, 520], F32)
                m8s = sb(stSd, "m8s", [NS, 16], F32)
                i8s = sb(stSd, "i8s", [NS, 16], mybir.dt.uint32)
                idf = sb(stSd, "idf", [NS, 16], F32)
                mm_ = sb(stSd, "mm_", [NS, 16], F32)
                ohs = sb(stSd, "ohs", [NS, 16, NBP], F32)
                ptb = sb(stSd, "ptb", [NS, NPG, 2], F32)
                idi = sb(stSd, "idi", [NS, 16], I32)
                idp = sb(stSd, "idp", [128, 2], I32)
                sc2s = sb(stSd, "sc2s", [NS, NBP], F32)
                pti = sb(stSd, "pti", [NS, NPG], I32)
                ptf = sb(stSd, "ptf", [NS, NPG], F32)
                kb.dma('sp', REC.dma_start(out=pti[:], in_=ptabs_i), w=['pti'])
                kb.op('dve', REC.tensor_copy(out=ptf[:], in_=pti[:]), r=['pti'], w=['ptf'])
                kb.op('dve', REC.tensor_scalar(out=ptb[:, :, 0], in0=ptf[:], scalar1=2.0, scalar2=None, op0=ALU.mult), r=['ptf'], w=['ptb'])
                kb.op('dve', REC.tensor_scalar(out=ptb[:, :, 1], in0=ptf[:], scalar1=2.0, scalar2=1.0, op0=ALU.mult, op1=ALU.add), r=['ptf'], w=['ptb'])
                iotp = cS[:, 16 + 1024 + 128 * NREP + 16:16 + 1024 + 128 * NREP + 16 + 128]
                for g_ in range(2):
                    kb.op('pool', REC.memset(scS[:, g_, 0:1], 5.0), r=['scS'], w=['scS'])
                    kb.op('pool', REC.memset(scS[:, g_, NBP - 1:NBP], 5.0), r=['scS'], w=['scS'])
                    kb.op('dve', REC.max(out=m8s[:, 0:8], in_=scS[:, g_, :]), r=['scS'], w=['m8s'])
                    kb.op('dve', REC.max_index(out=i8s[:, 0:8], in_max=m8s[:, 0:8], in_values=scS[:, g_, :]), r=['scS', 'm8s'], w=['i8s'])
                    kb.op('dve', REC.match_replace(out=sc2s[:], in_to_replace=m8s[:, 0:8], in_values=scS[:, g_, :], imm_value=-2.0), r=['scS', 'm8s'], w=['sc2s'])
                    kb.op('dve', REC.max(out=m8s[:, 8:16], in_=sc2s[:]), r=['sc2s'], w=['m8s'])
                    kb.op('dve', REC.max_index(out=i8s[:, 8:16], in_max=m8s[:, 8:16], in_values=sc2s[:]), r=['sc2s', 'm8s'], w=['i8s'])
                    kb.op('dve', REC.tensor_copy(out=idf[:], in_=i8s[:]), r=['i8s'], w=['idf'])
                    kb.op('dve', REC.tensor_tensor(out=ohs[:], in0=idf[:].unsqueeze(2).to_broadcast([NS, 16, NBP]), in1=iotp[:, 0:NBP].unsqueeze(1).to_broadcast([NS, 16, NBP]), op=ALU.is_equal),
                          r=['idf', 'cS'], w=['ohs'])
                    kb.op('dve', REC.tensor_tensor(out=ohs[:], in0=ohs[:], in1=ptb[:].rearrange("p a b -> p (a b)").unsqueeze(1).to_broadcast([NS, 16, NBP]), op=ALU.mult), r=['ohs', 'ptb'], w=['ohs'])
                    kb.op('dve', REC.tensor_reduce(out=idf[:], in_=ohs[:], axis=AX.X, op=ALU.add), r=['ohs'], w=['idf'])
                    kb.op('dve', REC.tensor_copy(out=idi[:], in_=idf[:]), r=['idf'], w=['idi'])
                    kb.dma('sp', REC.dma_start(out=idx_d[g_:g_ + 1, :].rearrange("o (s k) -> (o s) k", k=16), in_=idi[:]), r=['idi'], w=['idx_d'])
                    kb.dma('sp', REC.dma_start(out=idp[:], in_=idx_d[g_:g_ + 1, :].rearrange("o (h s k) -> (o s k) h", h=2, k=16), allow_slow_non_contiguous=True), r=['idx_d'], w=['idp'])
                    for half in range(2):
                        for buf, pl_, breg in ((ksg, pools[2], 'ksg'), (vsg, pools[3], 'vsg')):
                            kb.dma('pool', REC.indirect_dma_start(out=buf[:].rearrange("p a b -> p (a b)"), out_offset=None, in_=pl_.ap().rearrange("n (c f) -> (n c) f", c=2),
                                                                  in_offset=bass.IndirectOffsetOnAxis(ap=idp[:, half:half + 1], axis=0)), r=['idp'], w=[breg])
                        qrep(qr, 1 + NGR + half)
                        ov = attend(ksg, vsg, qr, 64, None, None, None, 3, True, True, ['ksg', 'vsg'], ghs=range(4 * g_, 4 * g_ + 4))
                        kb.op('dve', REC.tensor_scalar(out=ov[:], in0=ov[:], scalar1=slotm, scalar2=None, op0=ALU.mult), r=['ovS', 'cR'], w=['ovS'])
                        kb.op('pe', REC.matmul(PB[3][0:NS, 0:512], lhsT=BAll[:, 1 + NGR + half, :], rhs=ov[:, 0:512], start=(half == 0), stop=(half == 1)), r=['ovS', 'cR'], w=['pb3'])
                        kb.op('pe', REC.matmul(PB[2][0:NS, 0:8], lhsT=BAll[:, 1 + NGR + half, :], rhs=ov[:, 512:520], start=(half == 0), stop=(half == 1)), r=['ovS', 'cR'], w=['pb2'])
                    kb.op('act', REC.copy(out=accs[:, 256 * g_:256 * g_ + 256], in_=PB[3][0:NS, 256 * g_:256 * g_ + 256]), r=['pb3'], w=['accs'])
                    kb.op('act', REC.copy(out=accs[:, 512 + 4 * g_:516 + 4 * g_], in_=PB[2][0:NS, 4 * g_:4 * g_ + 4]), r=['pb2'], w=['accs'])
                newkey(1, 'ks', 'vs', accs, 'accs')
            kb.barrier()
            gv3 = gS[:].rearrange("p (h c) -> p h c", c=3)
            for br in range(3):
                kb.op('dve', REC.tensor_tensor(out=oall[:, br, :, :], in0=oall[:, br, :, :], in1=gv3[:, :, br:br + 1].to_broadcast([NS, 8, 64]), op=ALU.mult),
                      r=['oall%d' % br, 'gS'], w=['oall%d' % br])
            kb.op('dve', REC.tensor_tensor(out=oall[:, 0, :, :], in0=oall[:, 0, :, :], in1=oall[:, 1, :, :], op=ALU.add), r=['oall0', 'oall1'], w=['oall0'])
            kb.op('dve', REC.tensor_tensor(out=oall[:, 0, :, :], in0=oall[:, 0, :, :], in1=oall[:, 2, :, :], op=ALU.add), r=['oall0', 'oall2'], w=['oall0'])
            ocs = sb(stS, "ocs", [NS, D], BF16)
            kb.op('act', REC.copy(out=ocs[:, 0:512], in_=ogs[:]), r=['ogs'], w=['ocs'])
            kb.op('act', REC.copy(out=ocs[:, 512:1024], in_=oall[:, 0, :, :].rearrange("p a b -> p (a b)")), r=['oall0'], w=['ocs'])
            kb.dma('sp', REC.dma_start(out=oscr_s, in_=ocs[:]), r=['ocs'], w=['oscr_s'])
          kb.barrier()
        NTOK = NQ * 128 + (NS if do_sample else 0)
        h1_d = nc.dram_tensor("h1_d", [NTOK, D], F32, kind="Internal").ap()
        hnT_d = nc.dram_tensor("hnT_d", [128, 8, NTOK], BF16, kind="Internal").ap()
        tiles = [(tq * 128, 128, 'halo' if tq == 0 else 'own', tq) for tq in range(NQ)]
        if do_sample:
            tiles.append((NQ * 128, NS, 'sample', None))

        def castload(st_, name, src_ap, rows_k, ncols):
            dst = sb(st_, name, [128, rows_k, ncols], BF16)
            for k in range(rows_k):
                for c0 in range(0, ncols, 2048):
                    c1 = min(ncols, c0 + 2048)
                    kb.dma('pool', REC.dma_start(out=dst[:, k, c0:c1], in_=src_ap[k * 128:(k + 1) * 128, c0:c1]), w=[name])
            return dst

        def rmsn(ss, junk, src, gvec, dst, srcreg, dstreg, P):
            kb.op('act', REC.activation(out=junk[0:P, :], in_=src, func=AF.Square, accum_out=ss[0:P, 0:1]), r=[srcreg], w=['junkB', 'ssB'])
            kb.op('dve', REC.tensor_scalar(out=ss[0:P, 1:2], in0=ss[0:P, 0:1], scalar1=1.0 / D, scalar2=EPS, op0=ALU.mult, op1=ALU.add), r=['ssB'], w=['ssB1'])
            kb.op('act', REC.activation(out=ss[0:P, 3:4], in_=ss[0:P, 1:2], func=AF.Sqrt), r=['ssB1'], w=['ssB3'])
            kb.op('dve', REC.reciprocal(out=ss[0:P, 2:3], in_=ss[0:P, 3:4]), r=['ssB3'], w=['ssB2'])
            kb.op('dve', REC.scalar_tensor_tensor(out=dst, in0=src, scalar=ss[0:P, 2:3], in1=gvec[0:P, :], op0=ALU.mult, op1=ALU.mult),
                  r=[srcreg, 'ssB2', 'gv'], w=[dstreg])

        def tr8q(src_bf, dstT, srcreg, dstreg, P, nk=8, par=0):
            for k in range(nk):
                kb.op('pe', REC.transpose(out=pbf(4)[:, k * P:(k + 1) * P], in_=src_bf[0:P, k * 128:(k + 1) * 128], identity=identb[0:P, 0:P]),
                      r=[srcreg, 'identb'], w=['pb4'])
            kb.op('act', REC.copy(out=dstT[:, 0:nk, 0:P], in_=pbf(4)[:, 0:nk * P].rearrange("p (a b) -> p a b", b=P)), r=['pb4'], w=[dstreg])

        with contextlib.ExitStack() as stB:
            w_o_sb = castload(stB, 'w_o_sb', w_o, 8, D)
            gv1 = sb(stB, "gv1", [128, D], F32)
            kb.dma('sp', REC.dma_start(out=gv1[:], in_=vecsB[:, 0:1024].partition_broadcast(128)), w=['gv'])
            ss = sb(stB, "ssB", [128, 8], F32)
            junk = sb(stB, "junkB", [128, D], BF16)
            xb2 = [sb(stB, "xb2_%d" % i, [128, D], F32) for i in range(2)]
            ocat = [sb(stB, "ocat%d" % i, [128, D], BF16) for i in range(2)]
            ocT = [sb(stB, "ocT%d" % i, [128, 8, 128], BF16) for i in range(2)]
            h1 = [sb(stB, "h1_%d" % i, [128, D], F32) for i in range(2)]
            hnb = [sb(stB, "hnb%d" % i, [128, D], BF16) for i in range(2)]
            hnT = [sb(stB, "hnT%d" % i, [128, 8, 128], BF16) for i in range(2)]
            for ti, (t0, P, mode, tq) in enumerate(tiles):
                pr_ = ti % 2
                kb.alias = {r_: r_ + '_%d' % pr_ for r_ in ('xb2', 'ocat', 'ocT', 'h1', 'hnb', 'hnT')}
                x_ap = xs_i if mode == 'sample' else xl[(HALO + tq) * 128:(HALO + tq + 1) * 128, :]
                oc_ap = oscr_s if mode == 'sample' else oscr[tq * 128:(tq + 1) * 128, :]
                kb.dma('sp', REC.dma_start(out=xb2[pr_][0:P, :], in_=x_ap), w=['xb2'])
                kb.dma('sp', REC.dma_start(out=ocat[pr_][0:P, :], in_=oc_ap), r=['oscr', 'oscr_s'], w=['ocat'])
                tr8q(ocat[pr_], ocT[pr_], 'ocat', 'ocT', P)
                for half in range(2):
                    for k in range(8):
                        kb.op('pe', REC.matmul(PB[half][0:P, :], lhsT=ocT[pr_][:, k, 0:P], rhs=w_o_sb[:, k, half * 512:(half + 1) * 512],
                                               start=(k == 0), stop=(k == 7)), r=['ocT', 'w_o_sb'], w=['pb%d' % half])
                    kb.op('dve', REC.tensor_tensor(out=h1[pr_][0:P, half * 512:(half + 1) * 512], in0=PB[half][0:P, :], in1=xb2[pr_][0:P, half * 512:(half + 1) * 512], op=ALU.add),
                          r=['pb%d' % half, 'xb2'], w=['h1'])
                kb.dma('sp', REC.dma_start(out=h1_d[t0:t0 + P, :], in_=h1[pr_][0:P, :]), r=['h1'], w=['h1_d'])
                rmsn(ss, junk, h1[pr_][0:P, :], gv1, hnb[pr_][0:P, :], 'h1', 'hnb', P)
                tr8q(hnb[pr_], hnT[pr_], 'hnb', 'hnT', P)
                kb.dma('sp', REC.dma_start(out=hnT_d[:, :, t0:t0 + P], in_=hnT[pr_][:, :, 0:P]), r=['hnT'], w=['hnT_d'])
            kb.alias = {}
        kb.barrier()
        with contextlib.ExitStack() as stB:
            w_up_sb = castload(stB, 'w_up_sb', w_up, 8, 2 * DFF)
            w_dn_sb = castload(stB, 'w_dn_sb', w_down, 22, D)
            cw = sb(stB, "cw", [128, 44, 4], F32)
            kb.dma('sp', REC.dma_start(out=cw[:].rearrange("p a b -> p (a b)"), in_=convw), w=['cw'])
            carry = sb(stB, "carry", [128, 44, 2], F32)
            kb.op('pool', REC.memset(carry[:], 0.0), w=['carry'])
            GT = 512
            hg = sb(stB, "hg", [128, 8, GT], BF16)
            uext2 = [sb(stB, "uext%d" % i, [128, 2, GT + 2], F32) for i in range(2)]
            ca2 = [sb(stB, "ca%d" % i, [128, 2, GT], F32) for i in range(2)]
            actT = sb(stB, "actT", [128, 22, GT], BF16)
            hio = [sb(stB, "hio%d" % i, [128, D], F32) for i in range(2)]
            sc0T = sb(stB, "sc0T", [128, 44, NS], F32) if do_sample else None
            sc1T = sb(stB, "sc1T", [128, 44, NS], F32) if do_sample else None
            if do_sample:
                kb.dma('sp', REC.dma_start(out=sc0T[:].rearrange("p a b -> p (a b)"), in_=sct_d[:, 0:44 * NS]), r=['sct_d'], w=['sc0T'])
                kb.dma('sp', REC.dma_start(out=sc1T[:].rearrange("p a b -> p (a b)"), in_=sct_d[:, 44 * NS:88 * NS]), r=['sct_d'], w=['sc1T'])
            groups = [(0, 128, 'halo')] + [(128 + gi * GT, min(GT, NQ * 128 - 128 - gi * GT), 'own') for gi in range((NQ * 128 - 128 + GT - 1) // GT)]
            if do_sample:
                groups.append((NQ * 128, NS, 'sample'))
            for (t0, W, mode) in groups:
                is_halo = (mode == 'halo')
                c0 = W - 2 if is_halo else 0
                ncol = W - c0
                kb.dma('sp', REC.dma_start(out=hg[:, :, 0:W], in_=hnT_d[:, :, t0:t0 + W]), r=['hnT_d'], w=['hg'])
                for i in range(22):
                    bk = (2, 3) if i % 2 == 0 else (5, 6)
                    uext = uext2[i % 2]
                    ca = ca2[i % 2]
                    UR = 'uext%d' % (i % 2)
                    CR = 'ca%d' % (i % 2)
                    for ab in range(2):
                        col = ab * DFF + i * 128
                        for k in range(8):
                            kb.op('pe', REC.matmul(PB[bk[ab]][:, 0:ncol], lhsT=w_up_sb[:, k, col:col + 128], rhs=hg[:, k, c0:W],
                                                   start=(k == 0), stop=(k == 7)), r=['hg', 'w_up_sb'], w=['pb%d' % bk[ab]])
                    if is_halo:
                        for ab in range(2):
                            kb.op('act', REC.copy(out=carry[:, ab * 22 + i, :], in_=PB[bk[ab]][:, 0:2]), r=['pb%d' % bk[ab]], w=['carry'])
                        continue
                    for ab in range(2):
                        ci = ab * 22 + i
                        if mode == 'sample':
                            kb.op('act', REC.activation(out=ca[:, ab, 0:W], in_=sc0T[:, ci, :], func=AF.Identity, scale=cw[:, ci, 0:1], bias=cw[:, ci, 3:4]),
                                  r=['sc0T', 'cw', 'actT'], w=[CR])
                            kb.op('dve', REC.scalar_tensor_tensor(out=ca[:, ab, 0:W], in0=sc1T[:, ci, :], scalar=cw[:, ci, 1:2], in1=ca[:, ab, 0:W],
                                                                  op0=ALU.mult, op1=ALU.add), r=['sc1T', 'cw', CR], w=[CR])
                            kb.op('dve', REC.scalar_tensor_tensor(out=ca[:, ab, 0:W], in0=PB[bk[ab]][:, 0:W], scalar=cw[:, ci, 2:3], in1=ca[:, ab, 0:W],
                                                                  op0=ALU.mult, op1=ALU.add), r=['pb%d' % bk[ab], 'cw', CR], w=[CR])
                            continue
                        kb.op('act', REC.copy(out=uext[:, ab, 0:2], in_=carry[:, ci, :]), r=['carry', CR], w=[UR])
                        kb.op('act', REC.copy(out=uext[:, ab, 2:W + 2], in_=PB[bk[ab]][:, 0:W]), r=['pb%d' % bk[ab], CR], w=[UR])
                        kb.op('pool', REC.tensor_copy(out=carry[:, ci, :], in_=uext[:, ab, W:W + 2]), r=[UR], w=['carry'])
                        kb.op('act', REC.activation(out=ca[:, ab, 0:W], in_=uext[:, ab, 0:W], func=AF.Identity, scale=cw[:, ci, 0:1], bias=cw[:, ci, 3:4]),
                              r=[UR, 'cw', 'actT'], w=[CR])
                        kb.op('dve', REC.scalar_tensor_tensor(out=ca[:, ab, 0:W], in0=uext[:, ab, 1:W + 1], scalar=cw[:, ci, 1:2], in1=ca[:, ab, 0:W],
                                                              op0=ALU.mult, op1=ALU.add), r=[UR, 'cw', CR], w=[CR])
                        kb.op('dve', REC.scalar_tensor_tensor(out=ca[:, ab, 0:W], in0=uext[:, ab, 2:W + 2], scalar=cw[:, ci, 2:3], in1=ca[:, ab, 0:W],
                                                              op0=ALU.mult, op1=ALU.add), r=[UR, 'cw', CR], w=[CR])
                    kb.op('act', REC.activation(out=ca[:, 0, 0:W], in_=ca[:, 0, 0:W], func=AF.Silu), r=[CR], w=[CR])
                    kb.op('dve', REC.tensor_tensor(out=actT[:, i, 0:W], in0=ca[:, 0, 0:W], in1=ca[:, 1, 0:W], op=ALU.mult), r=[CR], w=['actT'])
                if is_halo:
                    continue
                if mode == 'sample':
                    for ci_, c0_ in enumerate(range(0, 2 * DFF, 512)):
                        for k in range(8):
                            kb.op('pe', REC.matmul(PB[2][0:W, :], lhsT=hg[:, k, 0:W], rhs=w_up_sb[:, k, c0_:c0_ + 512], start=(k == 0), stop=(k == 7)),
                                  r=['hg', 'w_up_sb'], w=['pb2'])
                        kb.op('act', REC.copy(out=hio[ci_ % 2][0:W, 0:512], in_=PB[2][0:W, :]), r=['pb2'], w=['hio%d' % (ci_ % 2)])
                        kb.dma('sp', REC.dma_start(out=scv_o[:, 1, c0_:c0_ + 512], in_=hio[ci_ % 2][0:W, 0:512]), r=['hio%d' % (ci_ % 2)], w=['scv_o'])
                for st_ in range(0, W, 128):
                    P = min(128, W - st_)
                    hb = hio[(st_ // 128) % 2]
                    hr = 'hio%d' % ((st_ // 128) % 2)
                    kb.dma('sp', REC.dma_start(out=hb[0:P, :], in_=h1_d[t0 + st_:t0 + st_ + P, :]), r=['h1_d'], w=[hr])
                    for half in range(2):
                        for i in range(22):
                            kb.op('pe', REC.matmul(PB[half][0:P, :], lhsT=actT[:, i, st_:st_ + P], rhs=w_dn_sb[:, i, half * 512:(half + 1) * 512],
                                                   start=(i == 0), stop=(i == 21)), r=['actT', 'w_dn_sb'], w=['pb%d' % half])
                        kb.op('dve', REC.tensor_tensor(out=hb[0:P, half * 512:(half + 1) * 512], in0=PB[half][0:P, :], in1=hb[0:P, half * 512:(half + 1) * 512], op=ALU.add),
                              r=['pb%d' % half, hr], w=[hr])
                    kb.dma('sp', REC.dma_start(out=h1_d[t0 + st_:t0 + st_ + P, :], in_=hb[0:P, :]), r=[hr], w=['h1_d'])
            for t_ in range(2):
                kb.dma('sp', REC.dma_start(out=cv_o[t_:t_ + 1, :].rearrange("o (c p) -> p (o c)", p=128), in_=carry[:, :, t_],
                                           allow_slow_non_contiguous=True), r=['carry'], w=['cv_o'])
        kb.barrier()
        with contextlib.ExitStack() as stB:
            w_pg_sb = castload(stB, 'w_pg_sb', w_pg, 8, D)
            w_ple_sb = castload(stB, 'w_ple_sb', w_ple, 2, D)
            gv2 = sb(stB, "gv2", [128, 2048], F32)
            kb.dma('sp', REC.dma_start(out=gv2[:], in_=vecsB[:, 1024:3072].partition_broadcast(128)), w=['gv'])
            ss = sb(stB, "ssB3", [128, 8], F32)
            junk = sb(stB, "junkB3", [128, D], BF16)
            h2 = [sb(stB, "h2_%d" % i, [128, D], F32) for i in range(2)]
            hnb = [sb(stB, "hnc%d" % i, [128, D], BF16) for i in range(2)]
            hnT = [sb(stB, "hnU%d" % i, [128, 8, 128], BF16) for i in range(2)]
            pbb = [sb(stB, "pbb%d" % i, [128, 256], BF16) for i in range(2)]
            peT = [sb(stB, "peT%d" % i, [128, 2, 128], BF16) for i in range(2)]
            gsg = [sb(stB, "gsg%d" % i, [128, D], F32) for i in range(2)]
            yb = [sb(stB, "yb%d" % i, [128, D], F32) for i in range(2)]
            for ti, (t0, P, mode, tq) in enumerate(tiles):
                if mode == 'halo':
                    continue
                pr_ = ti % 2
                kb.alias = {r_: r_ + '_%d' % pr_ for r_ in ('h2', 'hnb', 'hnT', 'pbb', 'peT', 'gsg', 'yb')}
                pe_ap = ps_i if mode == 'sample' else pl[(tq - 1) * 128:tq * 128, :]
                y_ap = ys_o if mode == 'sample' else y_o[(tq - 1) * 128:tq * 128, :]
                kb.dma('sp', REC.dma_start(out=h2[pr_][0:P, :], in_=h1_d[t0:t0 + P, :]), r=['h1_d'], w=['h2'])
                kb.dma('pool', REC.dma_start(out=pbb[pr_][0:P, :], in_=pe_ap), w=['pbb'])
                rmsn(ss, junk, h2[pr_][0:P, :], gv2[:, 0:1024], hnb[pr_][0:P, :], 'h2', 'hnb', P)
                tr8q(hnb[pr_], hnT[pr_], 'hnb', 'hnT', P)
                tr8q(pbb[pr_], peT[pr_], 'pbb', 'peT', P, nk=2)
                for half in range(2):
                    for k in range(8):
                        kb.op('pe', REC.matmul(PB[half][0:P, :], lhsT=hnT[pr_][:, k, 0:P], rhs=w_pg_sb[:, k, half * 512:(half + 1) * 512],
                                               start=(k == 0), stop=(k == 7)), r=['hnT', 'w_pg_sb'], w=['pb%d' % half])
                    kb.op('act', REC.activation(out=gsg[pr_][0:P, half * 512:(half + 1) * 512], in_=PB[half][0:P, :], func=AF.Sigmoid), r=['pb%d' % half], w=['gsg'])
                    for k in range(2):
                        kb.op('pe', REC.matmul(PB[2 + half][0:P, :], lhsT=peT[pr_][:, k, 0:P], rhs=w_ple_sb[:, k, half * 512:(half + 1) * 512],
                                               start=(k == 0), stop=(k == 1)), r=['peT', 'w_ple_sb'], w=['pb%d' % (2 + half)])
                    kb.op('dve', REC.tensor_tensor(out=gsg[pr_][0:P, half * 512:(half + 1) * 512], in0=PB[2 + half][0:P, :], in1=gsg[pr_][0:P, half * 512:(half + 1) * 512], op=ALU.mult),
                          r=['pb%d' % (2 + half), 'gsg'], w=['gsg'])
                kb.op('pool', REC.tensor_tensor(out=h2[pr_][0:P, :], in0=h2[pr_][0:P, :], in1=gsg[pr_][0:P, :], op=ALU.add), r=['h2', 'gsg'], w=['h2'])
                rmsn(ss, junk, h2[pr_][0:P, :], gv2[:, 1024:2048], yb[pr_][0:P, :], 'h2', 'yb', P)
                kb.dma('sp', REC.dma_start(out=y_ap, in_=yb[pr_][0:P, :]), r=['yb'], w=['y_o'])
            kb.alias = {}
        kb.emit()
    return nc


def _consts(S, pad):
    NB = S // 64
    p = np.arange(128, dtype=np.float32)[:, None]
    f = np.arange(128, dtype=np.float32)[None, :]
    blk = np.arange(NB, dtype=np.float32)[None, :]
    padblk = pad // 64
    valid = np.broadcast_to((blk >= padblk).astype(np.float32), (128, NB))
    first = np.broadcast_to((blk == padblk).astype(np.float32), (128, NB))
    D0 = p - 64.0 * blk
    ident = np.eye(128, dtype=np.float32)
    tri = np.where(p <= f, 0.0, NEG).astype(np.float32)
    wbs = []
    for t in range(5):
        dlt = (t - 4) * 128 + p - f
        wbs.append(np.where((dlt <= 0) & (dlt > -512), 0.0, NEG).astype(np.float32))
    tribd = ((p <= f) & ((p // 64) == (f // 64))).astype(np.float32)
    rmask = np.broadcast_to(((np.arange(128) % 64) != 0).astype(np.float32)[None, :], (128, 128))
    return np.ascontiguousarray(np.concatenate([valid, first, D0, ident, tri] + wbs + [tribd, rmask], axis=1).astype(np.float32))


def _rope_table(S, pad):
    pos = (np.arange(S) - pad).astype(np.float32)
    inv = (500000.0 ** (-np.arange(8, dtype=np.float32) / 8)).astype(np.float32)
    ang = (pos[:, None] * inv[None, :]).astype(np.float32)
    return np.concatenate([np.cos(ang), np.sin(ang)], axis=1).astype(np.float32)


_NC_CACHE = {}


def kernel(x_prompt, x_sample, p_prompt, p_sample, cache_k_cmp, cache_v_cmp, cache_k_slc,
           cache_v_slc, cache_k_win, cache_v_win, state_gla, state_conv, page_table,
           g_attn, w_in, w_gla_gate, b_gla_gate, g_gla_out, b_nsa_gate, w_cmp_k, w_cmp_v,
           w_o, g_ffn, w_up, w_conv, b_conv, w_down, g_ple, w_ple, w_ple_gate, g_final, _prompt_only=False, _dbg=None, _do_sample=True):
    x_prompt = np.asarray(x_prompt)
    B, S, _ = x_prompt.shape
    NSEQ = x_sample.shape[0]
    PAST = page_table.shape[1] * 128
    chunk = S // 4
    NT = S // 128
    OWN = NT // 4
    NPOOLP = np.asarray(cache_k_cmp).shape[1]
    key = (S, NSEQ // 8, PAST, _dbg, _do_sample)
    if key not in _NC_CACHE:
        _NC_CACHE[key] = build(S, NSEQ // 8, PAST, dbg=_dbg, do_sample=_do_sample, NPOOLP=NPOOLP)
    nc = _NC_CACHE[key]
    f32 = lambda a: np.ascontiguousarray(np.asarray(a, dtype=np.float32))
    vecsA = np.concatenate([f32(g_attn)[0], f32(g_gla_out)[0], f32(b_nsa_gate)[0]])[None, :]
    vecsB = np.concatenate([f32(g_ffn)[0], f32(g_ple)[0], f32(g_final)])[None, :]
    wgg = np.concatenate([f32(w_gla_gate)[0], f32(b_gla_gate)], axis=0)
    wcm = np.zeros((128, 8), np.float32)
    for ti, wsrc in enumerate((f32(w_cmp_k)[0], f32(w_cmp_v)[0])):
        for g in range(2):
            for m in range(2):
                wcm[64 * m:64 * m + 64, ti * 4 + g * 2 + m] = wsrc[:, g]
    convw = np.zeros((128, 44, 4), np.float32)
    wcv = f32(w_conv)[0]
    convw[:, :, 0:3] = wcv.T.reshape(44, 128, 3).transpose(1, 0, 2)
    convw[:, :, 3] = f32(b_conv)[0].reshape(44, 128).T
    nti = min(32, NT)
    indp = np.zeros((64, nti * 128), np.float32)
    for m in range(nti):
        for k_ in range(128):
            indp[2 * m + k_ // 64, m * 128 + k_] = 1.0
    in_maps = []
    for c in range(8):
        b, j = c // 4, c % 4
        pad = (3 - j) * chunk
        xl = np.zeros((S, D), np.float32)
        xl[pad:] = x_prompt[b, 0:(j + 1) * chunk]
        kbias = np.zeros((1, S), np.float32)
        kbias[0, :pad] = NEG
        in_maps.append(dict(
            xl=xl, pl=f32(p_prompt[0, b, j * chunk:(j + 1) * chunk]), cs=_rope_table(S, pad), cmask=_consts(S, pad), kbias=kbias,
            ind=indp, wc=wcm, w_in=f32(w_in)[0], wgg=wgg, vecsA=vecsA, vecsB=vecsB, w_o=f32(w_o)[0], w_up=f32(w_up)[0],
            convw=np.ascontiguousarray(convw.reshape(128, 176)), w_down=f32(w_down)[0], w_ple=f32(w_ple)[0], w_pg=f32(w_ple_gate)[0]))

    if _do_sample:
        NS = NSEQ // 8
        NPG = PAST // 128
        SPG = 128 // NPG
        NGR = NS // SPG
        NREP = 1 + NGR + 2
        pools_h = [np.ascontiguousarray(np.asarray(a, dtype=np.float32)[0].reshape(NPOOLP, 16384)) for a in (cache_k_cmp, cache_v_cmp, cache_k_slc, cache_v_slc)]
        inv = (500000.0 ** (-np.arange(8, dtype=np.float32) / 8)).astype(np.float32)
        ang = (np.float32(PAST) * inv).astype(np.float32)
        csrow = np.concatenate([np.cos(ang), np.sin(ang)]).astype(np.float32)
        pidx = np.arange(128)
        Rep = np.zeros((NS, NREP, 128), np.float32)
        for s_ in range(NS):
            Rep[s_, 0, :] = (pidx // 8 == s_)
            for r_ in range(NGR):
                Rep[s_, 1 + r_, :] = (r_ * SPG + pidx // NPG == s_)
            for hf in range(2):
                Rep[s_, 1 + NGR + hf, :] = (8 * hf + pidx // 16 == s_)
        selS = np.zeros((NS, NS, 64), np.float32)
        for s_ in range(NS):
            selS[s_, s_, :] = 1.0
        eye = np.eye(16, dtype=np.float32)
        maskW = np.zeros((128, 64), np.float32)
        maskW[pidx % 8 == 0, 0] = NEG
        wrow = np.stack([f32(w_cmp_k)[0], f32(w_cmp_v)[0]], axis=0)
        repc = np.concatenate([Rep.transpose(2, 1, 0).reshape(128, NREP * 16), np.broadcast_to(eye.reshape(1, 256), (128, 256)),
                               maskW, np.broadcast_to(wrow.reshape(1, 256), (128, 256)), (pidx % 16 != 15).astype(np.float32)[:, None]], axis=1)
        ptab = np.asarray(page_table).astype(np.int32)
        for c in range(8):
            sl = slice(c * NS, (c + 1) * NS)
            scs = np.concatenate([np.broadcast_to(csrow[None, :], (NS, 16)), selS.reshape(NS, 1024), Rep.reshape(NS, NREP * 128), eye[:NS],
                                  np.broadcast_to(np.arange(128, dtype=np.float32)[None, :], (NS, 128)), np.zeros((NS, NPG), np.float32),
                                  np.zeros((NS, 16), np.float32)], axis=1)
            pt_c = ptab[sl]
            ptab_pg = np.ascontiguousarray(pt_c.reshape(NGR, SPG * NPG).T)
            in_maps[c].update(dict(
                xs=f32(x_sample[sl, 0]), ps=f32(p_sample[0, sl, 0]), ptab=ptab_pg, ptabs=np.ascontiguousarray(pt_c),
                pool0=pools_h[0], pool1=pools_h[1], pool2=pools_h[2], pool3=pools_h[3],
                wink=np.ascontiguousarray(f32(cache_k_win)[0, sl].reshape(NS, 512, 128)), winv=np.ascontiguousarray(f32(cache_v_win)[0, sl].reshape(NS, 512, 128)),
                sgla=f32(state_gla)[0, sl], sconv=np.ascontiguousarray(f32(state_conv)[0, sl].reshape(NS, -1)),
                scs=np.ascontiguousarray(scs.astype(np.float32)), rep=np.ascontiguousarray(repc.astype(np.float32))))
    res = run_bass_kernel_spmd(nc, in_maps, core_ids=list(range(8)))
    R = res.results
    global _LAST, _LASTNC
    _LAST = R
    _LASTNC = nc
    y_prompt = np.zeros((B, S, D), np.float32)
    kv = np.zeros((B, S, 768), np.float32)
    for c in range(8):
        b, j = c // 4, c % 4
        y_prompt[b, j * chunk:(j + 1) * chunk] = R[c]["y"]
        kv[b, j * chunk:(j + 1) * chunk] = R[c]["kv6"]
    kv = kv.reshape(B, S, 6, 2, 64)
    nw = min(512, S)
    outs_p = [kv[None, :, :, i] for i in range(4)] + [kv[None, :, S - nw:, 4], kv[None, :, S - nw:, 5]]
    gl = np.stack([R[3]["glast"], R[7]["glast"]])[None]
    cv = np.stack([R[3]["convrows"], R[7]["convrows"]])[None]
    outs_p = [np.ascontiguousarray(o) for o in outs_p] + [gl, cv]
    if _prompt_only:
        return (y_prompt, None, *outs_p)
    cat = lambda k: np.concatenate([np.asarray(R[c][k]) for c in range(8)], axis=0)
    y_s = cat("ys")[:, None, :]
    skv = cat("skv6").reshape(NSEQ, 1, 6, 2, 64)
    outs_s = [np.ascontiguousarray(skv[None, :, :, i]) for i in range(4)]
    outs_s += [cat("swk").reshape(1, NSEQ, 512, 2, 64), cat("swv").reshape(1, NSEQ, 512, 2, 64)]
    outs_s += [cat("sglo")[None], cat("scvo")[None]]
    return (y_prompt, y_s, *outs_p, *outs_s)
```
